# Optimizing a Trainium2 kernel written in Bass

```python
import jax, jax.numpy as jnp
from jax import lax
import numpy as np

D_MODEL = 2048
BATCH = 1
SEQ = 16384
DEPTH = 2

D_MIX = D_MODEL
HEAD_DIM = 128
ATT_HEADS = 8
ATT_W = ATT_HEADS * HEAD_DIM
CONV_GROUPS = 4
CONV_W = CONV_GROUPS * HEAD_DIM
SGU_HEADS = 4
SGU_W = SGU_HEADS * HEAD_DIM
CONV_K = 31
CHUNK = 128
Q_BLOCK = 128
PLE_DIM = 256
EPS = 1e-6
SPLIT_SIZES = (ATT_W, ATT_W, ATT_W, ATT_W, ATT_HEADS, CONV_W, CONV_W, CONV_W, SGU_W, SGU_W, SGU_W)
N_IN = 4 * ATT_W + ATT_HEADS + 3 * CONV_W + 3 * SGU_W

kernel_name = "hymba_style_conv_sgu_fox_hybrid"


def rmsnorm(x, g):
    xf = x.astype(jnp.float32)
    inv = lax.rsqrt(jnp.mean(xf * xf, axis=-1, keepdims=True) + EPS)
    return (xf * inv * g.astype(jnp.float32)).astype(x.dtype)


def layernorm(x, g, b):
    xf = x.astype(jnp.float32)
    mu = jnp.mean(xf, axis=-1, keepdims=True)
    var = jnp.mean(jnp.square(xf - mu), axis=-1, keepdims=True)
    y = (xf - mu) * lax.rsqrt(var + EPS) * g.astype(jnp.float32) + b.astype(jnp.float32)
    return y.astype(x.dtype)


def split_columns(proj):
    idx = np.cumsum(np.array(SPLIT_SIZES))[:-1].tolist()
    return jnp.split(proj, idx, axis=-1)


def conformer_conv(a, b, dw, dw_b, ln_g, ln_b, pw, pw_b):
    y = a * jax.nn.sigmoid(b)
    y = lax.conv_general_dilated(
        y, dw[:, None, :], window_strides=(1,), padding=[(CONV_K - 1, 0)],
        dimension_numbers=('NWC', 'WIO', 'NWC'), feature_group_count=CONV_W) + dw_b
    y = jax.nn.silu(layernorm(y, ln_g, ln_b))
    return y @ pw + pw_b


def spatial_gating(u, v, ln_g, ln_b, w_s, b_s):
    bsz, seq, _ = v.shape
    u = jax.nn.gelu(u, approximate=False)
    v = layernorm(jax.nn.gelu(v, approximate=False), ln_g, ln_b)
    v = v.reshape(bsz, seq // CHUNK, CHUNK, SGU_HEADS, HEAD_DIM)
    causal = jnp.tril(jnp.ones((CHUNK, CHUNK), dtype=bool))
    w = jnp.where(causal, w_s, 0)
    sv = jnp.einsum('hts,bnshd->bnthd', w, v) + b_s.T[:, :, None]
    return u * sv.reshape(bsz, seq, SGU_W)


def forgetting_attention(q, k, v, f_logit, b_f):
    bsz, seq, _ = q.shape
    q = q.reshape(bsz, seq, ATT_HEADS, HEAD_DIM) * (HEAD_DIM ** -0.5)
    k = k.reshape(bsz, seq, ATT_HEADS, HEAD_DIM)
    v = v.reshape(bsz, seq, ATT_HEADS, HEAD_DIM)
    log_f = jax.nn.log_sigmoid(f_logit.astype(jnp.float32) + b_f.astype(jnp.float32))
    c = jnp.cumsum(log_f, axis=1)
    c_k = jnp.transpose(c, (0, 2, 1))[:, :, None, :]
    key_pos = jnp.arange(seq)

    def block(i):
        start = i * Q_BLOCK
        qb = lax.dynamic_slice_in_dim(q, start, Q_BLOCK, axis=1)
        cq = lax.dynamic_slice_in_dim(c, start, Q_BLOCK, axis=1)
        s = jnp.einsum('bqhd,bkhd->bhqk', qb, k).astype(jnp.float32)
        s = s + (jnp.transpose(cq, (0, 2, 1))[:, :, :, None] - c_k)
        q_pos = start + jnp.arange(Q_BLOCK)
        mask = q_pos[:, None] >= key_pos[None, :]
        s = jnp.where(mask, s, -1e30)
        prob = jax.nn.softmax(s, axis=-1).astype(v.dtype)
        return jnp.einsum('bhqk,bkhd->bqhd', prob, v)

    out = lax.map(block, jnp.arange(seq // Q_BLOCK))
    return jnp.moveaxis(out, 0, 1).reshape(bsz, seq, ATT_W)


def setup_inputs(seed: int = 0) -> dict:
    key = jax.random.key(seed)
    ks = jax.random.split(key, 24)
    f32 = jnp.float32
    nrm = lambda k, shape, scale: jax.random.normal(k, shape, f32) * scale
    return {
        "x": nrm(ks[0], (BATCH, SEQ, D_MODEL), 1.0),
        "p": nrm(ks[1], (DEPTH, BATCH, SEQ, PLE_DIM), 1.0),
        "norm_pre": 1.0 + nrm(ks[2], (DEPTH, D_MODEL), 0.02),
        "w_in": nrm(ks[3], (DEPTH, D_MODEL, N_IN), D_MODEL ** -0.5),
        "b_f": jnp.broadcast_to(jnp.linspace(1.0, 6.0, ATT_HEADS, dtype=f32), (DEPTH, ATT_HEADS)) + nrm(ks[4], (DEPTH, ATT_HEADS), 0.1),
        "conv_dw": nrm(ks[5], (DEPTH, CONV_K, CONV_W), CONV_K ** -0.5),
        "conv_dw_b": nrm(ks[6], (DEPTH, CONV_W), 0.01),
        "conv_ln_g": 1.0 + nrm(ks[7], (DEPTH, CONV_W), 0.02),
        "conv_ln_b": nrm(ks[8], (DEPTH, CONV_W), 0.01),
        "conv_pw": nrm(ks[9], (DEPTH, CONV_W, CONV_W), CONV_W ** -0.5),
        "conv_pw_b": nrm(ks[10], (DEPTH, CONV_W), 0.01),
        "sgu_ln_g": 1.0 + nrm(ks[11], (DEPTH, SGU_W), 0.02),
        "sgu_ln_b": nrm(ks[12], (DEPTH, SGU_W), 0.01),
        "sgu_w": nrm(ks[13], (DEPTH, SGU_HEADS, CHUNK, CHUNK), CHUNK ** -0.5),
        "sgu_b": 1.0 + nrm(ks[14], (DEPTH, SGU_HEADS, CHUNK), 0.01),
        "w_out": nrm(ks[15], (DEPTH, D_MIX, D_MODEL), D_MIX ** -0.5),
        "norm_post": 1.0 + nrm(ks[16], (DEPTH, D_MODEL), 0.02),
        "w_pg": nrm(ks[17], (DEPTH, D_MODEL, D_MODEL), D_MODEL ** -0.5),
        "w_pp": nrm(ks[18], (DEPTH, PLE_DIM, D_MODEL), PLE_DIM ** -0.5),
    }


def reference(x, p, norm_pre, w_in, b_f, conv_dw, conv_dw_b, conv_ln_g, conv_ln_b, conv_pw, conv_pw_b,
              sgu_ln_g, sgu_ln_b, sgu_w, sgu_b, w_out, norm_post, w_pg, w_pp):
    h = x
    for i in range(DEPTH):
        xn = rmsnorm(h, norm_pre[i])
        proj = xn @ w_in[i]
        (q, k, v, z_att, f_logit, glu_a, glu_b, z_conv, u_sgu, v_sgu, z_sgu) = split_columns(proj)
        y_conv = conformer_conv(glu_a, glu_b, conv_dw[i], conv_dw_b[i], conv_ln_g[i], conv_ln_b[i],
                                conv_pw[i], conv_pw_b[i]) * jax.nn.silu(z_conv)
        y_sgu = spatial_gating(u_sgu, v_sgu, sgu_ln_g[i], sgu_ln_b[i], sgu_w[i], sgu_b[i]) * jax.nn.silu(z_sgu)
        y_att = forgetting_attention(q, k, v, f_logit, b_f[i]) * jax.nn.silu(z_att)
        y = jnp.concatenate([y_conv, y_sgu, y_att], axis=-1) @ w_out[i]
        h = h + rmsnorm(y, norm_post[i])
        h = h + jax.nn.sigmoid(h @ w_pg[i]) * (p[i] @ w_pp[i])
    return h
```

```python
import numpy as np
from contextlib import ExitStack
import ml_dtypes
import concourse.bass as bass
import concourse.mybir as mybir
from concourse.bass_utils import run_bass_kernel_spmd

F32 = mybir.dt.float32
BF16 = mybir.dt.bfloat16
AF = mybir.ActivationFunctionType
ALU = mybir.AluOpType
AX = mybir.AxisListType

NCORE = 8
S = 16384
D = 2048
TOK = S // NCORE
NTB = TOK // 128
NKC = D // 128
NIN = 7176
EPS = 1e-6
SCALE = 128 ** -0.5
CONV_K = 31
HALO = CONV_K - 1
NKB = S // 128

SAME_ENG_SYNC = True
SAME_ENG_DIST = 3


class Buf:
    _n = 0

    def __init__(self, name=""):
        Buf._n += 1
        self.id = Buf._n
        self.name = name
        self.w = {}
        self.r = {}


class Prog:
    ENG = ("pe", "act", "dve", "pool", "sp")

    def __init__(self, nc):
        self.nc = nc
        self.q = {e: [] for e in self.ENG}
        self.cnt = {e: 0 for e in self.ENG}
        self.seen = {e: {} for e in self.ENG}

    def _deps(self, e, reads, writes, waw):
        need = {}
        for b in reads:
            for k, v in b.w.items():
                if need.get(k, 0) < v:
                    need[k] = v
        for b in writes:
            its = list(b.r.items())
            if waw:
                its += list(b.w.items())
            for k, v in its:
                if need.get(k, 0) < v:
                    need[k] = v
        waits = []
        seen = self.seen[e]
        for k, v in need.items():
            if k == e and (e == "pe" or not SAME_ENG_SYNC):
                continue
            if k == e and self.cnt[e] - v >= SAME_ENG_DIST:
                continue
            if seen.get(k, 0) >= v:
                continue
            seen[k] = v
            waits.append((k, v))
        return waits

    def _mark(self, k, v, reads, writes, waw):
        for b in reads:
            b.r[k] = v
        for b in writes:
            if waw:
                b.w = {k: v}
                b.r = {}
            else:
                b.w[k] = v

    def op(self, e, fn, reads=(), writes=(), waw=True):
        waits = self._deps(e, reads, writes, waw)
        self.cnt[e] += 1
        self.q[e].append((waits, fn, e, 1))
        self._mark(e, self.cnt[e], reads, writes, waw)

    def dma(self, qe, fn, owner, reads=(), writes=(), waw=True, inc=16):
        waits = self._deps(qe, reads, writes, waw)
        k = ("d", owner.name)
        self.cnt[k] = self.cnt.get(k, 0) + inc
        self.q[qe].append((waits, fn, k, inc))
        self._mark(k, self.cnt[k], reads, writes, waw)

    def barrier(self):
        for e in self.ENG:
            waits = []
            for k, v in self.cnt.items():
                if v == 0 or (k == e and e == "pe"):
                    continue
                if self.seen[e].get(k, 0) >= v:
                    continue
                self.seen[e][k] = v
                waits.append((k, v))
            if waits:
                self.q[e].append((waits, None, None, 0))

    def emit(self, stack):
        nc = self.nc
        sems = {}
        print("[kernel] semaphores:", len(self.cnt), "ops:", {e: len(v) for e, v in self.q.items()})
        for k in self.cnt:
            nm = "s_" + (k if isinstance(k, str) else "d_" + k[1])
            sems[k] = stack.enter_context(nc.semaphore(nm))
        block = stack.enter_context(nc.Block())
        emap = {"pe": block.tensor, "act": block.scalar, "dve": block.vector,
                "pool": block.gpsimd, "sp": block.sync}

        def mk(e):
            items = self.q[e]

            def body(eng):
                for waits, fn, sk, inc in items:
                    for k, v in waits:
                        eng.wait_ge(sems[k], v)
                    if fn is not None:
                        fn(eng).then_inc(sems[sk], inc)
            return body

        for e in self.ENG:
            if self.q[e]:
                emap[e](mk(e))


ARENA_BYTES = 200 * 1024


class Ctx:
    def __init__(self, name):
        self.nc = bass.Bass("TRN2", target_bir_lowering=False, num_devices=NCORE)
        self.P = Prog(self.nc)
        self.st = ExitStack()
        self.name = name
        self.arena = self.st.enter_context(self.nc.sbuf_tensor("arena", [128, ARENA_BYTES // 2], BF16))
        self.off = 0
        self.base = 0
        self.ps = self.st.enter_context(self.nc.psum_tensor("ps", [128, 8, 512], F32))
        self.b_ps = [Buf("ps%d" % i) for i in range(8)]
        self.vals = {}

    def persist_done(self):
        self.base = self.off

    def new_phase(self):
        self.P.barrier()
        self.off = self.base

    def din(self, name, shape, dt):
        return self.nc.dram_tensor(name, list(shape), dt, kind="ExternalInput").ap()

    def dout(self, name, shape, dt):
        return self.nc.dram_tensor(name, list(shape), dt, kind="ExternalOutput").ap()

    def dscr(self, name, shape, dt):
        return self.nc.dram_tensor(name, list(shape), dt).ap()

    def sb(self, name, shape, dt):
        esz = 2 if dt == BF16 else 4
        n = 1
        for d_ in shape[1:]:
            n *= d_
        nb = (n * esz + 63) // 64 * 64
        assert self.off + nb <= ARENA_BYTES, (name, self.off, nb)
        v = self.arena[:, self.off // 2:(self.off + nb) // 2]
        self.off += nb
        if esz == 4:
            v = v.bitcast(dt)
        v = v[:, 0:n]
        if len(shape) == 3:
            v = v.rearrange("p (a b) -> p a b", a=shape[1])
        if shape[0] < 128:
            v = v[0:shape[0]]
        return v

    def finish(self):
        self.P.barrier()
        self.P.emit(self.st)
        self.st.close()
        return self.nc


def _consts_tri(cx, name, dt, op, cm, step):
    P = cx.P
    if not hasattr(cx, "consts"):
        cx.consts = {}
    if name in cx.consts:
        return cx.consts[name]
    assert cx.base == 0, "constants must be created before the first phase"
    t = cx.sb(name, [128, 128], dt)
    b = Buf(name)
    P.op("pool", lambda e: e.memset(t[:], 1.0), writes=[b])
    P.op("pool", lambda e: e.affine_select(out=t[:], in_=t[:], pattern=[[step, 128]], compare_op=op,
                                           fill=0.0, base=0, channel_multiplier=cm),
         reads=[b], writes=[b])
    cx.consts[name] = (t, b)
    return t, b


A_TILES = [("q", 0), ("q", 1), ("k", 0), ("k", 1), ("v", 0), ("v", 1), ("zatt", 0), ("zatt", 1), ("zconv", 0),
           ("glu", 0), ("glu", 1), ("usg", 0), ("usg", 1), ("vsgu", 0)]


def w_in_perm():
    o = {}
    names = ["q", "k", "v", "zatt", "f", "ga", "gb", "zconv", "u", "vsgu", "zsgu"]
    sizes = [1024, 1024, 1024, 1024, 8, 512, 512, 512, 512, 512, 512]
    c = 0
    for n, s in zip(names, sizes):
        o[n] = c
        c += s
    r = np.arange
    idx = [r(o["q"], o["q"] + 1024), r(o["k"], o["k"] + 1024), r(o["v"], o["v"] + 1024), r(o["zatt"], o["zatt"] + 1024),
           r(o["zconv"], o["zconv"] + 512)]
    for i in range(4):
        idx += [r(o["ga"] + 128 * i, o["ga"] + 128 * i + 128), r(o["gb"] + 128 * i, o["gb"] + 128 * i + 128)]
    for i in range(4):
        idx += [r(o["u"] + 128 * i, o["u"] + 128 * i + 128), r(o["zsgu"] + 128 * i, o["zsgu"] + 128 * i + 128)]
    idx += [r(o["vsgu"], o["vsgu"] + 512), r(o["f"], o["f"] + 8)]
    return np.concatenate(idx)


def emit_A(cx, io, li):
    nc, P = cx.nc, cx.P
    cx.new_phase()
    h = io["h_in"][li]
    w_in = io["w_in"][li]
    gpre = io["gpre"][li]
    bfb = io["bfb"][li]
    sgg = io["sgg"][li]
    sgb = io["sgb"][li]
    sguw = io["sguw"][li]
    sgub = io["sgub"][li]
    o_q = io["blobq"]
    o_k = io["blobk"]
    o_v = io["blobv"].rearrange("(h p) (kb d) -> p h kb d", p=128, d=128)
    o_lfT = io["bloblf"].rearrange("kb (h j) -> h kb j", j=128)
    o_halo = io["blobhalo"]
    o_glu = io["glu"]
    o_gatt = io["gatt"]
    o_gconv = io["gconv"]
    o_ysgu = io["ysgu"]
    b_blobq, b_blobk, b_blobv = io["b_blobq"], io["b_blobk"], io["b_blobv"]
    b_blob32, b_bloblf = io["b_blobhalo"], io["b_bloblf"]
    pending_ag = []
    b_own = io["b_own"]
    b_hin = io["b_h"][li]

    xnT = cx.sb("xnT", [128, NKC, TOK], BF16)
    wt = [cx.sb("wt%d" % i, [128, NKC, 1032], BF16) for i in range(2)]
    ug = cx.sb("ug", [128, 4, TOK], BF16)
    small = cx.sb("small", [128, 32], F32)
    bfb_s = cx.sb("bfb_s", [128, 8], F32)
    sgg_s = cx.sb("sgg_s", [128, 512], F32)
    sgb_s = cx.sb("sgb_s", [128, 512], F32)
    sguw_s = cx.sb("sguw_s", [128, 4, 128], F32)
    sgub_s = cx.sb("sgub_s", [128, 512], F32)
    wsT = cx.sb("wsT", [128, 4, 128], BF16)
    vln = cx.sb("vln", [128, 512], BF16)
    lfst = [cx.sb("lfst%d" % i, [128, 8], F32) for i in range(2)]
    lfT_sb = [cx.sb("lfT_sb%d" % i, [8, 128], F32) for i in range(2)]
    b_lfT_sb = [Buf("lfT_sb0"), Buf("lfT_sb1")]
    ps = cx.ps

    union0 = cx.off
    hblk = [cx.sb("hblk%d" % i, [128, D], F32) for i in range(2)]
    xn = cx.sb("xn", [128, D], BF16)
    gbc = cx.sb("gbc", [128, D], F32)
    junk = cx.sb("junk", [128, D], BF16)
    b_xnT = [Buf("xnT%d" % i) for i in range(NTB)]
    b_wt = [Buf("wt0"), Buf("wt1")]
    b_ug = Buf("ug")
    b_hblk = [Buf("hb0"), Buf("hb1")]
    b_xn, b_gbc, b_junk, b_small = Buf("xn"), Buf("gbc"), Buf("junk"), Buf("small")
    b_par = Buf("par")
    b_wsT = Buf("wsT")
    b_stg = [Buf("stg%d" % i) for i in range(3)]
    b_stf = [Buf("stf%d" % i) for i in range(3)]
    b_vln = Buf("vln")
    b_lfst = [Buf("lf0"), Buf("lf1")]
    b_ps = cx.b_ps
    b_sw = Buf("sguw")

    identb, b_identb = _consts_tri(cx, "identb", BF16, ALU.is_equal, 1, -1)
    identf, b_identf = _consts_tri(cx, "identf", F32, ALU.is_equal, 1, -1)

    P.dma("sp", lambda e: e.dma_start(out=gbc[:], in_=gpre), b_gbc, writes=[b_gbc])
    for t_sb, t_dr in ((bfb_s, bfb), (sgg_s, sgg), (sgb_s, sgb), (sgub_s, sgub)):
        P.dma("sp", (lambda a, b: lambda e: e.dma_start(out=a[:], in_=b))(t_sb, t_dr), b_par, writes=[b_par], waw=False)
    P.dma("sp", lambda e: e.dma_start(out=sguw_s[:], in_=sguw), b_sw, writes=[b_sw])

    for hh in range(4):
        P.op("pool", (lambda hh: lambda e: e.affine_select(
            out=sguw_s[:, hh, :], in_=sguw_s[:, hh, :], pattern=[[-1, 128]], compare_op=ALU.is_ge,
            fill=0.0, base=0, channel_multiplier=1))(hh), reads=[b_sw], writes=[b_sw])
    for hh in range(4):
        P.op("pe", (lambda hh: lambda e: e.transpose(out=ps[:, 7, hh * 128:(hh + 1) * 128], in_=sguw_s[:, hh, :],
                                                     identity=identf[:]))(hh),
             reads=[b_sw, b_identf], writes=[b_ps[7]], waw=False)
    P.op("dve", lambda e: e.tensor_copy(out=wsT[:].rearrange("p h t -> p (h t)"), in_=ps[:, 7, :]),
         reads=[b_ps[7]], writes=[b_wsT])

    hv = h.rearrange("(tb p) d -> tb p d", p=128)
    for tb in range(NTB):
        hb, bh = hblk[tb % 2], b_hblk[tb % 2]
        P.dma("sp", (lambda hb, tb: lambda e: e.dma_start(out=hb[:], in_=hv[tb]))(hb, tb), bh, reads=[b_hin], writes=[bh])
        ss = small[:, 0:1]
        P.op("act", (lambda hb: lambda e: e.activation(out=junk[:], in_=hb[:], func=AF.Square, accum_out=small[:, 0:1]))(hb),
             reads=[bh], writes=[b_junk, b_small])
        P.op("act", lambda e: e.activation(out=small[:, 1:2], in_=small[:, 0:1], func=AF.Sqrt, bias=EPS, scale=1.0 / D),
             reads=[b_small], writes=[b_small])
        P.op("dve", lambda e: e.reciprocal(out=small[:, 2:3], in_=small[:, 1:2]), reads=[b_small], writes=[b_small])
        P.op("dve", (lambda hb: lambda e: e.scalar_tensor_tensor(out=xn[:], in0=hb[:], scalar=small[:, 2:3], in1=gbc[:],
                                                                  op0=ALU.mult, op1=ALU.mult))(hb),
             reads=[bh, b_small, b_gbc], writes=[b_xn])
        for half in range(2):
            bank = half
            pb = ps[:, bank, :].bitcast(BF16)
            for j in range(8):
                kc = half * 8 + j
                P.op("pe", (lambda pb, j, kc: lambda e: e.transpose(out=pb[:, j * 128:(j + 1) * 128],
                                                                    in_=xn[:, kc * 128:(kc + 1) * 128],
                                                                    identity=identb[:]))(pb, j, kc),
                     reads=[b_xn, b_identb], writes=[b_ps[bank]], waw=(j == 0))
            eng = "act" if half == 0 else "dve"
            dst = xnT[:, half * 8:(half + 1) * 8, tb * 128:(tb + 1) * 128]
            src = pb.rearrange("p (j t) -> p j t", j=8)
            if eng == "act":
                P.op("act", (lambda dst, src: lambda e: e.activation(out=dst, in_=src, func=AF.Copy))(dst, src),
                     reads=[b_ps[bank]], writes=[b_xnT[tb]], waw=False)
            else:
                P.op("dve", (lambda dst, src: lambda e: e.tensor_copy(out=dst, in_=src))(dst, src),
                     reads=[b_ps[bank]], writes=[b_xnT[tb]], waw=False)

    P.barrier()
    cx.off = union0
    stg = [cx.sb("stg%d" % i, [128, TOK], BF16) for i in range(3)]
    vstage = cx.sb("vstage", [128, 4, NTB, 128], BF16) if False else cx.sb("vstage", [128, 4 * NTB, 128], BF16)
    b_vstage = Buf("vstage")
    stf = [cx.sb("stf%d" % i, [128, 512], F32) for i in range(3)]
    wv = w_in.rearrange("(kc p) n -> p kc n", p=128)
    rot = {"bank": 0, "stg": 0, "stf": 0, "lf": 0}

    def nbank():
        b = 2 + rot["bank"] % 5
        rot["bank"] += 1
        return b

    def nstg():
        i = rot["stg"] % 3
        rot["stg"] += 1
        return i

    def nstf():
        i = rot["stf"] % 3
        rot["stf"] += 1
        return i

    def mm_feat(wti, cc, tg, bank):
        w = wt[wti][:, :, cbase[0]:cbase[0] + 520]
        for kc in range(NKC):
            P.op("pe", (lambda w, kc, cc, tg, bank: lambda e: e.matmul(
                ps[:, bank, :], lhsT=w[:, kc, cc * 128:(cc + 1) * 128], rhs=xnT[:, kc, tg * 512:(tg + 1) * 512],
                start=(kc == 0), stop=(kc == NKC - 1)))(w, kc, cc, tg, bank),
                reads=[b_wt[wti]] + b_xnT[tg * 4:(tg + 1) * 4], writes=[b_ps[bank]], waw=(kc == 0))

    def mm_tok(wti, tb, bank, c0, n):
        w = wt[wti][:, :, cbase[0]:cbase[0] + 520]
        for kc in range(NKC):
            P.op("pe", (lambda w, kc, tb, bank, c0, n: lambda e: e.matmul(
                ps[:, bank, 0:n], lhsT=xnT[:, kc, tb * 128:(tb + 1) * 128], rhs=w[:, kc, c0:c0 + n],
                start=(kc == 0), stop=(kc == NKC - 1)))(w, kc, tb, bank, c0, n),
                reads=[b_wt[wti], b_xnT[tb]], writes=[b_ps[bank]], waw=(kc == 0))

    col = 0
    cbase = [0]
    for ti, (kind, sub) in enumerate(A_TILES):
        if ti > 0 and A_TILES[ti - 1][0] in ("q", "k", "v") and A_TILES[ti - 1][1] == 1:
            pending_ag.append(A_TILES[ti - 1][0])
        wti = (ti // 2) % 2
        cbase[0] = (ti % 2) * 512
        if ti % 2 == 0:
            ncols = min(1024, NIN - col) if ti + 2 < len(A_TILES) else NIN - col
            P.dma("pool", (lambda wti, col, ncols: lambda e: e.dma_start(out=wt[wti][:, :, 0:ncols], in_=wv[:, :, col:col + ncols]))(wti, col, ncols),
                  b_wt[wti], writes=[b_wt[wti]])
            col += ncols
            while pending_ag:
                io["ag"](pending_ag.pop(0))
        if kind in ("q", "k", "zatt", "zconv"):
            dst = {"q": o_q, "k": o_k, "zatt": o_gatt, "zconv": o_gconv}[kind]
            b_dst = {"q": b_blobq, "k": b_blobk}.get(kind, b_own)
            func = AF.Copy if kind in ("q", "k") else AF.Silu
            for cc in range(4):
                si = nstg()
                for tg in range(4):
                    bank = nbank()
                    mm_feat(wti, cc, tg, bank)
                    P.op("act", (lambda si, bank, func, tg: lambda e: e.activation(out=stg[si][:, tg * 512:(tg + 1) * 512], in_=ps[:, bank, :], func=func))(si, bank, func, tg),
                         reads=[b_ps[bank]], writes=[b_stg[si]], waw=(tg == 0))
                r0 = (sub * 4 + cc) * 128
                P.dma("sp", (lambda dst, r0, si: lambda e: e.dma_start(out=dst[r0:r0 + 128, :], in_=stg[si][:]))(dst, r0, si),
                      b_stg[si], reads=[b_stg[si]], writes=[b_dst], waw=False)
        elif kind == "glu":
            for pr in range(2):
                ch = sub * 2 + pr
                for tg in range(4):
                    ba, bb = nbank(), nbank()
                    mm_feat(wti, 2 * pr, tg, ba)
                    mm_feat(wti, 2 * pr + 1, tg, bb)
                    s1, s2 = nstf(), nstf()
                    P.op("act", (lambda s1, bb: lambda e: e.activation(out=stf[s1][:], in_=ps[:, bb, :], func=AF.Sigmoid))(s1, bb),
                         reads=[b_ps[bb]], writes=[b_stf[s1]])
                    P.op("dve", (lambda s1, s2, ba: lambda e: e.tensor_tensor(out=stf[s2][:], in0=ps[:, ba, :], in1=stf[s1][:], op=ALU.mult))(s1, s2, ba),
                         reads=[b_ps[ba], b_stf[s1]], writes=[b_stf[s2]])
                    P.dma("sp", (lambda ch, tg, s2: lambda e: e.dma_start(out=o_glu[ch * 128:(ch + 1) * 128, tg * 512:(tg + 1) * 512], in_=stf[s2][:]))(ch, tg, s2),
                          b_stf[s2], reads=[b_stf[s2]], writes=[b_own], waw=False)
                    if tg == 3:
                        P.dma("sp", (lambda ch, s2: lambda e: e.dma_start(out=o_halo[ch * 128:(ch + 1) * 128, :], in_=stf[s2][:, 480:512]))(ch, s2),
                              b_stf[s2], reads=[b_stf[s2]], writes=[b_blob32], waw=False)
        elif kind == "usg":
            for pr in range(2):
                ch = sub * 2 + pr
                for tg in range(4):
                    ba, bb = nbank(), nbank()
                    mm_feat(wti, 2 * pr, tg, ba)
                    mm_feat(wti, 2 * pr + 1, tg, bb)
                    s1, s2 = nstf(), nstf()
                    P.op("act", (lambda s1, ba: lambda e: e.activation(out=stf[s1][:], in_=ps[:, ba, :], func=AF.Gelu))(s1, ba),
                         reads=[b_ps[ba]], writes=[b_stf[s1]])
                    P.op("act", (lambda s2, bb: lambda e: e.activation(out=stf[s2][:], in_=ps[:, bb, :], func=AF.Silu))(s2, bb),
                         reads=[b_ps[bb]], writes=[b_stf[s2]])
                    P.op("dve", (lambda ch, tg, s1, s2: lambda e: e.tensor_tensor(out=ug[:, ch, tg * 512:(tg + 1) * 512], in0=stf[s1][:], in1=stf[s2][:], op=ALU.mult))(ch, tg, s1, s2),
                         reads=[b_stf[s1], b_stf[s2]], writes=[b_ug], waw=False)
        elif kind == "v":
            for tb in range(NTB):
                bank = nbank()
                mm_tok(wti, tb, bank, 0, 512)
                P.op("act", (lambda tb, bank: lambda e: e.activation(out=vstage[:].rearrange("p (h k) d -> p h k d", h=4)[:, :, tb, :],
                                                                     in_=ps[:, bank, :].rearrange("p (h d) -> p h d", h=4), func=AF.Copy))(tb, bank),
                     reads=[b_ps[bank]], writes=[b_vstage], waw=(tb == 0))
            P.dma("sp", (lambda sub: lambda e: e.dma_start(out=o_v[:, sub * 4:(sub + 1) * 4, :, :],
                                                           in_=vstage[:].rearrange("p (h k) d -> p h k d", h=4)))(sub),
                  b_vstage, reads=[b_vstage], writes=[b_blobv], waw=False)
        elif kind == "vsgu":
            for tb in range(NTB):
                bank = nbank()
                mm_tok(wti, tb, bank, 512, 8)
                lfi = rot["lf"] % 2
                rot["lf"] += 1
                P.op("dve", (lambda lfi, bank: lambda e: e.tensor_tensor(out=lfst[lfi][:], in0=ps[:, bank, 0:8], in1=bfb_s[:], op=ALU.add))(lfi, bank),
                     reads=[b_ps[bank], b_par], writes=[b_lfst[lfi]])
                P.op("act", (lambda lfi: lambda e: e.activation(out=lfst[lfi][:], in_=lfst[lfi][:], func=AF.Exp, scale=-1.0))(lfi),
                     reads=[b_lfst[lfi]], writes=[b_lfst[lfi]])
                P.op("act", (lambda lfi: lambda e: e.activation(out=lfst[lfi][:], in_=lfst[lfi][:], func=AF.Ln, bias=1.0, scale=1.0))(lfi),
                     reads=[b_lfst[lfi]], writes=[b_lfst[lfi]])
                P.op("dve", (lambda lfi: lambda e: e.tensor_scalar(out=lfst[lfi][:], in0=lfst[lfi][:], scalar1=-1.0, scalar2=None, op0=ALU.mult))(lfi),
                     reads=[b_lfst[lfi]], writes=[b_lfst[lfi]])
                bank = nbank()
                P.op("pe", (lambda lfi, bank: lambda e: e.transpose(out=ps[0:8, bank, 0:128], in_=lfst[lfi][:], identity=identf[:]))(lfi, bank),
                     reads=[b_lfst[lfi], b_identf], writes=[b_ps[bank]])
                P.op("dve", (lambda tb, bank: lambda e: e.tensor_copy(out=lfT_sb[tb % 2][:], in_=ps[0:8, bank, 0:128]))(tb, bank),
                     reads=[b_ps[bank]], writes=[b_lfT_sb[tb % 2]])
                P.dma("sp", (lambda tb: lambda e: e.dma_start(out=o_lfT[:, tb, :], in_=lfT_sb[tb % 2][:]))(tb),
                      b_lfT_sb[tb % 2], reads=[b_lfT_sb[tb % 2]], writes=[b_bloblf], waw=False)
                bank = nbank()
                mm_tok(wti, tb, bank, 0, 512)
                s1 = nstf()
                P.op("act", (lambda s1, bank: lambda e: e.activation(out=stf[s1][:], in_=ps[:, bank, :], func=AF.Gelu))(s1, bank),
                     reads=[b_ps[bank]], writes=[b_stf[s1]])
                P.op("dve", (lambda s1: lambda e: e.bn_stats(out=small[:, 8:14], in_=stf[s1][:]))(s1),
                     reads=[b_stf[s1]], writes=[b_small])
                P.op("dve", lambda e: e.bn_aggr(out=small[:, 16:18], in_=small[:, 8:14]), reads=[b_small], writes=[b_small])
                P.op("act", lambda e: e.activation(out=small[:, 18:19], in_=small[:, 17:18], func=AF.Sqrt, bias=EPS, scale=1.0),
                     reads=[b_small], writes=[b_small])
                P.op("dve", lambda e: e.reciprocal(out=small[:, 19:20], in_=small[:, 18:19]), reads=[b_small], writes=[b_small])
                P.op("dve", (lambda s1: lambda e: e.tensor_scalar(out=stf[s1][:], in0=stf[s1][:], scalar1=small[:, 16:17], scalar2=small[:, 19:20],
                                                                   op0=ALU.subtract, op1=ALU.mult))(s1),
                     reads=[b_stf[s1], b_small], writes=[b_stf[s1]])
                P.op("dve", (lambda s1: lambda e: e.tensor_tensor(out=stf[s1][:], in0=stf[s1][:], in1=sgg_s[:], op=ALU.mult))(s1),
                     reads=[b_stf[s1], b_par], writes=[b_stf[s1]])
                P.op("dve", (lambda s1: lambda e: e.tensor_tensor(out=vln[:], in0=stf[s1][:], in1=sgb_s[:], op=ALU.add))(s1),
                     reads=[b_stf[s1], b_par], writes=[b_vln])
                bank = nbank()
                for hh in range(4):
                    P.op("pe", (lambda hh, bank: lambda e: e.matmul(ps[:, bank, hh * 128:(hh + 1) * 128], lhsT=vln[:, hh * 128:(hh + 1) * 128],
                                                                    rhs=wsT[:, hh, :], start=True, stop=True))(hh, bank),
                         reads=[b_vln, b_wsT], writes=[b_ps[bank]], waw=(hh == 0))
                s2 = nstf()
                P.op("dve", (lambda s2, bank: lambda e: e.tensor_tensor(out=stf[s2][:], in0=ps[:, bank, :], in1=sgub_s[:], op=ALU.add))(s2, bank),
                     reads=[b_ps[bank], b_par], writes=[b_stf[s2]])
                P.op("dve", (lambda s2, tb: lambda e: e.tensor_tensor(out=ug[:, :, tb * 128:(tb + 1) * 128],
                                                                      in0=stf[s2][:].rearrange("p (h t) -> p h t", h=4),
                                                                      in1=ug[:, :, tb * 128:(tb + 1) * 128], op=ALU.mult))(s2, tb),
                     reads=[b_stf[s2], b_ug], writes=[b_ug])
    while pending_ag:
        io["ag"](pending_ag.pop(0))
    for ch in range(4):
        P.dma("sp", (lambda ch: lambda e: e.dma_start(out=o_ysgu[ch * 128:(ch + 1) * 128, :], in_=ug[:, ch, :]))(ch),
              b_ug, reads=[b_ug], writes=[b_own], waw=False)


def emit_B(cx, io, li, do_conv=True, do_att=True, nq=S // 512):
    nc, P = cx.nc, cx.P
    cx.new_phase()
    gathhalo = io["gathhalo"].rearrange("(r c p) t -> r p c t", c=4, p=128)
    gathlf = io["gathlf"].rearrange("p (h j) -> h p j", j=128)
    glu_d = io["glu"]
    gconv_d = io["gconv"]
    cpar_d = io["cpar"][li]
    pw_d = io["pw"][li]
    o_yatt = io["yatt"]
    o_yconv = io["yconv"]
    b_g32, b_glf, b_own = io["b_gathhalo"], io["b_gathlf"], io["b_own"]
    b_yatt, b_yconv = io["b_yatt"], io["b_yconv"]
    b_cid = io["b_cid"]
    cid_sb, cmask = io["cid_sb"], io["cmask_sb"]

    def dyn(e, idx, lo, hi):
        key = (id(e), idx)
        if key not in cx.vals:
            reg = e.alloc_register("dyn%d" % idx)
            e.reg_load(reg, cid_sb[0:1, idx:idx + 1])
            cx.vals[key] = e.snap(reg, min_val=lo, max_val=hi)
        return cx.vals[key]

    ps = cx.ps
    b_ps = cx.b_ps
    ones_b, b_ones_b = _consts_tri(cx, "ones_b", BF16, ALU.is_ge, 0, 0)
    tri_b, b_tri_b = _consts_tri(cx, "tri_b", BF16, ALU.is_ge, -1, 1)

    if do_conv:
        ypad = [cx.sb("ypad%d" % i, [128, HALO + TOK], F32) for i in range(2)]
        acc = cx.sb("acc", [128, 4, TOK], F32)
        gconv = cx.sb("gconv", [128, 4, TOK], BF16)
        cpar = cx.sb("cpar", [128, 4, 36], F32)
        pw = cx.sb("pw", [128, 4, 512], BF16)
        accb = cx.sb("accb", [128, 4, 512], BF16)
        sqb = cx.sb("sqb", [128, 4, 512], BF16)
        sT = cx.sb("sT", [128, 4, 512], BF16)
        mu = cx.sb("mu", [128, 512], F32)
        var = cx.sb("var", [128, 512], F32)
        ycst = [cx.sb("ycst%d" % i, [128, 512], BF16) for i in range(2)]
        b_ypad = [Buf("yp0"), Buf("yp1")]
        b_acc = [Buf("acc%d" % i) for i in range(4)]
        b_gconv, b_cpar, b_pw, b_accb, b_sqb, b_sT, b_mu, b_var = [Buf(x) for x in "gconv cpar pw accb sqb sT mu var".split()]
        b_ycst = [Buf("yc0"), Buf("yc1")]
        P.dma("sp", lambda e: e.dma_start(out=cpar[:], in_=cpar_d), b_cpar, writes=[b_cpar])
        P.dma("sp", lambda e: e.dma_start(out=gconv[:], in_=gconv_d.rearrange("(c p) t -> p c t", p=128)), b_gconv, reads=[b_own], writes=[b_gconv])
        P.dma("pool", lambda e: e.dma_start(out=pw[:], in_=pw_d.rearrange("(c p) n -> p c n", p=128)), b_pw, writes=[b_pw])
        ypv = glu_d.rearrange("(c p) t -> c p t", p=128)
        halo_sb = cx.sb("halo_sb", [128, 4, 32], F32)
        b_halo = Buf("halo")

        def ld_halo(e):
            pv = dyn(e, 2, 0, NCORE - 1)
            return e.dma_start(out=halo_sb[:], in_=gathhalo[pv])
        P.dma("sp", ld_halo, b_halo, reads=[b_g32, b_cid], writes=[b_halo])
        def conv_taps():
            for cc in range(4):
                yp, byp = ypad[cc % 2], b_ypad[cc % 2]
                P.dma("sp", (lambda yp, cc: lambda e: e.dma_start(out=yp[:, HALO:HALO + TOK], in_=ypv[cc]))(yp, cc), byp, reads=[b_own], writes=[byp])

                P.op("dve", (lambda yp, cc: lambda e: e.tensor_scalar(out=yp[:, 0:HALO], in0=halo_sb[:, cc, 32 - HALO:32], scalar1=cmask[:, 0:1], scalar2=None, op0=ALU.mult))(yp, cc),
                     reads=[byp, b_halo, b_cid], writes=[byp], waw=False)
                P.op("dve", (lambda yp, cc: lambda e: e.tensor_scalar(out=acc[:, cc, :], in0=yp[:, 0:TOK], scalar1=cpar[:, cc, 0:1],
                                                                       scalar2=cpar[:, cc, 31:32], op0=ALU.mult, op1=ALU.add))(yp, cc),
                     reads=[byp, b_cpar], writes=[b_acc[cc]])
                yield
                for j in range(1, CONV_K):
                    P.op("dve", (lambda yp, cc, j: lambda e: e.scalar_tensor_tensor(out=acc[:, cc, :], in0=yp[:, j:j + TOK], scalar=cpar[:, cc, j:j + 1],
                                                                                    in1=acc[:, cc, :], op0=ALU.mult, op1=ALU.add))(yp, cc, j),
                         reads=[byp, b_cpar, b_acc[cc]], writes=[b_acc[cc]])
                    yield
        def conv_post():
            for tg in range(4):
                sl = slice(tg * 512, (tg + 1) * 512)
                P.op("act", (lambda sl: lambda e: e.activation(out=accb[:], in_=acc[:, :, sl], func=AF.Copy))(sl), reads=b_acc, writes=[b_accb])
                P.op("act", (lambda sl: lambda e: e.activation(out=sqb[:], in_=acc[:, :, sl], func=AF.Square))(sl), reads=b_acc, writes=[b_sqb])
                for cc in range(4):
                    P.op("pe", (lambda cc: lambda e: e.matmul(ps[:, 0, :], lhsT=ones_b[:], rhs=accb[:, cc, :], start=(cc == 0), stop=(cc == 3)))(cc),
                         reads=[b_ones_b, b_accb], writes=[b_ps[0]], waw=(cc == 0))
                for cc in range(4):
                    P.op("pe", (lambda cc: lambda e: e.matmul(ps[:, 1, :], lhsT=ones_b[:], rhs=sqb[:, cc, :], start=(cc == 0), stop=(cc == 3)))(cc),
                         reads=[b_ones_b, b_sqb], writes=[b_ps[1]], waw=(cc == 0))
                P.op("act", lambda e: e.activation(out=mu[:], in_=ps[:, 0, :], func=AF.Copy, scale=1.0 / 512), reads=[b_ps[0]], writes=[b_mu])
                P.op("dve", lambda e: e.tensor_tensor(out=var[:], in0=mu[:], in1=mu[:], op=ALU.mult), reads=[b_mu], writes=[b_var])
                P.op("dve", lambda e: e.scalar_tensor_tensor(out=var[:], in0=ps[:, 1, :], scalar=1.0 / 512, in1=var[:], op0=ALU.mult, op1=ALU.subtract),
                     reads=[b_ps[1], b_var], writes=[b_var])
                P.op("act", lambda e: e.activation(out=var[:], in_=var[:], func=AF.Sqrt, bias=EPS, scale=1.0), reads=[b_var], writes=[b_var])
                P.op("dve", lambda e: e.reciprocal(out=var[:], in_=var[:]), reads=[b_var], writes=[b_var])
                for cc in range(4):
                    P.op("dve", (lambda cc, sl: lambda e: e.tensor_tensor(out=acc[:, cc, sl], in0=acc[:, cc, sl], in1=mu[:], op=ALU.subtract))(cc, sl),
                         reads=[b_acc[cc], b_mu], writes=[b_acc[cc]])
                    P.op("dve", (lambda cc, sl: lambda e: e.tensor_tensor(out=acc[:, cc, sl], in0=acc[:, cc, sl], in1=var[:], op=ALU.mult))(cc, sl),
                         reads=[b_acc[cc], b_var], writes=[b_acc[cc]])
                    P.op("act", (lambda cc, sl: lambda e: e.activation(out=sT[:, cc, :], in_=acc[:, cc, sl], func=AF.Silu,
                                                                       scale=cpar[:, cc, 32:33], bias=cpar[:, cc, 33:34]))(cc, sl),
                         reads=[b_acc[cc], b_cpar], writes=[b_sT], waw=(cc == 0))
                for co in range(4):
                    bank = 2 + co % 2
                    for cc in range(4):
                        P.op("pe", (lambda cc, co, bank: lambda e: e.matmul(ps[:, bank, :], lhsT=pw[:, cc, co * 128:(co + 1) * 128], rhs=sT[:, cc, :],
                                                                            start=(cc == 0), stop=(cc == 3)))(cc, co, bank),
                             reads=[b_pw, b_sT], writes=[b_ps[bank]], waw=(cc == 0))
                    si = co % 2
                    P.op("dve", (lambda co, bank, si, sl: lambda e: e.scalar_tensor_tensor(out=ycst[si][:], in0=ps[:, bank, :], scalar=cpar[:, co, 34:35],
                                                                                           in1=gconv[:, co, sl], op0=ALU.add, op1=ALU.mult))(co, bank, si, sl),
                         reads=[b_ps[bank], b_cpar, b_gconv], writes=[b_ycst[si]])
                    P.dma("sp", (lambda co, si, sl: lambda e: e.dma_start(out=o_yconv[co * 128:(co + 1) * 128, sl], in_=ycst[si][:]))(co, si, sl),
                          b_ycst[si], reads=[b_ycst[si]], writes=[b_yconv], waw=False)

        taps_gen = conv_taps()
    else:
        taps_gen, conv_post = iter(()), (lambda: None)

    if do_att:
        NPC = 8
        PCW = S // NPC
        qT = cx.sb("qT_s", [128, S], BF16)
        kT = cx.sb("kT_s", [128, S], BF16)
        vv = cx.sb("v_s", [128, NKB, 128], BF16)
        lf = cx.sb("lf_s", [128, NKB], F32)
        lfT = cx.sb("lfT_s", [128, 128], F32)
        tot = cx.sb("tot", [128, 2], F32)
        totbc = cx.sb("totbc", [128, 128], F32)
        ck = cx.sb("ck", [128, NKB], F32)
        cend = cx.sb("cend", [128, NKB], F32)
        bias4 = [cx.sb("bias4_%d" % i, [128, 4, NKB], F32) for i in range(2)]
        pT = [cx.sb("pT%d" % i, [128, 512], BF16) for i in range(3)]
        rinv = cx.sb("rinv", [128, 512], F32)
        yst = [cx.sb("yst%d" % i, [128, 512], BF16) for i in range(2)]
        b_q = [Buf("q%d" % i) for i in range(NPC)]
        b_k = [Buf("k%d" % i) for i in range(NPC)]
        b_v = [Buf("v%d" % i) for i in range(NPC)]
        b_lf, b_lfT, b_tot, b_totbc, b_ck, b_cend, b_rinv = [Buf(x) for x in "lf lfT tot totbc ck cend rinv".split()]
        b_bias4 = [Buf("b40"), Buf("b41")]
        b_pT = [Buf("pT%d" % i) for i in range(3)]
        b_yst = [Buf("ys0"), Buf("ys1")]
        u32, b_u32 = _consts_tri(cx, "u32", F32, ALU.is_ge, -1, 1)
        su32, b_su32 = _consts_tri(cx, "su32", F32, ALU.is_gt, -1, 1)
        ui32 = u32
        ones32, b_ones32 = _consts_tri(cx, "ones32", F32, ALU.is_ge, 0, 0)

        identf, b_identf = _consts_tri(cx, "identf", F32, ALU.is_equal, 1, -1)
        g4 = {nm: io["gath" + nm].rearrange("(r h d) t -> r h d t", h=8, d=128) for nm in ("q", "k", "v")}
        def ld_lfT(e):
            cv = dyn(e, 3, 0, NCORE - 1)
            return e.dma_start(out=lfT[:], in_=gathlf[cv])
        P.dma("sp", ld_lfT, b_lfT, reads=[b_glf, b_cid], writes=[b_lfT])
        P.op("pe", lambda e: e.transpose(out=ps[:, 7, 256:384], in_=lfT[:], identity=identf[:]), reads=[b_lfT, b_identf], writes=[b_ps[7]])
        P.op("dve", lambda e: e.tensor_copy(out=lf[:], in_=ps[:, 7, 256:384]), reads=[b_ps[7]], writes=[b_lf])
        vflat = vv.rearrange("p a b -> p (a b)")
        for hf in range(2):
            r0, r1 = hf * 4, hf * 4 + 4
            for sec, dst, bb in (("k", kT, b_k), ("q", qT, b_q), ("v", vflat, b_v)):
                def ld(e, sec=sec, dst=dst, r0=r0, r1=r1):
                    cv = dyn(e, 3, 0, NCORE - 1)
                    return e.dma_start(out=dst[:, r0 * PCW:r1 * PCW].rearrange("p (r t) -> p r t", r=4),
                                       in_=g4[sec][r0:r1, cv].rearrange("r d t -> d r t"))
                P.dma("sp", ld, bb[r0], reads=[io["b_gath" + sec], b_cid], writes=bb[r0:r1])

        P.op("dve", lambda e: e.tensor_reduce(out=tot[:, 0:1], in_=lfT[:], axis=AX.X, op=ALU.add), reads=[b_lfT], writes=[b_tot])
        P.op("dve", lambda e: e.tensor_scalar(out=totbc[:], in0=ones32[:], scalar1=tot[:, 0:1], scalar2=None, op0=ALU.mult),
             reads=[b_ones32, b_tot], writes=[b_totbc])
        P.op("pe", lambda e: e.matmul(ps[:, 7, 0:128], lhsT=u32[:], rhs=lf[:], start=True, stop=False), reads=[b_u32, b_lf], writes=[b_ps[7]])
        P.op("pe", lambda e: e.matmul(ps[:, 7, 0:128], lhsT=totbc[:], rhs=su32[:], start=False, stop=True), reads=[b_totbc, b_su32], writes=[b_ps[7]], waw=False)
        P.op("pe", lambda e: e.matmul(ps[:, 7, 128:256], lhsT=totbc[:], rhs=ui32[:], start=True, stop=True), reads=[b_totbc, b_u32], writes=[b_ps[7]], waw=False)
        P.op("dve", lambda e: e.tensor_copy(out=ck[:], in_=ps[:, 7, 0:128]), reads=[b_ps[7]], writes=[b_ck])
        P.op("dve", lambda e: e.tensor_copy(out=cend[:], in_=ps[:, 7, 128:256]), reads=[b_ps[7]], writes=[b_cend])

        tiles = [(qi, kb) for qi in range(nq) for kb in range(4 * qi + 4)]
        LOOK = 2
        accL, b_accL = [mu, var], [b_mu, b_var]

        def t_qk(t):
            qi, kb = tiles[t]
            c0 = 128 * max(0, kb - 4 * qi)
            sb_ = t % 3
            kpc, qpc = (kb * 128) // PCW, (qi * 512) // PCW
            P.op("pe", (lambda sb_, kb, qi, c0: lambda e: e.matmul(ps[:, sb_, c0:512], lhsT=kT[:, kb * 128:(kb + 1) * 128],
                                                                   rhs=qT[:, qi * 512 + c0:(qi + 1) * 512], start=True, stop=True))(sb_, kb, qi, c0),
                 reads=[b_k[kpc], b_q[qpc]], writes=[b_ps[sb_]])

        def t_exp(t):
            qi, kb = tiles[t]
            b4, bb4 = bias4[qi % 2], b_bias4[qi % 2]
            if kb == 0:
                for j2 in range(2):
                    g = 4 * qi + 2 * j2
                    P.op("dve", (lambda b4, j2, g: lambda e: e.tensor_scalar(out=b4[:, j2, 0:g + 2], in0=ck[:, 0:g + 2], scalar1=-1.0, scalar2=cend[:, g:g + 1],
                                                                             op0=ALU.mult, op1=ALU.add))(b4, j2, g),
                         reads=[b_ck, b_cend], writes=[bb4], waw=(j2 == 0))
            j0 = max(0, kb - 4 * qi)
            c0 = 128 * j0
            sb_ = t % 3
            pt, bpt = pT[sb_], b_pT[sb_]
            firstw = True
            for j2 in range(2):
                lo, hi = max(c0, 256 * j2), 256 * (j2 + 1)
                if lo >= hi:
                    continue
                P.op("act", (lambda pt, sb_, lo, hi, j2, b4, kb: lambda e: e.activation(out=pt[:, lo:hi], in_=ps[:, sb_, lo:hi],
                                                                                       func=AF.Exp, bias=b4[:, j2, kb:kb + 1], scale=SCALE))(pt, sb_, lo, hi, j2, b4, kb),
                     reads=[b_ps[sb_], bb4], writes=[bpt], waw=firstw)
                firstw = False
            if kb >= 4 * qi:
                P.op("pool", (lambda pt, c0: lambda e: e.tensor_tensor(out=pt[:, c0:c0 + 128], in0=pt[:, c0:c0 + 128], in1=tri_b[:], op=ALU.mult))(pt, c0),
                     reads=[bpt, b_tri_b], writes=[bpt])

        def t_pv(t):
            qi, kb = tiles[t]
            nkb = 4 * qi + 4
            c0 = 128 * max(0, kb - 4 * qi)
            sb_ = t % 3
            pt, bpt = pT[sb_], b_pT[sb_]
            ob, lb = 3 + qi % 2, 5 + qi % 2
            kpc = (kb * 128) // PCW
            first, last = (kb == 0), (kb == nkb - 1)
            P.op("pe", (lambda ob, kb, pt, c0, first, last: lambda e: e.matmul(ps[:, ob, c0:512], lhsT=vv[:, kb, :], rhs=pt[:, c0:512],
                                                                               start=first, stop=last, skip_group_check=True))(ob, kb, pt, c0, first, last),
                 reads=[b_v[kpc], bpt], writes=[b_ps[ob]], waw=first)
            accl, baccl = accL[qi % 2], b_accL[qi % 2]
            if first:
                P.op("dve", (lambda accl, pt: lambda e: e.tensor_copy(out=accl[:], in_=pt[:]))(accl, pt), reads=[bpt], writes=[baccl])
            else:
                P.op("dve", (lambda accl, pt, c0: lambda e: e.tensor_tensor(out=accl[:, c0:512], in0=accl[:, c0:512], in1=pt[:, c0:512], op=ALU.add))(accl, pt, c0),
                     reads=[bpt, baccl], writes=[baccl])
            if last:
                P.op("pe", (lambda lb, accl: lambda e: e.matmul(ps[:, lb, :], lhsT=ones32[:], rhs=accl[:], start=True, stop=True))(lb, accl),
                     reads=[b_ones32, baccl], writes=[b_ps[lb]])
                yi = qi % 2
                P.op("dve", (lambda lb: lambda e: e.reciprocal(out=rinv[:], in_=ps[:, lb, :]))(lb), reads=[b_ps[lb]], writes=[b_rinv])
                P.op("dve", (lambda ob, yi: lambda e: e.tensor_tensor(out=yst[yi][:], in0=ps[:, ob, :], in1=rinv[:], op=ALU.mult))(ob, yi),
                     reads=[b_ps[ob], b_rinv], writes=[b_yst[yi]])
                P.dma("sp", (lambda qi, yi: lambda e: e.dma_start(out=o_yatt[:, qi * 512:(qi + 1) * 512], in_=yst[yi][:]))(qi, yi),
                      b_yst[yi], reads=[b_yst[yi]], writes=[b_yatt], waw=False)
                for _ in range(4):
                    next(taps_gen, None)

        for t in range(-LOOK, len(tiles)):
            if t + LOOK < len(tiles):
                t_qk(t + LOOK)
            if t >= 0:
                t_exp(t)
                t_pv(t)
    for _ in taps_gen:
        pass
    conv_post()


def emit_C(cx, io, li):
    nc, P = cx.nc, cx.P
    cx.new_phase()
    yc_d = io["yconv"]
    ys_d = io["ysgu"]
    gathya = io["gathya"]
    gy = gathya.rearrange("hd (c t) -> c hd t", c=NCORE)
    ga_d = io["gatt"]
    h_d = io["h_in"][li]
    p_d = io["p"][li]
    wo_d = io["w_out"][li]
    wpg_d = io["w_pg"][li]
    wpp_d = io["w_pp"][li]
    gpost_d = io["gpost"][li]
    o_h = io["h_out"][li]
    hmid = io["hmid"]
    b_own, b_yconv, b_gya, b_cid = io["b_own"], io["b_yconv"], io["b_gathya"], io["b_cid"]
    b_hin, b_hout = io["b_h"][li], io["b_h"][li + 1]
    cid_sb = io["cid_sb"]

    def dyn(e, idx, lo, hi):
        key = (id(e), idx)
        if key not in cx.vals:
            reg = e.alloc_register("dyn%d" % idx)
            e.reg_load(reg, cid_sb[0:1, idx:idx + 1])
            cx.vals[key] = e.snap(reg, min_val=lo, max_val=hi)
        return cx.vals[key]

    ps = cx.ps
    b_ps = cx.b_ps
    yT = cx.sb("yT", [128, NKC, TOK], BF16)
    W = cx.sb("W", [128, NKC, D], BF16)
    wpp = cx.sb("wpp", [128, 2, D], BF16)
    gpost = cx.sb("gpost", [128, D], F32)
    gat = cx.sb("gat", [128, 1, TOK], BF16)
    hb = [cx.sb("hb%d" % i, [128, D], F32) for i in range(2)]
    tmp = cx.sb("tmp", [128, D], F32)
    hmb = cx.sb("hmb", [128, D], BF16)
    junk = hmb
    pb32 = cx.sb("pb32", [128, 256], F32)
    pbb = cx.sb("pbb", [128, 256], BF16)
    hmT = cx.sb("hmT", [128, NKC, 128], BF16)
    pTb = cx.sb("pTb", [128, 2, 128], BF16)
    sig = [cx.sb("sig%d" % i, [128, 512], F32) for i in range(2)]
    ost = cx.sb("ost", [128, D], F32)
    small = cx.sb("small", [128, 8], F32)
    b_yT = [Buf("yT%d" % i) for i in range(NKC)]
    b_W = [Buf("W%d" % i) for i in range(4)]
    b_wpp, b_gpost, b_tmp, b_junk_unused, b_hmb, b_pb32, b_pbb, b_hmT, b_pTb, b_small = [
        Buf(x) for x in "wpp gpost tmp junk hmb pb32 pbb hmT pTb small".split()]
    b_gat = Buf("gat")
    b_hb = [Buf("hb0"), Buf("hb1")]
    b_sig = [Buf("sig0"), Buf("sig1")]
    b_ost = Buf("ost")
    b_hmid = [Buf("hmid%d" % i) for i in range(NTB)]
    identb, b_identb = _consts_tri(cx, "identb", BF16, ALU.is_equal, 1, -1)

    P.dma("sp", lambda e: e.dma_start(out=gpost[:], in_=gpost_d), b_gpost, writes=[b_gpost])
    wov = wo_d.rearrange("(kc p) n -> p kc n", p=128)
    P.dma("pool", lambda e: e.dma_start(out=W[:], in_=wov), b_W[0], writes=b_W)
    for c in range(4):
        P.dma("sp", (lambda c: lambda e: e.dma_start(out=yT[:, c, :], in_=yc_d[c * 128:(c + 1) * 128, :]))(c), b_yT[c], reads=[b_yconv], writes=[b_yT[c]])
        P.dma("sp", (lambda c: lambda e: e.dma_start(out=yT[:, 4 + c, :], in_=ys_d[c * 128:(c + 1) * 128, :]))(c), b_yT[4 + c], reads=[b_own], writes=[b_yT[4 + c]])
    for hf in range(2):
        def ld_ya(e, hf=hf):
            cv = dyn(e, 3, 0, NCORE - 1)
            return e.dma_start(out=yT[:, 8 + hf * 4:12 + hf * 4, :], in_=gy[cv, hf * 512:(hf + 1) * 512, :].rearrange("(h d) t -> d h t", d=128))
        P.dma("sp", ld_ya, b_yT[8 + hf * 4], reads=[b_gya, b_cid], writes=b_yT[8 + hf * 4:12 + hf * 4])
    for c in range(8):
        P.dma("sp", (lambda c: lambda e: e.dma_start(out=gat[:, 0, :], in_=ga_d[c * 128:(c + 1) * 128, :]))(c), b_gat, reads=[b_own], writes=[b_gat])
        eng = "pool" if c % 2 == 0 else "dve"
        P.op(eng, (lambda c: lambda e: e.tensor_tensor(out=yT[:, 8 + c, :], in0=yT[:, 8 + c, :], in1=gat[:, 0, :], op=ALU.mult))(c),
             reads=[b_yT[8 + c], b_gat], writes=[b_yT[8 + c]])

    hv = h_d.rearrange("(tb p) d -> tb p d", p=128)
    hmv = hmid.rearrange("(tb p) d -> tb p d", p=128)
    ohv = o_h.rearrange("(tb p) d -> tb p d", p=128)
    pv = p_d.rearrange("(tb p) d -> tb p d", p=128)
    for tb in range(NTB):
        base = 4 * (tb % 2)
        for ct in range(4):
            for kc in range(NKC):
                P.op("pe", (lambda ct, kc, tb, base: lambda e: e.matmul(ps[:, base + ct, :], lhsT=yT[:, kc, tb * 128:(tb + 1) * 128],
                                                                        rhs=W[:, kc, ct * 512:(ct + 1) * 512], start=(kc == 0), stop=(kc == NKC - 1)))(ct, kc, tb, base),
                     reads=[b_yT[kc], b_W[ct]], writes=[b_ps[base + ct]], waw=(kc == 0))
        hbt, bhb = hb[tb % 2], b_hb[tb % 2]
        P.dma("sp", (lambda hbt, tb: lambda e: e.dma_start(out=hbt[:], in_=hv[tb]))(hbt, tb), bhb, reads=[b_hin], writes=[bhb])
        pfull = ps[:, base:base + 4, :]
        bpf = b_ps[base:base + 4]
        P.op("act", (lambda pfull: lambda e: e.activation(out=junk[:].rearrange("p (c n) -> p c n", c=4), in_=pfull, func=AF.Square, accum_out=small[:, 0:1]))(pfull),
             reads=bpf, writes=[b_hmb, b_small])
        P.op("act", lambda e: e.activation(out=small[:, 1:2], in_=small[:, 0:1], func=AF.Sqrt, bias=EPS, scale=1.0 / D), reads=[b_small], writes=[b_small])
        P.op("dve", lambda e: e.reciprocal(out=small[:, 2:3], in_=small[:, 1:2]), reads=[b_small], writes=[b_small])
        P.op("dve", (lambda pfull: lambda e: e.scalar_tensor_tensor(out=tmp[:].rearrange("p (c n) -> p c n", c=4), in0=pfull, scalar=small[:, 2:3],
                                                                    in1=gpost[:].rearrange("p (c n) -> p c n", c=4), op0=ALU.mult, op1=ALU.mult))(pfull),
             reads=bpf + [b_small, b_gpost], writes=[b_tmp])
        P.op("dve", (lambda hbt: lambda e: e.tensor_tensor(out=tmp[:], in0=tmp[:], in1=hbt[:], op=ALU.add))(hbt), reads=[b_tmp, bhb], writes=[b_tmp])
        P.dma("sp", (lambda tb: lambda e: e.dma_start(out=hmv[tb], in_=tmp[:]))(tb), b_tmp, reads=[b_tmp], writes=[b_hmid[tb]])

    wgv = wpg_d.rearrange("(kc p) n -> p kc n", p=128)
    P.dma("pool", lambda e: e.dma_start(out=W[:], in_=wgv), b_W[0], writes=b_W)
    P.dma("pool", lambda e: e.dma_start(out=wpp[:], in_=wpp_d.rearrange("(kc p) n -> p kc n", p=128)), b_wpp, writes=[b_wpp])
    rot = 0
    for tb in range(NTB):
        hbt, bhb = hb[tb % 2], b_hb[tb % 2]
        P.dma("sp", (lambda hbt, tb: lambda e: e.dma_start(out=hbt[:], in_=hmv[tb]))(hbt, tb), bhb, reads=[b_hmid[tb]], writes=[bhb])
        P.dma("sp", (lambda tb: lambda e: e.dma_start(out=pb32[:], in_=pv[tb]))(tb), b_pb32, writes=[b_pb32])
        P.op("act", (lambda hbt: lambda e: e.activation(out=hmb[:], in_=hbt[:], func=AF.Copy))(hbt), reads=[bhb], writes=[b_hmb])
        P.op("act", lambda e: e.activation(out=pbb[:], in_=pb32[:], func=AF.Copy), reads=[b_pb32], writes=[b_pbb])
        for half in range(2):
            pbk = ps[:, half, :].bitcast(BF16)
            for j in range(8):
                kc = half * 8 + j
                P.op("pe", (lambda pbk, j, kc: lambda e: e.transpose(out=pbk[:, j * 128:(j + 1) * 128], in_=hmb[:, kc * 128:(kc + 1) * 128], identity=identb[:]))(pbk, j, kc),
                     reads=[b_hmb, b_identb], writes=[b_ps[half]], waw=(j == 0))
            P.op("dve", (lambda pbk, half: lambda e: e.tensor_copy(out=hmT[:, half * 8:(half + 1) * 8, :], in_=pbk.rearrange("p (j t) -> p j t", j=8)))(pbk, half),
                 reads=[b_ps[half]], writes=[b_hmT], waw=(half == 0))
        pbk = ps[:, 0, :].bitcast(BF16)
        for j in range(2):
            P.op("pe", (lambda pbk, j: lambda e: e.transpose(out=pbk[:, j * 128:(j + 1) * 128], in_=pbb[:, j * 128:(j + 1) * 128], identity=identb[:]))(pbk, j),
                 reads=[b_pbb, b_identb], writes=[b_ps[0]], waw=(j == 0))
        P.op("dve", (lambda pbk: lambda e: e.tensor_copy(out=pTb[:], in_=pbk[:, 0:256].rearrange("p (j t) -> p j t", j=2)))(pbk),
             reads=[b_ps[0]], writes=[b_pTb])
        for ct in range(4):
            gb_, pb_ = 2 + 2 * (rot % 3), 3 + 2 * (rot % 3)
            si = rot % 2
            rot += 1
            for kc in range(NKC):
                P.op("pe", (lambda gb_, kc, ct: lambda e: e.matmul(ps[:, gb_, :], lhsT=hmT[:, kc, :], rhs=W[:, kc, ct * 512:(ct + 1) * 512],
                                                                   start=(kc == 0), stop=(kc == NKC - 1)))(gb_, kc, ct),
                     reads=[b_hmT, b_W[ct]], writes=[b_ps[gb_]], waw=(kc == 0))
            for kc in range(2):
                P.op("pe", (lambda pb_, kc, ct: lambda e: e.matmul(ps[:, pb_, :], lhsT=pTb[:, kc, :], rhs=wpp[:, kc, ct * 512:(ct + 1) * 512],
                                                                   start=(kc == 0), stop=(kc == 1)))(pb_, kc, ct),
                     reads=[b_pTb, b_wpp], writes=[b_ps[pb_]], waw=(kc == 0))
            P.op("act", (lambda si, gb_: lambda e: e.activation(out=sig[si][:], in_=ps[:, gb_, :], func=AF.Sigmoid))(si, gb_),
                 reads=[b_ps[gb_]], writes=[b_sig[si]])
            P.op("dve", (lambda si, pb_: lambda e: e.tensor_tensor(out=sig[si][:], in0=sig[si][:], in1=ps[:, pb_, :], op=ALU.mult))(si, pb_),
                 reads=[b_sig[si], b_ps[pb_]], writes=[b_sig[si]])
            P.op("dve", (lambda si, hbt, ct: lambda e: e.tensor_tensor(out=ost[:, ct * 512:(ct + 1) * 512], in0=sig[si][:], in1=hbt[:, ct * 512:(ct + 1) * 512], op=ALU.add))(si, hbt, ct),
                 reads=[b_sig[si], bhb], writes=[b_ost], waw=(ct == 0))
        P.dma("sp", (lambda tb: lambda e: e.dma_start(out=ohv[tb], in_=ost[:]))(tb),
              b_ost, reads=[b_ost], writes=[b_hout], waw=False)


I32 = mybir.dt.int32
DBG = {}


def build_fused():
    cx = Ctx("F")
    nc, P = cx.nc, cx.P
    io = {}
    x = cx.din("x", [TOK, D], F32)
    out = cx.dout("out", [TOK, D], F32)
    io["p"] = cx.din("p", [2, TOK, 256], F32)
    io["w_in"] = cx.din("w_in", [2, D, NIN], F32)
    io["gpre"] = cx.din("gpre", [2, 128, D], F32)
    io["bfb"] = cx.din("bfb", [2, 128, 8], F32)
    io["sgg"] = cx.din("sgg", [2, 128, 512], F32)
    io["sgb"] = cx.din("sgb", [2, 128, 512], F32)
    io["sguw"] = cx.din("sguw", [2, 128, 4, 128], F32)
    io["sgub"] = cx.din("sgub", [2, 128, 512], F32)
    io["cpar"] = cx.din("cpar", [2, 128, 4, 36], F32)
    io["pw"] = cx.din("pw", [2, 512, 512], F32)
    io["w_out"] = cx.din("w_out", [2, D, D], F32)
    io["w_pg"] = cx.din("w_pg", [2, D, D], F32)
    io["w_pp"] = cx.din("w_pp", [2, 256, D], F32)
    io["gpost"] = cx.din("gpost", [2, 128, D], F32)
    cid = cx.din("cid", [1, 8], I32)
    cmask = cx.din("cmask", [128, 1], F32)
    h1 = cx.dscr("h1", [TOK, D], F32)
    io["hmid"] = cx.dscr("hmid", [TOK, D], F32)
    for nm in ("q", "k", "v"):
        io["blob" + nm] = cx.dscr("blob" + nm, [1024, TOK], BF16)
        io["gath" + nm] = cx.dscr("gath" + nm, [NCORE * 1024, TOK], BF16)
    io["bloblf"] = cx.dscr("bloblf", [16, 1024], F32)
    io["gathlf"] = cx.dscr("gathlf", [NCORE * 16, 1024], F32)
    io["blobhalo"] = cx.dscr("blobhalo", [512, 32], F32)
    io["gathhalo"] = cx.dscr("gathhalo", [NCORE * 512, 32], F32)
    io["glu"] = cx.dscr("glu", [512, TOK], F32)
    io["gatt"] = cx.dscr("gatt", [1024, TOK], BF16)
    io["gconv"] = cx.dscr("gconv", [512, TOK], BF16)
    io["ysgu"] = cx.dscr("ysgu", [512, TOK], BF16)
    io["yconv"] = cx.dscr("yconv", [512, TOK], BF16)
    io["yatt"] = cx.dscr("yatt", [128, S], BF16)
    io["gathya"] = cx.dscr("gathya", [NCORE * 128, S], BF16)
    io["h_in"] = [x, h1]
    io["h_out"] = [h1, out]
    io["b_h"] = [Buf("hx"), Buf("h1"), Buf("hout")]
    for n in ("blobq", "gathq", "blobk", "gathk", "blobv", "gathv", "bloblf", "gathlf", "blobhalo", "gathhalo", "own", "yconv", "yatt", "gathya", "cid"):
        io["b_" + n] = Buf(n)

    _consts_tri(cx, "identb", BF16, ALU.is_equal, 1, -1)
    _consts_tri(cx, "identf", F32, ALU.is_equal, 1, -1)
    _consts_tri(cx, "ones_b", BF16, ALU.is_ge, 0, 0)
    _consts_tri(cx, "tri_b", BF16, ALU.is_ge, -1, 1)
    _consts_tri(cx, "u32", F32, ALU.is_ge, -1, 1)
    _consts_tri(cx, "su32", F32, ALU.is_gt, -1, 1)
    _consts_tri(cx, "ones32", F32, ALU.is_ge, 0, 0)
    cid_sb = cx.sb("cid_sb", [1, 8], I32)
    cmask_sb = cx.sb("cmask_sb", [128, 1], F32)
    io["cid_sb"], io["cmask_sb"] = cid_sb, cmask_sb
    P.dma("sp", lambda e: e.dma_start(out=cid_sb[:], in_=cid), io["b_cid"], writes=[io["b_cid"]], waw=False)
    P.dma("sp", lambda e: e.dma_start(out=cmask_sb[:], in_=cmask), io["b_cid"], writes=[io["b_cid"]], waw=False)
    cx.persist_done()

    def allgather(src, dst, b_src, b_dst):
        P.dma("pool", lambda e: e.collective_compute("AllGather", ALU.bypass, replica_groups=[list(range(NCORE))],
                                                     ins=[src.opt()], outs=[dst.opt()]),
              b_dst, reads=[b_src], writes=[b_dst], inc=1)

    io["ag"] = lambda nm: allgather(io["blob" + nm], io["gath" + nm], io["b_blob" + nm], io["b_gath" + nm])
    nl = DBG.get("layers", 2)
    if nl == 1:
        io["h_out"] = [out, out]
    for li in range(nl):
        emit_A(cx, io, li)
        allgather(io["bloblf"], io["gathlf"], io["b_bloblf"], io["b_gathlf"])
        allgather(io["blobhalo"], io["gathhalo"], io["b_blobhalo"], io["b_gathhalo"])
        emit_B(cx, io, li, nq=DBG.get("nq", S // 512))
        allgather(io["yatt"], io["gathya"], io["b_yatt"], io["b_gathya"])
        emit_C(cx, io, li)
    return cx.finish()


_NC = {}


def _bc(v, n=128):
    v = np.asarray(v, np.float32)
    return np.ascontiguousarray(np.broadcast_to(v.reshape(1, -1), (n, v.size)))


def kernel(**inputs):
    W = {k: np.asarray(v) for k, v in inputs.items()}
    perm = w_in_perm()
    L = range(2)
    cpar = np.zeros((2, 512, 36), np.float32)
    for li in L:
        cpar[li, :, 0:31] = W["conv_dw"][li].T
        cpar[li, :, 31] = W["conv_dw_b"][li]
        cpar[li, :, 32] = W["conv_ln_g"][li]
        cpar[li, :, 33] = W["conv_ln_b"][li]
        cpar[li, :, 34] = W["conv_pw_b"][li]
    cpar = np.ascontiguousarray(cpar.reshape(2, 4, 128, 36).transpose(0, 2, 1, 3))
    common = {
        "w_in": np.ascontiguousarray(W["w_in"][:, :, perm]),
        "gpre": np.stack([_bc(W["norm_pre"][li]) for li in L]),
        "bfb": np.stack([_bc(W["b_f"][li]) for li in L]),
        "sgg": np.stack([_bc(W["sgu_ln_g"][li]) for li in L]),
        "sgb": np.stack([_bc(W["sgu_ln_b"][li]) for li in L]),
        "sguw": np.ascontiguousarray(np.transpose(W["sgu_w"], (0, 2, 1, 3))),
        "sgub": np.stack([_bc(W["sgu_b"][li].reshape(-1)) for li in L]),
        "cpar": cpar,
        "pw": np.ascontiguousarray(W["conv_pw"]),
        "w_out": np.ascontiguousarray(W["w_out"]),
        "w_pg": np.ascontiguousarray(W["w_pg"]),
        "w_pp": np.ascontiguousarray(W["w_pp"]),
        "gpost": np.stack([_bc(W["norm_post"][li]) for li in L]),
    }
    x = W["x"][0]
    in_maps = []
    for c in range(NCORE):
        cid = np.array([[c * 128, c * TOK, max(c - 1, 0), c, 0, 0, 0, 0]], np.int32)
        in_maps.append(dict(common,
                            x=np.ascontiguousarray(x[c * TOK:(c + 1) * TOK]),
                            p=np.ascontiguousarray(W["p"][:, 0, c * TOK:(c + 1) * TOK]),
                            cid=cid,
                            cmask=np.full((128, 1), 0.0 if c == 0 else 1.0, np.float32)))
    if "F" not in _NC:
        _NC["F"] = build_fused()
    res = run_bass_kernel_spmd(_NC["F"], in_maps, core_ids=list(range(NCORE)))
    out = np.concatenate([r["out"] for r in res.results], axis=0)
    return out.reshape(1, S, D).astype(np.float32)
```

```python
import numpy as np
from contextlib import ExitStack
import ml_dtypes
import concourse.bass as bass
import concourse.mybir as mybir
from concourse.bass_utils import run_bass_kernel_spmd

F32 = mybir.dt.float32
BF16 = mybir.dt.bfloat16
AF = mybir.ActivationFunctionType
ALU = mybir.AluOpType
AX = mybir.AxisListType

NCORE = 8
S = 16384
D = 2048
TOK = S // NCORE
NTB = TOK // 128
NKC = D // 128
NIN = 7176
EPS = 1e-6
SCALE = 128 ** -0.5
CONV_K = 31
HALO = CONV_K - 1
NKB = S // 128

SAME_ENG_SYNC = True
SAME_ENG_DIST = 3


class Buf:
    _n = 0

    def __init__(self, name=""):
        Buf._n += 1
        self.id = Buf._n
        self.name = name
        self.w = {}
        self.r = {}


class Prog:
    ENG = ("pe", "act", "dve", "pool", "sp")

    def __init__(self, nc):
        self.nc = nc
        self.q = {e: [] for e in self.ENG}
        self.cnt = {e: 0 for e in self.ENG}
        self.seen = {e: {} for e in self.ENG}

    def _deps(self, e, reads, writes, waw):
        need = {}
        for b in reads:
            for k, v in b.w.items():
                if need.get(k, 0) < v:
                    need[k] = v
        for b in writes:
            its = list(b.r.items())
            if waw:
                its += list(b.w.items())
            for k, v in its:
                if need.get(k, 0) < v:
                    need[k] = v
        waits = []
        seen = self.seen[e]
        for k, v in need.items():
            if k == e and (e == "pe" or not SAME_ENG_SYNC):
                continue
            if k == e and self.cnt[e] - v >= SAME_ENG_DIST:
                continue
            if seen.get(k, 0) >= v:
                continue
            seen[k] = v
            waits.append((k, v))
        return waits

    def _mark(self, k, v, reads, writes, waw):
        for b in reads:
            b.r[k] = v
        for b in writes:
            if waw:
                b.w = {k: v}
                b.r = {}
            else:
                b.w[k] = v

    def op(self, e, fn, reads=(), writes=(), waw=True):
        waits = self._deps(e, reads, writes, waw)
        self.cnt[e] += 1
        self.q[e].append((waits, fn, e, 1))
        self._mark(e, self.cnt[e], reads, writes, waw)

    def dma(self, qe, fn, owner, reads=(), writes=(), waw=True, inc=16):
        waits = self._deps(qe, reads, writes, waw)
        k = ("d", owner.name)
        self.cnt[k] = self.cnt.get(k, 0) + inc
        self.q[qe].append((waits, fn, k, inc))
        self._mark(k, self.cnt[k], reads, writes, waw)

    def barrier(self):
        for e in self.ENG:
            waits = []
            for k, v in self.cnt.items():
                if v == 0 or (k == e and e == "pe"):
                    continue
                if self.seen[e].get(k, 0) >= v:
                    continue
                self.seen[e][k] = v
                waits.append((k, v))
            if waits:
                self.q[e].append((waits, None, None, 0))

    def emit(self, stack):
        nc = self.nc
        sems = {}
        print("[kernel] semaphores:", len(self.cnt), "ops:", {e: len(v) for e, v in self.q.items()})
        for k in self.cnt:
            nm = "s_" + (k if isinstance(k, str) else "d_" + k[1])
            sems[k] = stack.enter_context(nc.semaphore(nm))
        block = stack.enter_context(nc.Block())
        emap = {"pe": block.tensor, "act": block.scalar, "dve": block.vector,
                "pool": block.gpsimd, "sp": block.sync}

        def mk(e):
            items = self.q[e]

            def body(eng):
                for waits, fn, sk, inc in items:
                    for k, v in waits:
                        eng.wait_ge(sems[k], v)
                    if fn is not None:
                        fn(eng).then_inc(sems[sk], inc)
            return body

        for e in self.ENG:
            if self.q[e]:
                emap[e](mk(e))


ARENA_BYTES = 200 * 1024


class Ctx:
    def __init__(self, name):
        self.nc = bass.Bass("TRN2", target_bir_lowering=False, num_devices=NCORE)
        self.P = Prog(self.nc)
        self.st = ExitStack()
        self.name = name
        self.arena = self.st.enter_context(self.nc.sbuf_tensor("arena", [128, ARENA_BYTES // 2], BF16))
        self.off = 0
        self.base = 0
        self.ps = self.st.enter_context(self.nc.psum_tensor("ps", [128, 8, 512], F32))
        self.b_ps = [Buf("ps%d" % i) for i in range(8)]
        self.vals = {}

    def persist_done(self):
        self.base = self.off

    def new_phase(self):
        self.P.barrier()
        self.off = self.base

    def din(self, name, shape, dt):
        return self.nc.dram_tensor(name, list(shape), dt, kind="ExternalInput").ap()

    def dout(self, name, shape, dt):
        return self.nc.dram_tensor(name, list(shape), dt, kind="ExternalOutput").ap()

    def dscr(self, name, shape, dt):
        return self.nc.dram_tensor(name, list(shape), dt).ap()

    def sb(self, name, shape, dt):
        esz = 2 if dt == BF16 else 4
        n = 1
        for d_ in shape[1:]:
            n *= d_
        nb = (n * esz + 63) // 64 * 64
        assert self.off + nb <= ARENA_BYTES, (name, self.off, nb)
        v = self.arena[:, self.off // 2:(self.off + nb) // 2]
        self.off += nb
        if esz == 4:
            v = v.bitcast(dt)
        v = v[:, 0:n]
        if len(shape) == 3:
            v = v.rearrange("p (a b) -> p a b", a=shape[1])
        if shape[0] < 128:
            v = v[0:shape[0]]
        return v

    def finish(self):
        self.P.barrier()
        self.P.emit(self.st)
        self.st.close()
        return self.nc


def _consts_tri(cx, name, dt, op, cm, step):
    P = cx.P
    if not hasattr(cx, "consts"):
        cx.consts = {}
    if name in cx.consts:
        return cx.consts[name]
    assert cx.base == 0, "constants must be created before the first phase"
    t = cx.sb(name, [128, 128], dt)
    b = Buf(name)
    P.op("pool", lambda e: e.memset(t[:], 1.0), writes=[b])
    P.op("pool", lambda e: e.affine_select(out=t[:], in_=t[:], pattern=[[step, 128]], compare_op=op,
                                           fill=0.0, base=0, channel_multiplier=cm),
         reads=[b], writes=[b])
    cx.consts[name] = (t, b)
    return t, b


A_TILES = [("q", 0), ("q", 1), ("k", 0), ("k", 1), ("v", 0), ("v", 1), ("zatt", 0), ("zatt", 1), ("zconv", 0),
           ("glu", 0), ("glu", 1), ("usg", 0), ("usg", 1), ("vsgu", 0)]


def w_in_perm():
    o = {}
    names = ["q", "k", "v", "zatt", "f", "ga", "gb", "zconv", "u", "vsgu", "zsgu"]
    sizes = [1024, 1024, 1024, 1024, 8, 512, 512, 512, 512, 512, 512]
    c = 0
    for n, s in zip(names, sizes):
        o[n] = c
        c += s
    r = np.arange
    idx = [r(o["q"], o["q"] + 1024), r(o["k"], o["k"] + 1024), r(o["v"], o["v"] + 1024), r(o["zatt"], o["zatt"] + 1024),
           r(o["zconv"], o["zconv"] + 512)]
    for i in range(4):
        idx += [r(o["ga"] + 128 * i, o["ga"] + 128 * i + 128), r(o["gb"] + 128 * i, o["gb"] + 128 * i + 128)]
    for i in range(4):
        idx += [r(o["u"] + 128 * i, o["u"] + 128 * i + 128), r(o["zsgu"] + 128 * i, o["zsgu"] + 128 * i + 128)]
    idx += [r(o["vsgu"], o["vsgu"] + 512), r(o["f"], o["f"] + 8)]
    return np.concatenate(idx)


def emit_A(cx, io, li):
    nc, P = cx.nc, cx.P
    cx.new_phase()
    h = io["h_in"][li]
    w_in = io["w_in"][li]
    gpre = io["gpre"][li]
    bfb = io["bfb"][li]
    sgg = io["sgg"][li]
    sgb = io["sgb"][li]
    sguw = io["sguw"][li]
    sgub = io["sgub"][li]
    o_q = io["blobq"]
    o_k = io["blobk"]
    o_v = io["blobv"].rearrange("(h p) (kb d) -> p h kb d", p=128, d=128)
    o_lfT = io["bloblf"].rearrange("kb (h j) -> h kb j", j=128)
    o_halo = io["blobhalo"]
    o_glu = io["glu"]
    o_gatt = io["gatt"]
    o_gconv = io["gconv"]
    o_ysgu = io["ysgu"]
    b_blobq, b_blobk, b_blobv = io["b_blobq"], io["b_blobk"], io["b_blobv"]
    b_blob32, b_bloblf = io["b_blobhalo"], io["b_bloblf"]
    pending_ag = []
    b_own = io["b_own"]
    b_hin = io["b_h"][li]

    xnT = cx.sb("xnT", [128, NKC, TOK], BF16)
    wt = [cx.sb("wt%d" % i, [128, NKC, 1032], BF16) for i in range(2)]
    ug = cx.sb("ug", [128, 4, TOK], BF16)
    small = cx.sb("small", [128, 32], F32)
    bfb_s = cx.sb("bfb_s", [128, 8], F32)
    sgg_s = cx.sb("sgg_s", [128, 512], F32)
    sgb_s = cx.sb("sgb_s", [128, 512], F32)
    sguw_s = cx.sb("sguw_s", [128, 4, 128], F32)
    sgub_s = cx.sb("sgub_s", [128, 512], F32)
    wsT = cx.sb("wsT", [128, 4, 128], BF16)
    vln = cx.sb("vln", [128, 512], BF16)
    lfst = [cx.sb("lfst%d" % i, [128, 8], F32) for i in range(2)]
    lfT_sb = [cx.sb("lfT_sb%d" % i, [8, 128], F32) for i in range(2)]
    b_lfT_sb = [Buf("lfT_sb0"), Buf("lfT_sb1")]
    ps = cx.ps

    union0 = cx.off
    hblk = [cx.sb("hblk%d" % i, [128, D], F32) for i in range(2)]
    xn = cx.sb("xn", [128, D], BF16)
    gbc = cx.sb("gbc", [128, D], F32)
    junk = cx.sb("junk", [128, D], BF16)
    b_xnT = [Buf("xnT%d" % i) for i in range(NTB)]
    b_wt = [Buf("wt0"), Buf("wt1")]
    b_ug = Buf("ug")
    b_hblk = [Buf("hb0"), Buf("hb1")]
    b_xn, b_gbc, b_junk, b_small = Buf("xn"), Buf("gbc"), Buf("junk"), Buf("small")
    b_par = Buf("par")
    b_wsT = Buf("wsT")
    b_stg = [Buf("stg%d" % i) for i in range(3)]
    b_stf = [Buf("stf%d" % i) for i in range(3)]
    b_vln = Buf("vln")
    b_lfst = [Buf("lf0"), Buf("lf1")]
    b_ps = cx.b_ps
    b_sw = Buf("sguw")

    identb, b_identb = _consts_tri(cx, "identb", BF16, ALU.is_equal, 1, -1)
    identf, b_identf = _consts_tri(cx, "identf", F32, ALU.is_equal, 1, -1)

    P.dma("sp", lambda e: e.dma_start(out=gbc[:], in_=gpre), b_gbc, writes=[b_gbc])
    for t_sb, t_dr in ((bfb_s, bfb), (sgg_s, sgg), (sgb_s, sgb), (sgub_s, sgub)):
        P.dma("sp", (lambda a, b: lambda e: e.dma_start(out=a[:], in_=b))(t_sb, t_dr), b_par, writes=[b_par], waw=False)
    P.dma("sp", lambda e: e.dma_start(out=sguw_s[:], in_=sguw), b_sw, writes=[b_sw])

    for hh in range(4):
        P.op("pool", (lambda hh: lambda e: e.affine_select(
            out=sguw_s[:, hh, :], in_=sguw_s[:, hh, :], pattern=[[-1, 128]], compare_op=ALU.is_ge,
            fill=0.0, base=0, channel_multiplier=1))(hh), reads=[b_sw], writes=[b_sw])
    for hh in range(4):
        P.op("pe", (lambda hh: lambda e: e.transpose(out=ps[:, 7, hh * 128:(hh + 1) * 128], in_=sguw_s[:, hh, :],
                                                     identity=identf[:]))(hh),
             reads=[b_sw, b_identf], writes=[b_ps[7]], waw=False)
    P.op("dve", lambda e: e.tensor_copy(out=wsT[:].rearrange("p h t -> p (h t)"), in_=ps[:, 7, :]),
         reads=[b_ps[7]], writes=[b_wsT])

    hv = h.rearrange("(tb p) d -> tb p d", p=128)
    for tb in range(NTB):
        hb, bh = hblk[tb % 2], b_hblk[tb % 2]
        P.dma("sp", (lambda hb, tb: lambda e: e.dma_start(out=hb[:], in_=hv[tb]))(hb, tb), bh, reads=[b_hin], writes=[bh])
        ss = small[:, 0:1]
        P.op("act", (lambda hb: lambda e: e.activation(out=junk[:], in_=hb[:], func=AF.Square, accum_out=small[:, 0:1]))(hb),
             reads=[bh], writes=[b_junk, b_small])
        P.op("act", lambda e: e.activation(out=small[:, 1:2], in_=small[:, 0:1], func=AF.Sqrt, bias=EPS, scale=1.0 / D),
             reads=[b_small], writes=[b_small])
        P.op("dve", lambda e: e.reciprocal(out=small[:, 2:3], in_=small[:, 1:2]), reads=[b_small], writes=[b_small])
        P.op("dve", (lambda hb: lambda e: e.scalar_tensor_tensor(out=xn[:], in0=hb[:], scalar=small[:, 2:3], in1=gbc[:],
                                                                  op0=ALU.mult, op1=ALU.mult))(hb),
             reads=[bh, b_small, b_gbc], writes=[b_xn])
        for half in range(2):
            bank = half
            pb = ps[:, bank, :].bitcast(BF16)
            for j in range(8):
                kc = half * 8 + j
                P.op("pe", (lambda pb, j, kc: lambda e: e.transpose(out=pb[:, j * 128:(j + 1) * 128],
                                                                    in_=xn[:, kc * 128:(kc + 1) * 128],
                                                                    identity=identb[:]))(pb, j, kc),
                     reads=[b_xn, b_identb], writes=[b_ps[bank]], waw=(j == 0))
            eng = "act" if half == 0 else "dve"
            dst = xnT[:, half * 8:(half + 1) * 8, tb * 128:(tb + 1) * 128]
            src = pb.rearrange("p (j t) -> p j t", j=8)
            if eng == "act":
                P.op("act", (lambda dst, src: lambda e: e.activation(out=dst, in_=src, func=AF.Copy))(dst, src),
                     reads=[b_ps[bank]], writes=[b_xnT[tb]], waw=False)
            else:
                P.op("dve", (lambda dst, src: lambda e: e.tensor_copy(out=dst, in_=src))(dst, src),
                     reads=[b_ps[bank]], writes=[b_xnT[tb]], waw=False)

    P.barrier()
    cx.off = union0
    stg = [cx.sb("stg%d" % i, [128, TOK], BF16) for i in range(3)]
    vstage = cx.sb("vstage", [128, 4, NTB, 128], BF16) if False else cx.sb("vstage", [128, 4 * NTB, 128], BF16)
    b_vstage = Buf("vstage")
    stf = [cx.sb("stf%d" % i, [128, 512], F32) for i in range(3)]
    wv = w_in.rearrange("(kc p) n -> p kc n", p=128)
    rot = {"bank": 0, "stg": 0, "stf": 0, "lf": 0}

    def nbank():
        b = 2 + rot["bank"] % 5
        rot["bank"] += 1
        return b

    def nstg():
        i = rot["stg"] % 3
        rot["stg"] += 1
        return i

    def nstf():
        i = rot["stf"] % 3
        rot["stf"] += 1
        return i

    def mm_feat(wti, cc, tg, bank):
        w = wt[wti][:, :, cbase[0]:cbase[0] + 520]
        for kc in range(NKC):
            P.op("pe", (lambda w, kc, cc, tg, bank: lambda e: e.matmul(
                ps[:, bank, :], lhsT=w[:, kc, cc * 128:(cc + 1) * 128], rhs=xnT[:, kc, tg * 512:(tg + 1) * 512],
                start=(kc == 0), stop=(kc == NKC - 1)))(w, kc, cc, tg, bank),
                reads=[b_wt[wti]] + b_xnT[tg * 4:(tg + 1) * 4], writes=[b_ps[bank]], waw=(kc == 0))

    def mm_tok(wti, tb, bank, c0, n):
        w = wt[wti][:, :, cbase[0]:cbase[0] + 520]
        for kc in range(NKC):
            P.op("pe", (lambda w, kc, tb, bank, c0, n: lambda e: e.matmul(
                ps[:, bank, 0:n], lhsT=xnT[:, kc, tb * 128:(tb + 1) * 128], rhs=w[:, kc, c0:c0 + n],
                start=(kc == 0), stop=(kc == NKC - 1)))(w, kc, tb, bank, c0, n),
                reads=[b_wt[wti], b_xnT[tb]], writes=[b_ps[bank]], waw=(kc == 0))

    col = 0
    cbase = [0]
    for ti, (kind, sub) in enumerate(A_TILES):
        if ti > 0 and A_TILES[ti - 1][0] in ("q", "k", "v") and A_TILES[ti - 1][1] == 1:
            pending_ag.append(A_TILES[ti - 1][0])
        wti = (ti // 2) % 2
        cbase[0] = (ti % 2) * 512
        if ti % 2 == 0:
            ncols = min(1024, NIN - col) if ti + 2 < len(A_TILES) else NIN - col
            P.dma("pool", (lambda wti, col, ncols: lambda e: e.dma_start(out=wt[wti][:, :, 0:ncols], in_=wv[:, :, col:col + ncols]))(wti, col, ncols),
                  b_wt[wti], writes=[b_wt[wti]])
            col += ncols
            while pending_ag:
                io["ag"](pending_ag.pop(0))
        if kind in ("q", "k", "zatt", "zconv"):
            dst = {"q": o_q, "k": o_k, "zatt": o_gatt, "zconv": o_gconv}[kind]
            b_dst = {"q": b_blobq, "k": b_blobk}.get(kind, b_own)
            func = AF.Copy if kind in ("q", "k") else AF.Silu
            for cc in range(4):
                si = nstg()
                for tg in range(4):
                    bank = nbank()
                    mm_feat(wti, cc, tg, bank)
                    P.op("act", (lambda si, bank, func, tg: lambda e: e.activation(out=stg[si][:, tg * 512:(tg + 1) * 512], in_=ps[:, bank, :], func=func))(si, bank, func, tg),
                         reads=[b_ps[bank]], writes=[b_stg[si]], waw=(tg == 0))
                r0 = (sub * 4 + cc) * 128
                P.dma("sp", (lambda dst, r0, si: lambda e: e.dma_start(out=dst[r0:r0 + 128, :], in_=stg[si][:]))(dst, r0, si),
                      b_stg[si], reads=[b_stg[si]], writes=[b_dst], waw=False)
        elif kind == "glu":
            for pr in range(2):
                ch = sub * 2 + pr
                for tg in range(4):
                    ba, bb = nbank(), nbank()
                    mm_feat(wti, 2 * pr, tg, ba)
                    mm_feat(wti, 2 * pr + 1, tg, bb)
                    s1, s2 = nstf(), nstf()
                    P.op("act", (lambda s1, bb: lambda e: e.activation(out=stf[s1][:], in_=ps[:, bb, :], func=AF.Sigmoid))(s1, bb),
                         reads=[b_ps[bb]], writes=[b_stf[s1]])
                    P.op("dve", (lambda s1, s2, ba: lambda e: e.tensor_tensor(out=stf[s2][:], in0=ps[:, ba, :], in1=stf[s1][:], op=ALU.mult))(s1, s2, ba),
                         reads=[b_ps[ba], b_stf[s1]], writes=[b_stf[s2]])
                    P.dma("sp", (lambda ch, tg, s2: lambda e: e.dma_start(out=o_glu[ch * 128:(ch + 1) * 128, tg * 512:(tg + 1) * 512], in_=stf[s2][:]))(ch, tg, s2),
                          b_stf[s2], reads=[b_stf[s2]], writes=[b_own], waw=False)
                    if tg == 3:
                        P.dma("sp", (lambda ch, s2: lambda e: e.dma_start(out=o_halo[ch * 128:(ch + 1) * 128, :], in_=stf[s2][:, 480:512]))(ch, s2),
                              b_stf[s2], reads=[b_stf[s2]], writes=[b_blob32], waw=False)
        elif kind == "usg":
            for pr in range(2):
                ch = sub * 2 + pr
                for tg in range(4):
                    ba, bb = nbank(), nbank()
                    mm_feat(wti, 2 * pr, tg, ba)
                    mm_feat(wti, 2 * pr + 1, tg, bb)
                    s1, s2 = nstf(), nstf()
                    P.op("act", (lambda s1, ba: lambda e: e.activation(out=stf[s1][:], in_=ps[:, ba, :], func=AF.Gelu))(s1, ba),
                         reads=[b_ps[ba]], writes=[b_stf[s1]])
                    P.op("act", (lambda s2, bb: lambda e: e.activation(out=stf[s2][:], in_=ps[:, bb, :], func=AF.Silu))(s2, bb),
                         reads=[b_ps[bb]], writes=[b_stf[s2]])
                    P.op("dve", (lambda ch, tg, s1, s2: lambda e: e.tensor_tensor(out=ug[:, ch, tg * 512:(tg + 1) * 512], in0=stf[s1][:], in1=stf[s2][:], op=ALU.mult))(ch, tg, s1, s2),
                         reads=[b_stf[s1], b_stf[s2]], writes=[b_ug], waw=False)
        elif kind == "v":
            for tb in range(NTB):
                bank = nbank()
                mm_tok(wti, tb, bank, 0, 512)
                P.op("act", (lambda tb, bank: lambda e: e.activation(out=vstage[:].rearrange("p (h k) d -> p h k d", h=4)[:, :, tb, :],
                                                                     in_=ps[:, bank, :].rearrange("p (h d) -> p h d", h=4), func=AF.Copy))(tb, bank),
                     reads=[b_ps[bank]], writes=[b_vstage], waw=(tb == 0))
            P.dma("sp", (lambda sub: lambda e: e.dma_start(out=o_v[:, sub * 4:(sub + 1) * 4, :, :],
                                                           in_=vstage[:].rearrange("p (h k) d -> p h k d", h=4)))(sub),
                  b_vstage, reads=[b_vstage], writes=[b_blobv], waw=False)
        elif kind == "vsgu":
            for tb in range(NTB):
                bank = nbank()
                mm_tok(wti, tb, bank, 512, 8)
                lfi = rot["lf"] % 2
                rot["lf"] += 1
                P.op("dve", (lambda lfi, bank: lambda e: e.tensor_tensor(out=lfst[lfi][:], in0=ps[:, bank, 0:8], in1=bfb_s[:], op=ALU.add))(lfi, bank),
                     reads=[b_ps[bank], b_par], writes=[b_lfst[lfi]])
                P.op("act", (lambda lfi: lambda e: e.activation(out=lfst[lfi][:], in_=lfst[lfi][:], func=AF.Exp, scale=-1.0))(lfi),
                     reads=[b_lfst[lfi]], writes=[b_lfst[lfi]])
                P.op("act", (lambda lfi: lambda e: e.activation(out=lfst[lfi][:], in_=lfst[lfi][:], func=AF.Ln, bias=1.0, scale=1.0))(lfi),
                     reads=[b_lfst[lfi]], writes=[b_lfst[lfi]])
                P.op("dve", (lambda lfi: lambda e: e.tensor_scalar(out=lfst[lfi][:], in0=lfst[lfi][:], scalar1=-1.0, scalar2=None, op0=ALU.mult))(lfi),
                     reads=[b_lfst[lfi]], writes=[b_lfst[lfi]])
                bank = nbank()
                P.op("pe", (lambda lfi, bank: lambda e: e.transpose(out=ps[0:8, bank, 0:128], in_=lfst[lfi][:], identity=identf[:]))(lfi, bank),
                     reads=[b_lfst[lfi], b_identf], writes=[b_ps[bank]])
                P.op("dve", (lambda tb, bank: lambda e: e.tensor_copy(out=lfT_sb[tb % 2][:], in_=ps[0:8, bank, 0:128]))(tb, bank),
                     reads=[b_ps[bank]], writes=[b_lfT_sb[tb % 2]])
                P.dma("sp", (lambda tb: lambda e: e.dma_start(out=o_lfT[:, tb, :], in_=lfT_sb[tb % 2][:]))(tb),
                      b_lfT_sb[tb % 2], reads=[b_lfT_sb[tb % 2]], writes=[b_bloblf], waw=False)
                bank = nbank()
                mm_tok(wti, tb, bank, 0, 512)
                s1 = nstf()
                P.op("act", (lambda s1, bank: lambda e: e.activation(out=stf[s1][:], in_=ps[:, bank, :], func=AF.Gelu))(s1, bank),
                     reads=[b_ps[bank]], writes=[b_stf[s1]])
                P.op("dve", (lambda s1: lambda e: e.bn_stats(out=small[:, 8:14], in_=stf[s1][:]))(s1),
                     reads=[b_stf[s1]], writes=[b_small])
                P.op("dve", lambda e: e.bn_aggr(out=small[:, 16:18], in_=small[:, 8:14]), reads=[b_small], writes=[b_small])
                P.op("act", lambda e: e.activation(out=small[:, 18:19], in_=small[:, 17:18], func=AF.Sqrt, bias=EPS, scale=1.0),
                     reads=[b_small], writes=[b_small])
                P.op("dve", lambda e: e.reciprocal(out=small[:, 19:20], in_=small[:, 18:19]), reads=[b_small], writes=[b_small])
                P.op("dve", (lambda s1: lambda e: e.tensor_scalar(out=stf[s1][:], in0=stf[s1][:], scalar1=small[:, 16:17], scalar2=small[:, 19:20],
                                                                   op0=ALU.subtract, op1=ALU.mult))(s1),
                     reads=[b_stf[s1], b_small], writes=[b_stf[s1]])
                P.op("dve", (lambda s1: lambda e: e.tensor_tensor(out=stf[s1][:], in0=stf[s1][:], in1=sgg_s[:], op=ALU.mult))(s1),
                     reads=[b_stf[s1], b_par], writes=[b_stf[s1]])
                P.op("dve", (lambda s1: lambda e: e.tensor_tensor(out=vln[:], in0=stf[s1][:], in1=sgb_s[:], op=ALU.add))(s1),
                     reads=[b_stf[s1], b_par], writes=[b_vln])
                bank = nbank()
                for hh in range(4):
                    P.op("pe", (lambda hh, bank: lambda e: e.matmul(ps[:, bank, hh * 128:(hh + 1) * 128], lhsT=vln[:, hh * 128:(hh + 1) * 128],
                                                                    rhs=wsT[:, hh, :], start=True, stop=True))(hh, bank),
                         reads=[b_vln, b_wsT], writes=[b_ps[bank]], waw=(hh == 0))
                s2 = nstf()
                P.op("dve", (lambda s2, bank: lambda e: e.tensor_tensor(out=stf[s2][:], in0=ps[:, bank, :], in1=sgub_s[:], op=ALU.add))(s2, bank),
                     reads=[b_ps[bank], b_par], writes=[b_stf[s2]])
                P.op("dve", (lambda s2, tb: lambda e: e.tensor_tensor(out=ug[:, :, tb * 128:(tb + 1) * 128],
                                                                      in0=stf[s2][:].rearrange("p (h t) -> p h t", h=4),
                                                                      in1=ug[:, :, tb * 128:(tb + 1) * 128], op=ALU.mult))(s2, tb),
                     reads=[b_stf[s2], b_ug], writes=[b_ug])
    while pending_ag:
        io["ag"](pending_ag.pop(0))
    for ch in range(4):
        P.dma("sp", (lambda ch: lambda e: e.dma_start(out=o_ysgu[ch * 128:(ch + 1) * 128, :], in_=ug[:, ch, :]))(ch),
              b_ug, reads=[b_ug], writes=[b_own], waw=False)


def emit_B(cx, io, li, do_conv=True, do_att=True, nq=S // 512):
    nc, P = cx.nc, cx.P
    cx.new_phase()
    gathhalo = io["gathhalo"].rearrange("(r c p) t -> r p c t", c=4, p=128)
    gathlf = io["gathlf"].rearrange("p (h j) -> h p j", j=128)
    glu_d = io["glu"]
    gconv_d = io["gconv"]
    cpar_d = io["cpar"][li]
    pw_d = io["pw"][li]
    o_yatt = io["yatt"]
    o_yconv = io["yconv"]
    b_g32, b_glf, b_own = io["b_gathhalo"], io["b_gathlf"], io["b_own"]
    b_yatt, b_yconv = io["b_yatt"], io["b_yconv"]
    b_cid = io["b_cid"]
    cid_sb, cmask = io["cid_sb"], io["cmask_sb"]

    def dyn(e, idx, lo, hi):
        key = (id(e), idx)
        if key not in cx.vals:
            reg = e.alloc_register("dyn%d" % idx)
            e.reg_load(reg, cid_sb[0:1, idx:idx + 1])
            cx.vals[key] = e.snap(reg, min_val=lo, max_val=hi)
        return cx.vals[key]

    ps = cx.ps
    b_ps = cx.b_ps
    ones_b, b_ones_b = _consts_tri(cx, "ones_b", BF16, ALU.is_ge, 0, 0)
    tri_b, b_tri_b = _consts_tri(cx, "tri_b", BF16, ALU.is_ge, -1, 1)

    if do_conv:
        ypad = [cx.sb("ypad%d" % i, [128, HALO + TOK], F32) for i in range(2)]
        acc = cx.sb("acc", [128, 4, TOK], F32)
        gconv = cx.sb("gconv", [128, 4, TOK], BF16)
        cpar = cx.sb("cpar", [128, 4, 36], F32)
        pw = cx.sb("pw", [128, 4, 512], BF16)
        accb = cx.sb("accb", [128, 4, 512], BF16)
        sqb = cx.sb("sqb", [128, 4, 512], BF16)
        sT = cx.sb("sT", [128, 4, 512], BF16)
        mu = cx.sb("mu", [128, 512], F32)
        var = cx.sb("var", [128, 512], F32)
        ycst = [cx.sb("ycst%d" % i, [128, 512], BF16) for i in range(2)]
        b_ypad = [Buf("yp0"), Buf("yp1")]
        b_acc = [Buf("acc%d" % i) for i in range(4)]
        b_gconv, b_cpar, b_pw, b_accb, b_sqb, b_sT, b_mu, b_var = [Buf(x) for x in "gconv cpar pw accb sqb sT mu var".split()]
        b_ycst = [Buf("yc0"), Buf("yc1")]
        P.dma("sp", lambda e: e.dma_start(out=cpar[:], in_=cpar_d), b_cpar, writes=[b_cpar])
        P.dma("sp", lambda e: e.dma_start(out=gconv[:], in_=gconv_d.rearrange("(c p) t -> p c t", p=128)), b_gconv, reads=[b_own], writes=[b_gconv])
        P.dma("pool", lambda e: e.dma_start(out=pw[:], in_=pw_d.rearrange("(c p) n -> p c n", p=128)), b_pw, writes=[b_pw])
        ypv = glu_d.rearrange("(c p) t -> c p t", p=128)
        halo_sb = cx.sb("halo_sb", [128, 4, 32], F32)
        b_halo = Buf("halo")

        def ld_halo(e):
            pv = dyn(e, 2, 0, NCORE - 1)
            return e.dma_start(out=halo_sb[:], in_=gathhalo[pv])
        P.dma("sp", ld_halo, b_halo, reads=[b_g32, b_cid], writes=[b_halo])
        def conv_taps():
            for cc in range(4):
                yp, byp = ypad[cc % 2], b_ypad[cc % 2]
                P.dma("sp", (lambda yp, cc: lambda e: e.dma_start(out=yp[:, HALO:HALO + TOK], in_=ypv[cc]))(yp, cc), byp, reads=[b_own], writes=[byp])

                P.op("dve", (lambda yp, cc: lambda e: e.tensor_scalar(out=yp[:, 0:HALO], in0=halo_sb[:, cc, 32 - HALO:32], scalar1=cmask[:, 0:1], scalar2=None, op0=ALU.mult))(yp, cc),
                     reads=[byp, b_halo, b_cid], writes=[byp], waw=False)
                P.op("dve", (lambda yp, cc: lambda e: e.tensor_scalar(out=acc[:, cc, :], in0=yp[:, 0:TOK], scalar1=cpar[:, cc, 0:1],
                                                                       scalar2=cpar[:, cc, 31:32], op0=ALU.mult, op1=ALU.add))(yp, cc),
                     reads=[byp, b_cpar], writes=[b_acc[cc]])
                yield
                for j in range(1, CONV_K):
                    P.op("dve", (lambda yp, cc, j: lambda e: e.scalar_tensor_tensor(out=acc[:, cc, :], in0=yp[:, j:j + TOK], scalar=cpar[:, cc, j:j + 1],
                                                                                    in1=acc[:, cc, :], op0=ALU.mult, op1=ALU.add))(yp, cc, j),
                         reads=[byp, b_cpar, b_acc[cc]], writes=[b_acc[cc]])
                    yield
        def conv_post():
            for tg in range(4):
                sl = slice(tg * 512, (tg + 1) * 512)
                P.op("act", (lambda sl: lambda e: e.activation(out=accb[:], in_=acc[:, :, sl], func=AF.Copy))(sl), reads=b_acc, writes=[b_accb])
                P.op("act", (lambda sl: lambda e: e.activation(out=sqb[:], in_=acc[:, :, sl], func=AF.Square))(sl), reads=b_acc, writes=[b_sqb])
                for cc in range(4):
                    P.op("pe", (lambda cc: lambda e: e.matmul(ps[:, 0, :], lhsT=ones_b[:], rhs=accb[:, cc, :], start=(cc == 0), stop=(cc == 3)))(cc),
                         reads=[b_ones_b, b_accb], writes=[b_ps[0]], waw=(cc == 0))
                for cc in range(4):
                    P.op("pe", (lambda cc: lambda e: e.matmul(ps[:, 1, :], lhsT=ones_b[:], rhs=sqb[:, cc, :], start=(cc == 0), stop=(cc == 3)))(cc),
                         reads=[b_ones_b, b_sqb], writes=[b_ps[1]], waw=(cc == 0))
                P.op("act", lambda e: e.activation(out=mu[:], in_=ps[:, 0, :], func=AF.Copy, scale=1.0 / 512), reads=[b_ps[0]], writes=[b_mu])
                P.op("dve", lambda e: e.tensor_tensor(out=var[:], in0=mu[:], in1=mu[:], op=ALU.mult), reads=[b_mu], writes=[b_var])
                P.op("dve", lambda e: e.scalar_tensor_tensor(out=var[:], in0=ps[:, 1, :], scalar=1.0 / 512, in1=var[:], op0=ALU.mult, op1=ALU.subtract),
                     reads=[b_ps[1], b_var], writes=[b_var])
                P.op("act", lambda e: e.activation(out=var[:], in_=var[:], func=AF.Sqrt, bias=EPS, scale=1.0), reads=[b_var], writes=[b_var])
                P.op("dve", lambda e: e.reciprocal(out=var[:], in_=var[:]), reads=[b_var], writes=[b_var])
                for cc in range(4):
                    P.op("dve", (lambda cc, sl: lambda e: e.tensor_tensor(out=acc[:, cc, sl], in0=acc[:, cc, sl], in1=mu[:], op=ALU.subtract))(cc, sl),
                         reads=[b_acc[cc], b_mu], writes=[b_acc[cc]])
                    P.op("dve", (lambda cc, sl: lambda e: e.tensor_tensor(out=acc[:, cc, sl], in0=acc[:, cc, sl], in1=var[:], op=ALU.mult))(cc, sl),
                         reads=[b_acc[cc], b_var], writes=[b_acc[cc]])
                    P.op("act", (lambda cc, sl: lambda e: e.activation(out=sT[:, cc, :], in_=acc[:, cc, sl], func=AF.Silu,
                                                                       scale=cpar[:, cc, 32:33], bias=cpar[:, cc, 33:34]))(cc, sl),
                         reads=[b_acc[cc], b_cpar], writes=[b_sT], waw=(cc == 0))
                for co in range(4):
                    bank = 2 + co % 2
                    for cc in range(4):
                        P.op("pe", (lambda cc, co, bank: lambda e: e.matmul(ps[:, bank, :], lhsT=pw[:, cc, co * 128:(co + 1) * 128], rhs=sT[:, cc, :],
                                                                            start=(cc == 0), stop=(cc == 3)))(cc, co, bank),
                             reads=[b_pw, b_sT], writes=[b_ps[bank]], waw=(cc == 0))
                    si = co % 2
                    P.op("dve", (lambda co, bank, si, sl: lambda e: e.scalar_tensor_tensor(out=ycst[si][:], in0=ps[:, bank, :], scalar=cpar[:, co, 34:35],
                                                                                           in1=gconv[:, co, sl], op0=ALU.add, op1=ALU.mult))(co, bank, si, sl),
                         reads=[b_ps[bank], b_cpar, b_gconv], writes=[b_ycst[si]])
                    P.dma("sp", (lambda co, si, sl: lambda e: e.dma_start(out=o_yconv[co * 128:(co + 1) * 128, sl], in_=ycst[si][:]))(co, si, sl),
                          b_ycst[si], reads=[b_ycst[si]], writes=[b_yconv], waw=False)

        taps_gen = conv_taps()
    else:
        taps_gen, conv_post = iter(()), (lambda: None)

    if do_att:
        NPC = 8
        PCW = S // NPC
        qT = cx.sb("qT_s", [128, S], BF16)
        kT = cx.sb("kT_s", [128, S], BF16)
        vv = cx.sb("v_s", [128, NKB, 128], BF16)
        lf = cx.sb("lf_s", [128, NKB], F32)
        lfT = cx.sb("lfT_s", [128, 128], F32)
        tot = cx.sb("tot", [128, 2], F32)
        totbc = cx.sb("totbc", [128, 128], F32)
        ck = cx.sb("ck", [128, NKB], F32)
        cend = cx.sb("cend", [128, NKB], F32)
        bias4 = [cx.sb("bias4_%d" % i, [128, 4, NKB], F32) for i in range(2)]
        pT = [cx.sb("pT%d" % i, [128, 512], BF16) for i in range(3)]
        rinv = cx.sb("rinv", [128, 512], F32)
        yst = [cx.sb("yst%d" % i, [128, 512], BF16) for i in range(2)]
        b_q = [Buf("q%d" % i) for i in range(NPC)]
        b_k = [Buf("k%d" % i) for i in range(NPC)]
        b_v = [Buf("v%d" % i) for i in range(NPC)]
        b_lf, b_lfT, b_tot, b_totbc, b_ck, b_cend, b_rinv = [Buf(x) for x in "lf lfT tot totbc ck cend rinv".split()]
        b_bias4 = [Buf("b40"), Buf("b41")]
        b_pT = [Buf("pT%d" % i) for i in range(3)]
        b_yst = [Buf("ys0"), Buf("ys1")]
        u32, b_u32 = _consts_tri(cx, "u32", F32, ALU.is_ge, -1, 1)
        su32, b_su32 = _consts_tri(cx, "su32", F32, ALU.is_gt, -1, 1)
        ui32 = u32
        ones32, b_ones32 = _consts_tri(cx, "ones32", F32, ALU.is_ge, 0, 0)

        identf, b_identf = _consts_tri(cx, "identf", F32, ALU.is_equal, 1, -1)
        g4 = {nm: io["gath" + nm].rearrange("(r h d) t -> r h d t", h=8, d=128) for nm in ("q", "k", "v")}
        def ld_lfT(e):
            cv = dyn(e, 3, 0, NCORE - 1)
            return e.dma_start(out=lfT[:], in_=gathlf[cv])
        P.dma("sp", ld_lfT, b_lfT, reads=[b_glf, b_cid], writes=[b_lfT])
        P.op("pe", lambda e: e.transpose(out=ps[:, 7, 256:384], in_=lfT[:], identity=identf[:]), reads=[b_lfT, b_identf], writes=[b_ps[7]])
        P.op("dve", lambda e: e.tensor_copy(out=lf[:], in_=ps[:, 7, 256:384]), reads=[b_ps[7]], writes=[b_lf])
        vflat = vv.rearrange("p a b -> p (a b)")
        for hf in range(2):
            r0, r1 = hf * 4, hf * 4 + 4
            for sec, dst, bb in (("k", kT, b_k), ("q", qT, b_q), ("v", vflat, b_v)):
                def ld(e, sec=sec, dst=dst, r0=r0, r1=r1):
                    cv = dyn(e, 3, 0, NCORE - 1)
                    return e.dma_start(out=dst[:, r0 * PCW:r1 * PCW].rearrange("p (r t) -> p r t", r=4),
                                       in_=g4[sec][r0:r1, cv].rearrange("r d t -> d r t"))
                P.dma("sp", ld, bb[r0], reads=[io["b_gath" + sec], b_cid], writes=bb[r0:r1])

        P.op("dve", lambda e: e.tensor_reduce(out=tot[:, 0:1], in_=lfT[:], axis=AX.X, op=ALU.add), reads=[b_lfT], writes=[b_tot])
        P.op("dve", lambda e: e.tensor_scalar(out=totbc[:], in0=ones32[:], scalar1=tot[:, 0:1], scalar2=None, op0=ALU.mult),
             reads=[b_ones32, b_tot], writes=[b_totbc])
        P.op("pe", lambda e: e.matmul(ps[:, 7, 0:128], lhsT=u32[:], rhs=lf[:], start=True, stop=False), reads=[b_u32, b_lf], writes=[b_ps[7]])
        P.op("pe", lambda e: e.matmul(ps[:, 7, 0:128], lhsT=totbc[:], rhs=su32[:], start=False, stop=True), reads=[b_totbc, b_su32], writes=[b_ps[7]], waw=False)
        P.op("pe", lambda e: e.matmul(ps[:, 7, 128:256], lhsT=totbc[:], rhs=ui32[:], start=True, stop=True), reads=[b_totbc, b_u32], writes=[b_ps[7]], waw=False)
        P.op("dve", lambda e: e.tensor_copy(out=ck[:], in_=ps[:, 7, 0:128]), reads=[b_ps[7]], writes=[b_ck])
        P.op("dve", lambda e: e.tensor_copy(out=cend[:], in_=ps[:, 7, 128:256]), reads=[b_ps[7]], writes=[b_cend])

        tiles = [(qi, kb) for qi in range(nq) for kb in range(4 * qi + 4)]
        LOOK = 2

        def t_qk(t):
            qi, kb = tiles[t]
            c0 = 128 * max(0, kb - 4 * qi)
            sb_ = t % 3
            kpc, qpc = (kb * 128) // PCW, (qi * 512) // PCW
            P.op("pe", (lambda sb_, kb, qi, c0: lambda e: e.matmul(ps[:, sb_, c0:512], lhsT=kT[:, kb * 128:(kb + 1) * 128],
                                                                   rhs=qT[:, qi * 512 + c0:(qi + 1) * 512], start=True, stop=True))(sb_, kb, qi, c0),
                 reads=[b_k[kpc], b_q[qpc]], writes=[b_ps[sb_]])

        def t_exp(t):
            qi, kb = tiles[t]
            b4, bb4 = bias4[qi % 2], b_bias4[qi % 2]
            if kb == 0:
                for j2 in range(2):
                    g = 4 * qi + 2 * j2
                    P.op("dve", (lambda b4, j2, g: lambda e: e.tensor_scalar(out=b4[:, j2, 0:g + 2], in0=ck[:, 0:g + 2], scalar1=-1.0, scalar2=cend[:, g:g + 1],
                                                                             op0=ALU.mult, op1=ALU.add))(b4, j2, g),
                         reads=[b_ck, b_cend], writes=[bb4], waw=(j2 == 0))
            j0 = max(0, kb - 4 * qi)
            c0 = 128 * j0
            sb_ = t % 3
            pt, bpt = pT[sb_], b_pT[sb_]
            firstw = True
            for j2 in range(2):
                lo, hi = max(c0, 256 * j2), 256 * (j2 + 1)
                if lo >= hi:
                    continue
                P.op("act", (lambda pt, sb_, lo, hi, j2, b4, kb: lambda e: e.activation(out=pt[:, lo:hi], in_=ps[:, sb_, lo:hi],
                                                                                       func=AF.Exp, bias=b4[:, j2, kb:kb + 1], scale=SCALE))(pt, sb_, lo, hi, j2, b4, kb),
                     reads=[b_ps[sb_], bb4], writes=[bpt], waw=firstw)
                firstw = False
            if kb >= 4 * qi:
                P.op("pool", (lambda pt, c0: lambda e: e.tensor_tensor(out=pt[:, c0:c0 + 128], in0=pt[:, c0:c0 + 128], in1=tri_b[:], op=ALU.mult))(pt, c0),
                     reads=[bpt, b_tri_b], writes=[bpt])

        def t_pv(t):
            qi, kb = tiles[t]
            nkb = 4 * qi + 4
            c0 = 128 * max(0, kb - 4 * qi)
            sb_ = t % 3
            pt, bpt = pT[sb_], b_pT[sb_]
            ob, lb = 3 + qi % 2, 5 + qi % 2
            kpc = (kb * 128) // PCW
            first, last = (kb == 0), (kb == nkb - 1)
            P.op("pe", (lambda ob, kb, pt, c0, first, last: lambda e: e.matmul(ps[:, ob, c0:512], lhsT=vv[:, kb, :], rhs=pt[:, c0:512],
                                                                               start=first, stop=last, skip_group_check=True))(ob, kb, pt, c0, first, last),
                 reads=[b_v[kpc], bpt], writes=[b_ps[ob]], waw=first)
            P.op("pe", (lambda lb, pt, c0, first, last: lambda e: e.matmul(ps[:, lb, c0:512], lhsT=ones_b[:], rhs=pt[:, c0:512],
                                                                           start=first, stop=last, skip_group_check=True))(lb, pt, c0, first, last),
                 reads=[b_ones_b, bpt], writes=[b_ps[lb]], waw=first)
            if last:
                yi = qi % 2
                P.op("dve", (lambda lb: lambda e: e.reciprocal(out=rinv[:], in_=ps[:, lb, :]))(lb), reads=[b_ps[lb]], writes=[b_rinv])
                P.op("dve", (lambda ob, yi: lambda e: e.tensor_tensor(out=yst[yi][:], in0=ps[:, ob, :], in1=rinv[:], op=ALU.mult))(ob, yi),
                     reads=[b_ps[ob], b_rinv], writes=[b_yst[yi]])
                P.dma("sp", (lambda qi, yi: lambda e: e.dma_start(out=o_yatt[:, qi * 512:(qi + 1) * 512], in_=yst[yi][:]))(qi, yi),
                      b_yst[yi], reads=[b_yst[yi]], writes=[b_yatt], waw=False)
                for _ in range(4):
                    next(taps_gen, None)

        for t in range(-LOOK, len(tiles)):
            if t + LOOK < len(tiles):
                t_qk(t + LOOK)
            if t >= 0:
                t_exp(t)
                t_pv(t)
    for _ in taps_gen:
        pass
    conv_post()


def emit_C(cx, io, li):
    nc, P = cx.nc, cx.P
    cx.new_phase()
    yc_d = io["yconv"]
    ys_d = io["ysgu"]
    gathya = io["gathya"]
    gy = gathya.rearrange("hd (c t) -> c hd t", c=NCORE)
    ga_d = io["gatt"]
    h_d = io["h_in"][li]
    p_d = io["p"][li]
    wo_d = io["w_out"][li]
    wpg_d = io["w_pg"][li]
    wpp_d = io["w_pp"][li]
    gpost_d = io["gpost"][li]
    o_h = io["h_out"][li]
    hmid = io["hmid"]
    b_own, b_yconv, b_gya, b_cid = io["b_own"], io["b_yconv"], io["b_gathya"], io["b_cid"]
    b_hin, b_hout = io["b_h"][li], io["b_h"][li + 1]
    cid_sb = io["cid_sb"]

    def dyn(e, idx, lo, hi):
        key = (id(e), idx)
        if key not in cx.vals:
            reg = e.alloc_register("dyn%d" % idx)
            e.reg_load(reg, cid_sb[0:1, idx:idx + 1])
            cx.vals[key] = e.snap(reg, min_val=lo, max_val=hi)
        return cx.vals[key]

    ps = cx.ps
    b_ps = cx.b_ps
    yT = cx.sb("yT", [128, NKC, TOK], BF16)
    W = cx.sb("W", [128, NKC, D], BF16)
    wpp = cx.sb("wpp", [128, 2, D], BF16)
    gpost = cx.sb("gpost", [128, D], F32)
    gat = cx.sb("gat", [128, 1, TOK], BF16)
    hb = [cx.sb("hb%d" % i, [128, D], F32) for i in range(2)]
    tmp = cx.sb("tmp", [128, D], F32)
    hmb = cx.sb("hmb", [128, D], BF16)
    junk = hmb
    pb32 = cx.sb("pb32", [128, 256], F32)
    pbb = cx.sb("pbb", [128, 256], BF16)
    hmT = cx.sb("hmT", [128, NKC, 128], BF16)
    pTb = cx.sb("pTb", [128, 2, 128], BF16)
    sig = [cx.sb("sig%d" % i, [128, 512], F32) for i in range(2)]
    ost = cx.sb("ost", [128, D], F32)
    small = cx.sb("small", [128, 8], F32)
    b_yT = [Buf("yT%d" % i) for i in range(NKC)]
    b_W = [Buf("W%d" % i) for i in range(4)]
    b_wpp, b_gpost, b_tmp, b_junk_unused, b_hmb, b_pb32, b_pbb, b_hmT, b_pTb, b_small = [
        Buf(x) for x in "wpp gpost tmp junk hmb pb32 pbb hmT pTb small".split()]
    b_gat = Buf("gat")
    b_hb = [Buf("hb0"), Buf("hb1")]
    b_sig = [Buf("sig0"), Buf("sig1")]
    b_ost = Buf("ost")
    b_hmid = [Buf("hmid%d" % i) for i in range(NTB)]
    identb, b_identb = _consts_tri(cx, "identb", BF16, ALU.is_equal, 1, -1)

    P.dma("sp", lambda e: e.dma_start(out=gpost[:], in_=gpost_d), b_gpost, writes=[b_gpost])
    wov = wo_d.rearrange("(kc p) n -> p kc n", p=128)
    P.dma("pool", lambda e: e.dma_start(out=W[:], in_=wov), b_W[0], writes=b_W)
    for c in range(4):
        P.dma("sp", (lambda c: lambda e: e.dma_start(out=yT[:, c, :], in_=yc_d[c * 128:(c + 1) * 128, :]))(c), b_yT[c], reads=[b_yconv], writes=[b_yT[c]])
        P.dma("sp", (lambda c: lambda e: e.dma_start(out=yT[:, 4 + c, :], in_=ys_d[c * 128:(c + 1) * 128, :]))(c), b_yT[4 + c], reads=[b_own], writes=[b_yT[4 + c]])
    for hf in range(2):
        def ld_ya(e, hf=hf):
            cv = dyn(e, 3, 0, NCORE - 1)
            return e.dma_start(out=yT[:, 8 + hf * 4:12 + hf * 4, :], in_=gy[cv, hf * 512:(hf + 1) * 512, :].rearrange("(h d) t -> d h t", d=128))
        P.dma("sp", ld_ya, b_yT[8 + hf * 4], reads=[b_gya, b_cid], writes=b_yT[8 + hf * 4:12 + hf * 4])
    for c in range(8):
        P.dma("sp", (lambda c: lambda e: e.dma_start(out=gat[:, 0, :], in_=ga_d[c * 128:(c + 1) * 128, :]))(c), b_gat, reads=[b_own], writes=[b_gat])
        eng = "pool" if c % 2 == 0 else "dve"
        P.op(eng, (lambda c: lambda e: e.tensor_tensor(out=yT[:, 8 + c, :], in0=yT[:, 8 + c, :], in1=gat[:, 0, :], op=ALU.mult))(c),
             reads=[b_yT[8 + c], b_gat], writes=[b_yT[8 + c]])

    hv = h_d.rearrange("(tb p) d -> tb p d", p=128)
    hmv = hmid.rearrange("(tb p) d -> tb p d", p=128)
    ohv = o_h.rearrange("(tb p) d -> tb p d", p=128)
    pv = p_d.rearrange("(tb p) d -> tb p d", p=128)
    for tb in range(NTB):
        base = 4 * (tb % 2)
        for ct in range(4):
            for kc in range(NKC):
                P.op("pe", (lambda ct, kc, tb, base: lambda e: e.matmul(ps[:, base + ct, :], lhsT=yT[:, kc, tb * 128:(tb + 1) * 128],
                                                                        rhs=W[:, kc, ct * 512:(ct + 1) * 512], start=(kc == 0), stop=(kc == NKC - 1)))(ct, kc, tb, base),
                     reads=[b_yT[kc], b_W[ct]], writes=[b_ps[base + ct]], waw=(kc == 0))
        hbt, bhb = hb[tb % 2], b_hb[tb % 2]
        P.dma("sp", (lambda hbt, tb: lambda e: e.dma_start(out=hbt[:], in_=hv[tb]))(hbt, tb), bhb, reads=[b_hin], writes=[bhb])
        pfull = ps[:, base:base + 4, :]
        bpf = b_ps[base:base + 4]
        P.op("act", (lambda pfull: lambda e: e.activation(out=junk[:].rearrange("p (c n) -> p c n", c=4), in_=pfull, func=AF.Square, accum_out=small[:, 0:1]))(pfull),
             reads=bpf, writes=[b_hmb, b_small])
        P.op("act", lambda e: e.activation(out=small[:, 1:2], in_=small[:, 0:1], func=AF.Sqrt, bias=EPS, scale=1.0 / D), reads=[b_small], writes=[b_small])
        P.op("dve", lambda e: e.reciprocal(out=small[:, 2:3], in_=small[:, 1:2]), reads=[b_small], writes=[b_small])
        P.op("dve", (lambda pfull: lambda e: e.scalar_tensor_tensor(out=tmp[:].rearrange("p (c n) -> p c n", c=4), in0=pfull, scalar=small[:, 2:3],
                                                                    in1=gpost[:].rearrange("p (c n) -> p c n", c=4), op0=ALU.mult, op1=ALU.mult))(pfull),
             reads=bpf + [b_small, b_gpost], writes=[b_tmp])
        P.op("dve", (lambda hbt: lambda e: e.tensor_tensor(out=tmp[:], in0=tmp[:], in1=hbt[:], op=ALU.add))(hbt), reads=[b_tmp, bhb], writes=[b_tmp])
        P.dma("sp", (lambda tb: lambda e: e.dma_start(out=hmv[tb], in_=tmp[:]))(tb), b_tmp, reads=[b_tmp], writes=[b_hmid[tb]])

    wgv = wpg_d.rearrange("(kc p) n -> p kc n", p=128)
    P.dma("pool", lambda e: e.dma_start(out=W[:], in_=wgv), b_W[0], writes=b_W)
    P.dma("pool", lambda e: e.dma_start(out=wpp[:], in_=wpp_d.rearrange("(kc p) n -> p kc n", p=128)), b_wpp, writes=[b_wpp])
    rot = 0
    for tb in range(NTB):
        hbt, bhb = hb[tb % 2], b_hb[tb % 2]
        P.dma("sp", (lambda hbt, tb: lambda e: e.dma_start(out=hbt[:], in_=hmv[tb]))(hbt, tb), bhb, reads=[b_hmid[tb]], writes=[bhb])
        P.dma("sp", (lambda tb: lambda e: e.dma_start(out=pb32[:], in_=pv[tb]))(tb), b_pb32, writes=[b_pb32])
        P.op("act", (lambda hbt: lambda e: e.activation(out=hmb[:], in_=hbt[:], func=AF.Copy))(hbt), reads=[bhb], writes=[b_hmb])
        P.op("act", lambda e: e.activation(out=pbb[:], in_=pb32[:], func=AF.Copy), reads=[b_pb32], writes=[b_pbb])
        for half in range(2):
            pbk = ps[:, half, :].bitcast(BF16)
            for j in range(8):
                kc = half * 8 + j
                P.op("pe", (lambda pbk, j, kc: lambda e: e.transpose(out=pbk[:, j * 128:(j + 1) * 128], in_=hmb[:, kc * 128:(kc + 1) * 128], identity=identb[:]))(pbk, j, kc),
                     reads=[b_hmb, b_identb], writes=[b_ps[half]], waw=(j == 0))
            P.op("dve", (lambda pbk, half: lambda e: e.tensor_copy(out=hmT[:, half * 8:(half + 1) * 8, :], in_=pbk.rearrange("p (j t) -> p j t", j=8)))(pbk, half),
                 reads=[b_ps[half]], writes=[b_hmT], waw=(half == 0))
        pbk = ps[:, 0, :].bitcast(BF16)
        for j in range(2):
            P.op("pe", (lambda pbk, j: lambda e: e.transpose(out=pbk[:, j * 128:(j + 1) * 128], in_=pbb[:, j * 128:(j + 1) * 128], identity=identb[:]))(pbk, j),
                 reads=[b_pbb, b_identb], writes=[b_ps[0]], waw=(j == 0))
        P.op("dve", (lambda pbk: lambda e: e.tensor_copy(out=pTb[:], in_=pbk[:, 0:256].rearrange("p (j t) -> p j t", j=2)))(pbk),
             reads=[b_ps[0]], writes=[b_pTb])
        for ct in range(4):
            gb_, pb_ = 2 + 2 * (rot % 3), 3 + 2 * (rot % 3)
            si = rot % 2
            rot += 1
            for kc in range(NKC):
                P.op("pe", (lambda gb_, kc, ct: lambda e: e.matmul(ps[:, gb_, :], lhsT=hmT[:, kc, :], rhs=W[:, kc, ct * 512:(ct + 1) * 512],
                                                                   start=(kc == 0), stop=(kc == NKC - 1)))(gb_, kc, ct),
                     reads=[b_hmT, b_W[ct]], writes=[b_ps[gb_]], waw=(kc == 0))
            for kc in range(2):
                P.op("pe", (lambda pb_, kc, ct: lambda e: e.matmul(ps[:, pb_, :], lhsT=pTb[:, kc, :], rhs=wpp[:, kc, ct * 512:(ct + 1) * 512],
                                                                   start=(kc == 0), stop=(kc == 1)))(pb_, kc, ct),
                     reads=[b_pTb, b_wpp], writes=[b_ps[pb_]], waw=(kc == 0))
            P.op("act", (lambda si, gb_: lambda e: e.activation(out=sig[si][:], in_=ps[:, gb_, :], func=AF.Sigmoid))(si, gb_),
                 reads=[b_ps[gb_]], writes=[b_sig[si]])
            P.op("dve", (lambda si, pb_: lambda e: e.tensor_tensor(out=sig[si][:], in0=sig[si][:], in1=ps[:, pb_, :], op=ALU.mult))(si, pb_),
                 reads=[b_sig[si], b_ps[pb_]], writes=[b_sig[si]])
            P.op("dve", (lambda si, hbt, ct: lambda e: e.tensor_tensor(out=ost[:, ct * 512:(ct + 1) * 512], in0=sig[si][:], in1=hbt[:, ct * 512:(ct + 1) * 512], op=ALU.add))(si, hbt, ct),
                 reads=[b_sig[si], bhb], writes=[b_ost], waw=(ct == 0))
        P.dma("sp", (lambda tb: lambda e: e.dma_start(out=ohv[tb], in_=ost[:]))(tb),
              b_ost, reads=[b_ost], writes=[b_hout], waw=False)


I32 = mybir.dt.int32
DBG = {}


def build_fused():
    cx = Ctx("F")
    nc, P = cx.nc, cx.P
    io = {}
    x = cx.din("x", [TOK, D], F32)
    out = cx.dout("out", [TOK, D], F32)
    io["p"] = cx.din("p", [2, TOK, 256], F32)
    io["w_in"] = cx.din("w_in", [2, D, NIN], F32)
    io["gpre"] = cx.din("gpre", [2, 128, D], F32)
    io["bfb"] = cx.din("bfb", [2, 128, 8], F32)
    io["sgg"] = cx.din("sgg", [2, 128, 512], F32)
    io["sgb"] = cx.din("sgb", [2, 128, 512], F32)
    io["sguw"] = cx.din("sguw", [2, 128, 4, 128], F32)
    io["sgub"] = cx.din("sgub", [2, 128, 512], F32)
    io["cpar"] = cx.din("cpar", [2, 128, 4, 36], F32)
    io["pw"] = cx.din("pw", [2, 512, 512], F32)
    io["w_out"] = cx.din("w_out", [2, D, D], F32)
    io["w_pg"] = cx.din("w_pg", [2, D, D], F32)
    io["w_pp"] = cx.din("w_pp", [2, 256, D], F32)
    io["gpost"] = cx.din("gpost", [2, 128, D], F32)
    cid = cx.din("cid", [1, 8], I32)
    cmask = cx.din("cmask", [128, 1], F32)
    h1 = cx.dscr("h1", [TOK, D], F32)
    io["hmid"] = cx.dscr("hmid", [TOK, D], F32)
    for nm in ("q", "k", "v"):
        io["blob" + nm] = cx.dscr("blob" + nm, [1024, TOK], BF16)
        io["gath" + nm] = cx.dscr("gath" + nm, [NCORE * 1024, TOK], BF16)
    io["bloblf"] = cx.dscr("bloblf", [16, 1024], F32)
    io["gathlf"] = cx.dscr("gathlf", [NCORE * 16, 1024], F32)
    io["blobhalo"] = cx.dscr("blobhalo", [512, 32], F32)
    io["gathhalo"] = cx.dscr("gathhalo", [NCORE * 512, 32], F32)
    io["glu"] = cx.dscr("glu", [512, TOK], F32)
    io["gatt"] = cx.dscr("gatt", [1024, TOK], BF16)
    io["gconv"] = cx.dscr("gconv", [512, TOK], BF16)
    io["ysgu"] = cx.dscr("ysgu", [512, TOK], BF16)
    io["yconv"] = cx.dscr("yconv", [512, TOK], BF16)
    io["yatt"] = cx.dscr("yatt", [128, S], BF16)
    io["gathya"] = cx.dscr("gathya", [NCORE * 128, S], BF16)
    io["h_in"] = [x, h1]
    io["h_out"] = [h1, out]
    io["b_h"] = [Buf("hx"), Buf("h1"), Buf("hout")]
    for n in ("blobq", "gathq", "blobk", "gathk", "blobv", "gathv", "bloblf", "gathlf", "blobhalo", "gathhalo", "own", "yconv", "yatt", "gathya", "cid"):
        io["b_" + n] = Buf(n)

    _consts_tri(cx, "identb", BF16, ALU.is_equal, 1, -1)
    _consts_tri(cx, "identf", F32, ALU.is_equal, 1, -1)
    _consts_tri(cx, "ones_b", BF16, ALU.is_ge, 0, 0)
    _consts_tri(cx, "tri_b", BF16, ALU.is_ge, -1, 1)
    _consts_tri(cx, "u32", F32, ALU.is_ge, -1, 1)
    _consts_tri(cx, "su32", F32, ALU.is_gt, -1, 1)
    _consts_tri(cx, "ones32", F32, ALU.is_ge, 0, 0)
    cid_sb = cx.sb("cid_sb", [1, 8], I32)
    cmask_sb = cx.sb("cmask_sb", [128, 1], F32)
    io["cid_sb"], io["cmask_sb"] = cid_sb, cmask_sb
    P.dma("sp", lambda e: e.dma_start(out=cid_sb[:], in_=cid), io["b_cid"], writes=[io["b_cid"]], waw=False)
    P.dma("sp", lambda e: e.dma_start(out=cmask_sb[:], in_=cmask), io["b_cid"], writes=[io["b_cid"]], waw=False)
    cx.persist_done()

    def allgather(src, dst, b_src, b_dst):
        P.dma("pool", lambda e: e.collective_compute("AllGather", ALU.bypass, replica_groups=[list(range(NCORE))],
                                                     ins=[src.opt()], outs=[dst.opt()], dma_qos="P2"),
              b_dst, reads=[b_src], writes=[b_dst], inc=1)

    io["ag"] = lambda nm: allgather(io["blob" + nm], io["gath" + nm], io["b_blob" + nm], io["b_gath" + nm])
    nl = DBG.get("layers", 2)
    if nl == 1:
        io["h_out"] = [out, out]
    for li in range(nl):
        emit_A(cx, io, li)
        allgather(io["bloblf"], io["gathlf"], io["b_bloblf"], io["b_gathlf"])
        allgather(io["blobhalo"], io["gathhalo"], io["b_blobhalo"], io["b_gathhalo"])
        emit_B(cx, io, li, nq=DBG.get("nq", S // 512))
        allgather(io["yatt"], io["gathya"], io["b_yatt"], io["b_gathya"])
        emit_C(cx, io, li)
    return cx.finish()


_NC = {}


def _bc(v, n=128):
    v = np.asarray(v, np.float32)
    return np.ascontiguousarray(np.broadcast_to(v.reshape(1, -1), (n, v.size)))


def kernel(**inputs):
    W = {k: np.asarray(v) for k, v in inputs.items()}
    perm = w_in_perm()
    L = range(2)
    cpar = np.zeros((2, 512, 36), np.float32)
    for li in L:
        cpar[li, :, 0:31] = W["conv_dw"][li].T
        cpar[li, :, 31] = W["conv_dw_b"][li]
        cpar[li, :, 32] = W["conv_ln_g"][li]
        cpar[li, :, 33] = W["conv_ln_b"][li]
        cpar[li, :, 34] = W["conv_pw_b"][li]
    cpar = np.ascontiguousarray(cpar.reshape(2, 4, 128, 36).transpose(0, 2, 1, 3))
    common = {
        "w_in": np.ascontiguousarray(W["w_in"][:, :, perm]),
        "gpre": np.stack([_bc(W["norm_pre"][li]) for li in L]),
        "bfb": np.stack([_bc(W["b_f"][li]) for li in L]),
        "sgg": np.stack([_bc(W["sgu_ln_g"][li]) for li in L]),
        "sgb": np.stack([_bc(W["sgu_ln_b"][li]) for li in L]),
        "sguw": np.ascontiguousarray(np.transpose(W["sgu_w"], (0, 2, 1, 3))),
        "sgub": np.stack([_bc(W["sgu_b"][li].reshape(-1)) for li in L]),
        "cpar": cpar,
        "pw": np.ascontiguousarray(W["conv_pw"]),
        "w_out": np.ascontiguousarray(W["w_out"]),
        "w_pg": np.ascontiguousarray(W["w_pg"]),
        "w_pp": np.ascontiguousarray(W["w_pp"]),
        "gpost": np.stack([_bc(W["norm_post"][li]) for li in L]),
    }
    x = W["x"][0]
    in_maps = []
    for c in range(NCORE):
        cid = np.array([[c * 128, c * TOK, max(c - 1, 0), c, 0, 0, 0, 0]], np.int32)
        in_maps.append(dict(common,
                            x=np.ascontiguousarray(x[c * TOK:(c + 1) * TOK]),
                            p=np.ascontiguousarray(W["p"][:, 0, c * TOK:(c + 1) * TOK]),
                            cid=cid,
                            cmask=np.full((128, 1), 0.0 if c == 0 else 1.0, np.float32)))
    if "F" not in _NC:
        _NC["F"] = build_fused()
    res = run_bass_kernel_spmd(_NC["F"], in_maps, core_ids=list(range(NCORE)))
    out = np.concatenate([r["out"] for r in res.results], axis=0)
    return out.reshape(1, S, D).astype(np.float32)
```

```python
import numpy as np
from contextlib import ExitStack
import ml_dtypes
import concourse.bass as bass
import concourse.mybir as mybir
from concourse.bass_utils import run_bass_kernel_spmd

F32 = mybir.dt.float32
BF16 = mybir.dt.bfloat16
AF = mybir.ActivationFunctionType
ALU = mybir.AluOpType
AX = mybir.AxisListType

NCORE = 8
S = 16384
D = 2048
TOK = S // NCORE
NTB = TOK // 128
NKC = D // 128
NIN = 7176
EPS = 1e-6
SCALE = 128 ** -0.5
CONV_K = 31
HALO = CONV_K - 1
NKB = S // 128

SAME_ENG_SYNC = True
SAME_ENG_DIST = 3


class Buf:
    _n = 0

    def __init__(self, name=""):
        Buf._n += 1
        self.id = Buf._n
        self.name = name
        self.w = {}
        self.r = {}


class Prog:
    ENG = ("pe", "act", "dve", "pool", "sp")

    def __init__(self, nc):
        self.nc = nc
        self.q = {e: [] for e in self.ENG}
        self.cnt = {e: 0 for e in self.ENG}
        self.seen = {e: {} for e in self.ENG}

    def _deps(self, e, reads, writes, waw):
        need = {}
        for b in reads:
            for k, v in b.w.items():
                if need.get(k, 0) < v:
                    need[k] = v
        for b in writes:
            its = list(b.r.items())
            if waw:
                its += list(b.w.items())
            for k, v in its:
                if need.get(k, 0) < v:
                    need[k] = v
        waits = []
        seen = self.seen[e]
        for k, v in need.items():
            if k == e and (e == "pe" or not SAME_ENG_SYNC):
                continue
            if k == e and self.cnt[e] - v >= SAME_ENG_DIST:
                continue
            if seen.get(k, 0) >= v:
                continue
            seen[k] = v
            waits.append((k, v))
        return waits

    def _mark(self, k, v, reads, writes, waw):
        for b in reads:
            b.r[k] = v
        for b in writes:
            if waw:
                b.w = {k: v}
                b.r = {}
            else:
                b.w[k] = v

    def op(self, e, fn, reads=(), writes=(), waw=True):
        waits = self._deps(e, reads, writes, waw)
        self.cnt[e] += 1
        self.q[e].append((waits, fn, e, 1))
        self._mark(e, self.cnt[e], reads, writes, waw)

    def dma(self, qe, fn, owner, reads=(), writes=(), waw=True, inc=16):
        waits = self._deps(qe, reads, writes, waw)
        k = ("d", owner.name)
        self.cnt[k] = self.cnt.get(k, 0) + inc
        self.q[qe].append((waits, fn, k, inc))
        self._mark(k, self.cnt[k], reads, writes, waw)

    def barrier(self):
        for e in self.ENG:
            waits = []
            for k, v in self.cnt.items():
                if v == 0 or (k == e and e == "pe"):
                    continue
                if self.seen[e].get(k, 0) >= v:
                    continue
                self.seen[e][k] = v
                waits.append((k, v))
            if waits:
                self.q[e].append((waits, None, None, 0))

    def emit(self, stack):
        nc = self.nc
        sems = {}
        print("[kernel] semaphores:", len(self.cnt), "ops:", {e: len(v) for e, v in self.q.items()})
        for k in self.cnt:
            nm = "s_" + (k if isinstance(k, str) else "d_" + k[1])
            sems[k] = stack.enter_context(nc.semaphore(nm))
        block = stack.enter_context(nc.Block())
        emap = {"pe": block.tensor, "act": block.scalar, "dve": block.vector,
                "pool": block.gpsimd, "sp": block.sync}

        def mk(e):
            items = self.q[e]

            def body(eng):
                for waits, fn, sk, inc in items:
                    for k, v in waits:
                        eng.wait_ge(sems[k], v)
                    if fn is not None:
                        fn(eng).then_inc(sems[sk], inc)
            return body

        for e in self.ENG:
            if self.q[e]:
                emap[e](mk(e))


ARENA_BYTES = 200 * 1024


class Ctx:
    def __init__(self, name):
        self.nc = bass.Bass("TRN2", target_bir_lowering=False, num_devices=NCORE)
        self.P = Prog(self.nc)
        self.st = ExitStack()
        self.name = name
        self.arena = self.st.enter_context(self.nc.sbuf_tensor("arena", [128, ARENA_BYTES // 2], BF16))
        self.off = 0
        self.base = 0
        self.ps = self.st.enter_context(self.nc.psum_tensor("ps", [128, 8, 512], F32))
        self.b_ps = [Buf("ps%d" % i) for i in range(8)]
        self.vals = {}

    def persist_done(self):
        self.base = self.off

    def new_phase(self):
        self.P.barrier()
        self.off = self.base

    def din(self, name, shape, dt):
        return self.nc.dram_tensor(name, list(shape), dt, kind="ExternalInput").ap()

    def dout(self, name, shape, dt):
        return self.nc.dram_tensor(name, list(shape), dt, kind="ExternalOutput").ap()

    def dscr(self, name, shape, dt):
        return self.nc.dram_tensor(name, list(shape), dt).ap()

    def sb(self, name, shape, dt):
        esz = 2 if dt == BF16 else 4
        n = 1
        for d_ in shape[1:]:
            n *= d_
        nb = (n * esz + 63) // 64 * 64
        assert self.off + nb <= ARENA_BYTES, (name, self.off, nb)
        v = self.arena[:, self.off // 2:(self.off + nb) // 2]
        self.off += nb
        if esz == 4:
            v = v.bitcast(dt)
        v = v[:, 0:n]
        if len(shape) == 3:
            v = v.rearrange("p (a b) -> p a b", a=shape[1])
        if shape[0] < 128:
            v = v[0:shape[0]]
        return v

    def finish(self):
        self.P.barrier()
        self.P.emit(self.st)
        self.st.close()
        return self.nc


def _consts_tri(cx, name, dt, op, cm, step):
    P = cx.P
    if not hasattr(cx, "consts"):
        cx.consts = {}
    if name in cx.consts:
        return cx.consts[name]
    assert cx.base == 0, "constants must be created before the first phase"
    t = cx.sb(name, [128, 128], dt)
    b = Buf(name)
    P.op("pool", lambda e: e.memset(t[:], 1.0), writes=[b])
    P.op("pool", lambda e: e.affine_select(out=t[:], in_=t[:], pattern=[[step, 128]], compare_op=op,
                                           fill=0.0, base=0, channel_multiplier=cm),
         reads=[b], writes=[b])
    cx.consts[name] = (t, b)
    return t, b


A_TILES = [("q", 0), ("q", 1), ("k", 0), ("k", 1), ("v", 0), ("v", 1), ("zatt", 0), ("zatt", 1), ("zconv", 0),
           ("glu", 0), ("glu", 1), ("usg", 0), ("usg", 1), ("vsgu", 0)]


def w_in_perm():
    o = {}
    names = ["q", "k", "v", "zatt", "f", "ga", "gb", "zconv", "u", "vsgu", "zsgu"]
    sizes = [1024, 1024, 1024, 1024, 8, 512, 512, 512, 512, 512, 512]
    c = 0
    for n, s in zip(names, sizes):
        o[n] = c
        c += s
    r = np.arange
    idx = [r(o["q"], o["q"] + 1024), r(o["k"], o["k"] + 1024), r(o["v"], o["v"] + 1024), r(o["zatt"], o["zatt"] + 1024),
           r(o["zconv"], o["zconv"] + 512)]
    for i in range(4):
        idx += [r(o["ga"] + 128 * i, o["ga"] + 128 * i + 128), r(o["gb"] + 128 * i, o["gb"] + 128 * i + 128)]
    for i in range(4):
        idx += [r(o["u"] + 128 * i, o["u"] + 128 * i + 128), r(o["zsgu"] + 128 * i, o["zsgu"] + 128 * i + 128)]
    idx += [r(o["vsgu"], o["vsgu"] + 512), r(o["f"], o["f"] + 8)]
    return np.concatenate(idx)


def emit_A(cx, io, li):
    nc, P = cx.nc, cx.P
    cx.new_phase()
    h = io["h_in"][li]
    w_in = io["w_in"][li]
    gpre = io["gpre"][li]
    bfb = io["bfb"][li]
    sgg = io["sgg"][li]
    sgb = io["sgb"][li]
    sguw = io["sguw"][li]
    sgub = io["sgub"][li]
    o_q = io["blobq"]
    o_k = io["blobk"]
    o_v = io["blobv"].rearrange("(h p) (kb d) -> p h kb d", p=128, d=128)
    o_lfT = io["bloblf"].rearrange("kb (h j) -> h kb j", j=128)
    o_halo = io["blobhalo"]
    o_glu = io["glu"]
    o_gatt = io["gatt"]
    o_gconv = io["gconv"]
    o_ysgu = io["ysgu"]
    b_blobq, b_blobk, b_blobv = io["b_blobq"], io["b_blobk"], io["b_blobv"]
    b_blob32, b_bloblf = io["b_blobhalo"], io["b_bloblf"]
    pending_ag = []
    b_own = io["b_own"]
    b_hin = io["b_h"][li]

    xnT = cx.sb("xnT", [128, NKC, TOK], BF16)
    wt = [cx.sb("wt%d" % i, [128, NKC, 1032], BF16) for i in range(2)]
    ug = cx.sb("ug", [128, 4, TOK], BF16)
    small = cx.sb("small", [128, 32], F32)
    bfb_s = cx.sb("bfb_s", [128, 8], F32)
    sgg_s = cx.sb("sgg_s", [128, 512], F32)
    sgb_s = cx.sb("sgb_s", [128, 512], F32)
    sguw_s = cx.sb("sguw_s", [128, 4, 128], F32)
    sgub_s = cx.sb("sgub_s", [128, 512], F32)
    wsT = cx.sb("wsT", [128, 4, 128], BF16)
    vln = cx.sb("vln", [128, 512], BF16)
    lfst = [cx.sb("lfst%d" % i, [128, 8], F32) for i in range(2)]
    lfT_sb = [cx.sb("lfT_sb%d" % i, [8, 128], F32) for i in range(2)]
    b_lfT_sb = [Buf("lfT_sb0"), Buf("lfT_sb1")]
    ps = cx.ps

    union0 = cx.off
    hblk = [cx.sb("hblk%d" % i, [128, D], F32) for i in range(2)]
    xn = cx.sb("xn", [128, D], BF16)
    gbc = cx.sb("gbc", [128, D], F32)
    junk = cx.sb("junk", [128, D], BF16)
    b_xnT = [Buf("xnT%d" % i) for i in range(NTB)]
    b_wt = [Buf("wt0"), Buf("wt1")]
    b_ug = Buf("ug")
    b_hblk = [Buf("hb0"), Buf("hb1")]
    b_xn, b_gbc, b_junk, b_small = Buf("xn"), Buf("gbc"), Buf("junk"), Buf("small")
    b_par = Buf("par")
    b_wsT = Buf("wsT")
    b_stg = [Buf("stg%d" % i) for i in range(3)]
    b_stf = [Buf("stf%d" % i) for i in range(3)]
    b_vln = Buf("vln")
    b_lfst = [Buf("lf0"), Buf("lf1")]
    b_ps = cx.b_ps
    b_sw = Buf("sguw")

    identb, b_identb = _consts_tri(cx, "identb", BF16, ALU.is_equal, 1, -1)
    identf, b_identf = _consts_tri(cx, "identf", F32, ALU.is_equal, 1, -1)

    P.dma("sp", lambda e: e.dma_start(out=gbc[:], in_=gpre), b_gbc, writes=[b_gbc])
    for t_sb, t_dr in ((bfb_s, bfb), (sgg_s, sgg), (sgb_s, sgb), (sgub_s, sgub)):
        P.dma("sp", (lambda a, b: lambda e: e.dma_start(out=a[:], in_=b))(t_sb, t_dr), b_par, writes=[b_par], waw=False)
    P.dma("sp", lambda e: e.dma_start(out=sguw_s[:], in_=sguw), b_sw, writes=[b_sw])

    for hh in range(4):
        P.op("pool", (lambda hh: lambda e: e.affine_select(
            out=sguw_s[:, hh, :], in_=sguw_s[:, hh, :], pattern=[[-1, 128]], compare_op=ALU.is_ge,
            fill=0.0, base=0, channel_multiplier=1))(hh), reads=[b_sw], writes=[b_sw])
    for hh in range(4):
        P.op("pe", (lambda hh: lambda e: e.transpose(out=ps[:, 7, hh * 128:(hh + 1) * 128], in_=sguw_s[:, hh, :],
                                                     identity=identf[:]))(hh),
             reads=[b_sw, b_identf], writes=[b_ps[7]], waw=False)
    P.op("dve", lambda e: e.tensor_copy(out=wsT[:].rearrange("p h t -> p (h t)"), in_=ps[:, 7, :]),
         reads=[b_ps[7]], writes=[b_wsT])

    hv = h.rearrange("(tb p) d -> tb p d", p=128)
    for tb in range(NTB):
        hb, bh = hblk[tb % 2], b_hblk[tb % 2]
        P.dma("sp", (lambda hb, tb: lambda e: e.dma_start(out=hb[:], in_=hv[tb]))(hb, tb), bh, reads=[b_hin], writes=[bh])
        ss = small[:, 0:1]
        P.op("act", (lambda hb: lambda e: e.activation(out=junk[:], in_=hb[:], func=AF.Square, accum_out=small[:, 0:1]))(hb),
             reads=[bh], writes=[b_junk, b_small])
        P.op("act", lambda e: e.activation(out=small[:, 1:2], in_=small[:, 0:1], func=AF.Sqrt, bias=EPS, scale=1.0 / D),
             reads=[b_small], writes=[b_small])
        P.op("dve", lambda e: e.reciprocal(out=small[:, 2:3], in_=small[:, 1:2]), reads=[b_small], writes=[b_small])
        P.op("dve", (lambda hb: lambda e: e.scalar_tensor_tensor(out=xn[:], in0=hb[:], scalar=small[:, 2:3], in1=gbc[:],
                                                                  op0=ALU.mult, op1=ALU.mult))(hb),
             reads=[bh, b_small, b_gbc], writes=[b_xn])
        for half in range(2):
            bank = half
            pb = ps[:, bank, :].bitcast(BF16)
            for j in range(8):
                kc = half * 8 + j
                P.op("pe", (lambda pb, j, kc: lambda e: e.transpose(out=pb[:, j * 128:(j + 1) * 128],
                                                                    in_=xn[:, kc * 128:(kc + 1) * 128],
                                                                    identity=identb[:]))(pb, j, kc),
                     reads=[b_xn, b_identb], writes=[b_ps[bank]], waw=(j == 0))
            eng = "act" if half == 0 else "dve"
            dst = xnT[:, half * 8:(half + 1) * 8, tb * 128:(tb + 1) * 128]
            src = pb.rearrange("p (j t) -> p j t", j=8)
            if eng == "act":
                P.op("act", (lambda dst, src: lambda e: e.activation(out=dst, in_=src, func=AF.Copy))(dst, src),
                     reads=[b_ps[bank]], writes=[b_xnT[tb]], waw=False)
            else:
                P.op("dve", (lambda dst, src: lambda e: e.tensor_copy(out=dst, in_=src))(dst, src),
                     reads=[b_ps[bank]], writes=[b_xnT[tb]], waw=False)

    P.barrier()
    cx.off = union0
    stg = [cx.sb("stg%d" % i, [128, TOK], BF16) for i in range(3)]
    vstage = cx.sb("vstage", [128, 4, NTB, 128], BF16) if False else cx.sb("vstage", [128, 4 * NTB, 128], BF16)
    b_vstage = Buf("vstage")
    stf = [cx.sb("stf%d" % i, [128, 512], F32) for i in range(3)]
    wv = w_in.rearrange("(kc p) n -> p kc n", p=128)
    rot = {"bank": 0, "stg": 0, "stf": 0, "lf": 0}

    def nbank():
        b = 2 + rot["bank"] % 5
        rot["bank"] += 1
        return b

    def nstg():
        i = rot["stg"] % 3
        rot["stg"] += 1
        return i

    def nstf():
        i = rot["stf"] % 3
        rot["stf"] += 1
        return i

    def mm_feat(wti, cc, tg, bank):
        w = wt[wti][:, :, cbase[0]:cbase[0] + 520]
        for kc in range(NKC):
            P.op("pe", (lambda w, kc, cc, tg, bank: lambda e: e.matmul(
                ps[:, bank, :], lhsT=w[:, kc, cc * 128:(cc + 1) * 128], rhs=xnT[:, kc, tg * 512:(tg + 1) * 512],
                start=(kc == 0), stop=(kc == NKC - 1)))(w, kc, cc, tg, bank),
                reads=[b_wt[wti]] + b_xnT[tg * 4:(tg + 1) * 4], writes=[b_ps[bank]], waw=(kc == 0))

    def mm_tok(wti, tb, bank, c0, n):
        w = wt[wti][:, :, cbase[0]:cbase[0] + 520]
        for kc in range(NKC):
            P.op("pe", (lambda w, kc, tb, bank, c0, n: lambda e: e.matmul(
                ps[:, bank, 0:n], lhsT=xnT[:, kc, tb * 128:(tb + 1) * 128], rhs=w[:, kc, c0:c0 + n],
                start=(kc == 0), stop=(kc == NKC - 1)))(w, kc, tb, bank, c0, n),
                reads=[b_wt[wti], b_xnT[tb]], writes=[b_ps[bank]], waw=(kc == 0))

    col = 0
    cbase = [0]
    for ti, (kind, sub) in enumerate(A_TILES):
        if ti > 0 and A_TILES[ti - 1][0] in ("q", "k", "v") and A_TILES[ti - 1][1] == 1:
            pending_ag.append(A_TILES[ti - 1][0])
        wti = (ti // 2) % 2
        cbase[0] = (ti % 2) * 512
        if ti % 2 == 0:
            ncols = min(1024, NIN - col) if ti + 2 < len(A_TILES) else NIN - col
            P.dma("pool", (lambda wti, col, ncols: lambda e: e.dma_start(out=wt[wti][:, :, 0:ncols], in_=wv[:, :, col:col + ncols]))(wti, col, ncols),
                  b_wt[wti], writes=[b_wt[wti]])
            col += ncols
            while pending_ag:
                io["ag"](pending_ag.pop(0))
        if kind in ("q", "k", "zatt", "zconv"):
            dst = {"q": o_q, "k": o_k, "zatt": o_gatt, "zconv": o_gconv}[kind]
            b_dst = {"q": b_blobq, "k": b_blobk}.get(kind, b_own)
            func = AF.Copy if kind in ("q", "k") else AF.Silu
            for cc in range(4):
                si = nstg()
                for tg in range(4):
                    bank = nbank()
                    mm_feat(wti, cc, tg, bank)
                    P.op("act", (lambda si, bank, func, tg: lambda e: e.activation(out=stg[si][:, tg * 512:(tg + 1) * 512], in_=ps[:, bank, :], func=func))(si, bank, func, tg),
                         reads=[b_ps[bank]], writes=[b_stg[si]], waw=(tg == 0))
                r0 = (sub * 4 + cc) * 128
                P.dma("sp", (lambda dst, r0, si: lambda e: e.dma_start(out=dst[r0:r0 + 128, :], in_=stg[si][:]))(dst, r0, si),
                      b_stg[si], reads=[b_stg[si]], writes=[b_dst], waw=False)
        elif kind == "glu":
            for pr in range(2):
                ch = sub * 2 + pr
                for tg in range(4):
                    ba, bb = nbank(), nbank()
                    mm_feat(wti, 2 * pr, tg, ba)
                    mm_feat(wti, 2 * pr + 1, tg, bb)
                    s1, s2 = nstf(), nstf()
                    P.op("act", (lambda s1, bb: lambda e: e.activation(out=stf[s1][:], in_=ps[:, bb, :], func=AF.Sigmoid))(s1, bb),
                         reads=[b_ps[bb]], writes=[b_stf[s1]])
                    P.op("dve", (lambda s1, s2, ba: lambda e: e.tensor_tensor(out=stf[s2][:], in0=ps[:, ba, :], in1=stf[s1][:], op=ALU.mult))(s1, s2, ba),
                         reads=[b_ps[ba], b_stf[s1]], writes=[b_stf[s2]])
                    P.dma("sp", (lambda ch, tg, s2: lambda e: e.dma_start(out=o_glu[ch * 128:(ch + 1) * 128, tg * 512:(tg + 1) * 512], in_=stf[s2][:]))(ch, tg, s2),
                          b_stf[s2], reads=[b_stf[s2]], writes=[b_own], waw=False)
                    if tg == 3:
                        P.dma("sp", (lambda ch, s2: lambda e: e.dma_start(out=o_halo[ch * 128:(ch + 1) * 128, :], in_=stf[s2][:, 480:512]))(ch, s2),
                              b_stf[s2], reads=[b_stf[s2]], writes=[b_blob32], waw=False)
        elif kind == "usg":
            for pr in range(2):
                ch = sub * 2 + pr
                for tg in range(4):
                    ba, bb = nbank(), nbank()
                    mm_feat(wti, 2 * pr, tg, ba)
                    mm_feat(wti, 2 * pr + 1, tg, bb)
                    s1, s2 = nstf(), nstf()
                    P.op("act", (lambda s1, ba: lambda e: e.activation(out=stf[s1][:], in_=ps[:, ba, :], func=AF.Gelu))(s1, ba),
                         reads=[b_ps[ba]], writes=[b_stf[s1]])
                    P.op("act", (lambda s2, bb: lambda e: e.activation(out=stf[s2][:], in_=ps[:, bb, :], func=AF.Silu))(s2, bb),
                         reads=[b_ps[bb]], writes=[b_stf[s2]])
                    P.op("dve", (lambda ch, tg, s1, s2: lambda e: e.tensor_tensor(out=ug[:, ch, tg * 512:(tg + 1) * 512], in0=stf[s1][:], in1=stf[s2][:], op=ALU.mult))(ch, tg, s1, s2),
                         reads=[b_stf[s1], b_stf[s2]], writes=[b_ug], waw=False)
        elif kind == "v":
            for tb in range(NTB):
                bank = nbank()
                mm_tok(wti, tb, bank, 0, 512)
                P.op("act", (lambda tb, bank: lambda e: e.activation(out=vstage[:].rearrange("p (h k) d -> p h k d", h=4)[:, :, tb, :],
                                                                     in_=ps[:, bank, :].rearrange("p (h d) -> p h d", h=4), func=AF.Copy))(tb, bank),
                     reads=[b_ps[bank]], writes=[b_vstage], waw=(tb == 0))
            P.dma("sp", (lambda sub: lambda e: e.dma_start(out=o_v[:, sub * 4:(sub + 1) * 4, :, :],
                                                           in_=vstage[:].rearrange("p (h k) d -> p h k d", h=4)))(sub),
                  b_vstage, reads=[b_vstage], writes=[b_blobv], waw=False)
        elif kind == "vsgu":
            for tb in range(NTB):
                bank = nbank()
                mm_tok(wti, tb, bank, 512, 8)
                lfi = rot["lf"] % 2
                rot["lf"] += 1
                P.op("dve", (lambda lfi, bank: lambda e: e.tensor_tensor(out=lfst[lfi][:], in0=ps[:, bank, 0:8], in1=bfb_s[:], op=ALU.add))(lfi, bank),
                     reads=[b_ps[bank], b_par], writes=[b_lfst[lfi]])
                P.op("act", (lambda lfi: lambda e: e.activation(out=lfst[lfi][:], in_=lfst[lfi][:], func=AF.Exp, scale=-1.0))(lfi),
                     reads=[b_lfst[lfi]], writes=[b_lfst[lfi]])
                P.op("act", (lambda lfi: lambda e: e.activation(out=lfst[lfi][:], in_=lfst[lfi][:], func=AF.Ln, bias=1.0, scale=1.0))(lfi),
                     reads=[b_lfst[lfi]], writes=[b_lfst[lfi]])
                P.op("dve", (lambda lfi: lambda e: e.tensor_scalar(out=lfst[lfi][:], in0=lfst[lfi][:], scalar1=-1.0, scalar2=None, op0=ALU.mult))(lfi),
                     reads=[b_lfst[lfi]], writes=[b_lfst[lfi]])
                bank = nbank()
                P.op("pe", (lambda lfi, bank: lambda e: e.transpose(out=ps[0:8, bank, 0:128], in_=lfst[lfi][:], identity=identf[:]))(lfi, bank),
                     reads=[b_lfst[lfi], b_identf], writes=[b_ps[bank]])
                P.op("dve", (lambda tb, bank: lambda e: e.tensor_copy(out=lfT_sb[tb % 2][:], in_=ps[0:8, bank, 0:128]))(tb, bank),
                     reads=[b_ps[bank]], writes=[b_lfT_sb[tb % 2]])
                P.dma("sp", (lambda tb: lambda e: e.dma_start(out=o_lfT[:, tb, :], in_=lfT_sb[tb % 2][:]))(tb),
                      b_lfT_sb[tb % 2], reads=[b_lfT_sb[tb % 2]], writes=[b_bloblf], waw=False)
                bank = nbank()
                mm_tok(wti, tb, bank, 0, 512)
                s1 = nstf()
                P.op("act", (lambda s1, bank: lambda e: e.activation(out=stf[s1][:], in_=ps[:, bank, :], func=AF.Gelu))(s1, bank),
                     reads=[b_ps[bank]], writes=[b_stf[s1]])
                P.op("dve", (lambda s1: lambda e: e.bn_stats(out=small[:, 8:14], in_=stf[s1][:]))(s1),
                     reads=[b_stf[s1]], writes=[b_small])
                P.op("dve", lambda e: e.bn_aggr(out=small[:, 16:18], in_=small[:, 8:14]), reads=[b_small], writes=[b_small])
                P.op("act", lambda e: e.activation(out=small[:, 18:19], in_=small[:, 17:18], func=AF.Sqrt, bias=EPS, scale=1.0),
                     reads=[b_small], writes=[b_small])
                P.op("dve", lambda e: e.reciprocal(out=small[:, 19:20], in_=small[:, 18:19]), reads=[b_small], writes=[b_small])
                P.op("dve", (lambda s1: lambda e: e.tensor_scalar(out=stf[s1][:], in0=stf[s1][:], scalar1=small[:, 16:17], scalar2=small[:, 19:20],
                                                                   op0=ALU.subtract, op1=ALU.mult))(s1),
                     reads=[b_stf[s1], b_small], writes=[b_stf[s1]])
                P.op("dve", (lambda s1: lambda e: e.tensor_tensor(out=stf[s1][:], in0=stf[s1][:], in1=sgg_s[:], op=ALU.mult))(s1),
                     reads=[b_stf[s1], b_par], writes=[b_stf[s1]])
                P.op("dve", (lambda s1: lambda e: e.tensor_tensor(out=vln[:], in0=stf[s1][:], in1=sgb_s[:], op=ALU.add))(s1),
                     reads=[b_stf[s1], b_par], writes=[b_vln])
                bank = nbank()
                for hh in range(4):
                    P.op("pe", (lambda hh, bank: lambda e: e.matmul(ps[:, bank, hh * 128:(hh + 1) * 128], lhsT=vln[:, hh * 128:(hh + 1) * 128],
                                                                    rhs=wsT[:, hh, :], start=True, stop=True))(hh, bank),
                         reads=[b_vln, b_wsT], writes=[b_ps[bank]], waw=(hh == 0))
                s2 = nstf()
                P.op("dve", (lambda s2, bank: lambda e: e.tensor_tensor(out=stf[s2][:], in0=ps[:, bank, :], in1=sgub_s[:], op=ALU.add))(s2, bank),
                     reads=[b_ps[bank], b_par], writes=[b_stf[s2]])
                P.op("dve", (lambda s2, tb: lambda e: e.tensor_tensor(out=ug[:, :, tb * 128:(tb + 1) * 128],
                                                                      in0=stf[s2][:].rearrange("p (h t) -> p h t", h=4),
                                                                      in1=ug[:, :, tb * 128:(tb + 1) * 128], op=ALU.mult))(s2, tb),
                     reads=[b_stf[s2], b_ug], writes=[b_ug])
    while pending_ag:
        io["ag"](pending_ag.pop(0))
    for ch in range(4):
        P.dma("sp", (lambda ch: lambda e: e.dma_start(out=o_ysgu[ch * 128:(ch + 1) * 128, :], in_=ug[:, ch, :]))(ch),
              b_ug, reads=[b_ug], writes=[b_own], waw=False)


def emit_B(cx, io, li, do_conv=True, do_att=True, nq=S // 512):
    nc, P = cx.nc, cx.P
    cx.new_phase()
    gathhalo = io["gathhalo"].rearrange("(r c p) t -> r p c t", c=4, p=128)
    gathlf = io["gathlf"].rearrange("p (h j) -> h p j", j=128)
    glu_d = io["glu"]
    gconv_d = io["gconv"]
    cpar_d = io["cpar"][li]
    pw_d = io["pw"][li]
    o_yatt = io["yatt"]
    o_yconv = io["yconv"]
    b_g32, b_glf, b_own = io["b_gathhalo"], io["b_gathlf"], io["b_own"]
    b_yatt, b_yconv = io["b_yatt"], io["b_yconv"]
    b_cid = io["b_cid"]
    cid_sb, cmask = io["cid_sb"], io["cmask_sb"]

    def dyn(e, idx, lo, hi):
        key = (id(e), idx)
        if key not in cx.vals:
            reg = e.alloc_register("dyn%d" % idx)
            e.reg_load(reg, cid_sb[0:1, idx:idx + 1])
            cx.vals[key] = e.snap(reg, min_val=lo, max_val=hi)
        return cx.vals[key]

    ps = cx.ps
    b_ps = cx.b_ps
    ones_b, b_ones_b = _consts_tri(cx, "ones_b", BF16, ALU.is_ge, 0, 0)
    tri_b, b_tri_b = _consts_tri(cx, "tri_b", BF16, ALU.is_ge, -1, 1)

    if do_conv:
        ypad = [cx.sb("ypad%d" % i, [128, HALO + TOK], F32) for i in range(2)]
        acc = cx.sb("acc", [128, 4, TOK], F32)
        gconv = cx.sb("gconv", [128, 4, TOK], BF16)
        cpar = cx.sb("cpar", [128, 4, 36], F32)
        pw = cx.sb("pw", [128, 4, 512], BF16)
        accb = cx.sb("accb", [128, 4, 512], BF16)
        sqb = cx.sb("sqb", [128, 4, 512], BF16)
        sT = cx.sb("sT", [128, 4, 512], BF16)
        mu = cx.sb("mu", [128, 512], F32)
        var = cx.sb("var", [128, 512], F32)
        ycst = [cx.sb("ycst%d" % i, [128, 512], BF16) for i in range(2)]
        b_ypad = [Buf("yp0"), Buf("yp1")]
        b_acc = [Buf("acc%d" % i) for i in range(4)]
        b_gconv, b_cpar, b_pw, b_accb, b_sqb, b_sT, b_mu, b_var = [Buf(x) for x in "gconv cpar pw accb sqb sT mu var".split()]
        b_ycst = [Buf("yc0"), Buf("yc1")]
        P.dma("sp", lambda e: e.dma_start(out=cpar[:], in_=cpar_d), b_cpar, writes=[b_cpar])
        P.dma("sp", lambda e: e.dma_start(out=gconv[:], in_=gconv_d.rearrange("(c p) t -> p c t", p=128)), b_gconv, reads=[b_own], writes=[b_gconv])
        P.dma("pool", lambda e: e.dma_start(out=pw[:], in_=pw_d.rearrange("(c p) n -> p c n", p=128)), b_pw, writes=[b_pw])
        ypv = glu_d.rearrange("(c p) t -> c p t", p=128)
        halo_sb = cx.sb("halo_sb", [128, 4, 32], F32)
        b_halo = Buf("halo")

        def ld_halo(e):
            pv = dyn(e, 2, 0, NCORE - 1)
            return e.dma_start(out=halo_sb[:], in_=gathhalo[pv])
        P.dma("sp", ld_halo, b_halo, reads=[b_g32, b_cid], writes=[b_halo])
        def conv_taps():
            for cc in range(4):
                yp, byp = ypad[cc % 2], b_ypad[cc % 2]
                P.dma("sp", (lambda yp, cc: lambda e: e.dma_start(out=yp[:, HALO:HALO + TOK], in_=ypv[cc]))(yp, cc), byp, reads=[b_own], writes=[byp])

                P.op("dve", (lambda yp, cc: lambda e: e.tensor_scalar(out=yp[:, 0:HALO], in0=halo_sb[:, cc, 32 - HALO:32], scalar1=cmask[:, 0:1], scalar2=None, op0=ALU.mult))(yp, cc),
                     reads=[byp, b_halo, b_cid], writes=[byp], waw=False)
                P.op("dve", (lambda yp, cc: lambda e: e.tensor_scalar(out=acc[:, cc, :], in0=yp[:, 0:TOK], scalar1=cpar[:, cc, 0:1],
                                                                       scalar2=cpar[:, cc, 31:32], op0=ALU.mult, op1=ALU.add))(yp, cc),
                     reads=[byp, b_cpar], writes=[b_acc[cc]])
                yield
                for j in range(1, CONV_K):
                    P.op("dve", (lambda yp, cc, j: lambda e: e.scalar_tensor_tensor(out=acc[:, cc, :], in0=yp[:, j:j + TOK], scalar=cpar[:, cc, j:j + 1],
                                                                                    in1=acc[:, cc, :], op0=ALU.mult, op1=ALU.add))(yp, cc, j),
                         reads=[byp, b_cpar, b_acc[cc]], writes=[b_acc[cc]])
                    yield
        def conv_post():
            for tg in range(4):
                sl = slice(tg * 512, (tg + 1) * 512)
                P.op("act", (lambda sl: lambda e: e.activation(out=accb[:], in_=acc[:, :, sl], func=AF.Copy))(sl), reads=b_acc, writes=[b_accb])
                P.op("act", (lambda sl: lambda e: e.activation(out=sqb[:], in_=acc[:, :, sl], func=AF.Square))(sl), reads=b_acc, writes=[b_sqb])
                for cc in range(4):
                    P.op("pe", (lambda cc: lambda e: e.matmul(ps[:, 0, :], lhsT=ones_b[:], rhs=accb[:, cc, :], start=(cc == 0), stop=(cc == 3)))(cc),
                         reads=[b_ones_b, b_accb], writes=[b_ps[0]], waw=(cc == 0))
                for cc in range(4):
                    P.op("pe", (lambda cc: lambda e: e.matmul(ps[:, 1, :], lhsT=ones_b[:], rhs=sqb[:, cc, :], start=(cc == 0), stop=(cc == 3)))(cc),
                         reads=[b_ones_b, b_sqb], writes=[b_ps[1]], waw=(cc == 0))
                P.op("act", lambda e: e.activation(out=mu[:], in_=ps[:, 0, :], func=AF.Copy, scale=1.0 / 512), reads=[b_ps[0]], writes=[b_mu])
                P.op("dve", lambda e: e.tensor_tensor(out=var[:], in0=mu[:], in1=mu[:], op=ALU.mult), reads=[b_mu], writes=[b_var])
                P.op("dve", lambda e: e.scalar_tensor_tensor(out=var[:], in0=ps[:, 1, :], scalar=1.0 / 512, in1=var[:], op0=ALU.mult, op1=ALU.subtract),
                     reads=[b_ps[1], b_var], writes=[b_var])
                P.op("act", lambda e: e.activation(out=var[:], in_=var[:], func=AF.Sqrt, bias=EPS, scale=1.0), reads=[b_var], writes=[b_var])
                P.op("dve", lambda e: e.reciprocal(out=var[:], in_=var[:]), reads=[b_var], writes=[b_var])
                for cc in range(4):
                    P.op("dve", (lambda cc, sl: lambda e: e.tensor_tensor(out=acc[:, cc, sl], in0=acc[:, cc, sl], in1=mu[:], op=ALU.subtract))(cc, sl),
                         reads=[b_acc[cc], b_mu], writes=[b_acc[cc]])
                    P.op("dve", (lambda cc, sl: lambda e: e.tensor_tensor(out=acc[:, cc, sl], in0=acc[:, cc, sl], in1=var[:], op=ALU.mult))(cc, sl),
                         reads=[b_acc[cc], b_var], writes=[b_acc[cc]])
                    P.op("act", (lambda cc, sl: lambda e: e.activation(out=sT[:, cc, :], in_=acc[:, cc, sl], func=AF.Silu,
                                                                       scale=cpar[:, cc, 32:33], bias=cpar[:, cc, 33:34]))(cc, sl),
                         reads=[b_acc[cc], b_cpar], writes=[b_sT], waw=(cc == 0))
                for co in range(4):
                    bank = 2 + co % 2
                    for cc in range(4):
                        P.op("pe", (lambda cc, co, bank: lambda e: e.matmul(ps[:, bank, :], lhsT=pw[:, cc, co * 128:(co + 1) * 128], rhs=sT[:, cc, :],
                                                                            start=(cc == 0), stop=(cc == 3)))(cc, co, bank),
                             reads=[b_pw, b_sT], writes=[b_ps[bank]], waw=(cc == 0))
                    si = co % 2
                    P.op("dve", (lambda co, bank, si, sl: lambda e: e.scalar_tensor_tensor(out=ycst[si][:], in0=ps[:, bank, :], scalar=cpar[:, co, 34:35],
                                                                                           in1=gconv[:, co, sl], op0=ALU.add, op1=ALU.mult))(co, bank, si, sl),
                         reads=[b_ps[bank], b_cpar, b_gconv], writes=[b_ycst[si]])
                    P.dma("sp", (lambda co, si, sl: lambda e: e.dma_start(out=o_yconv[co * 128:(co + 1) * 128, sl], in_=ycst[si][:]))(co, si, sl),
                          b_ycst[si], reads=[b_ycst[si]], writes=[b_yconv], waw=False)

        taps_gen = conv_taps()
    else:
        taps_gen, conv_post = iter(()), (lambda: None)

    if do_att:
        NPC = 8
        PCW = S // NPC
        qT = cx.sb("qT_s", [128, S], BF16)
        kT = cx.sb("kT_s", [128, S], BF16)
        vv = cx.sb("v_s", [128, NKB, 128], BF16)
        lf = cx.sb("lf_s", [128, NKB], F32)
        lfT = cx.sb("lfT_s", [128, 128], F32)
        tot = cx.sb("tot", [128, 2], F32)
        totbc = cx.sb("totbc", [128, 128], F32)
        ck = cx.sb("ck", [128, NKB], F32)
        cend = cx.sb("cend", [128, NKB], F32)
        bias4 = [cx.sb("bias4_%d" % i, [128, 4, NKB], F32) for i in range(2)]
        pT = [cx.sb("pT%d" % i, [128, 512], BF16) for i in range(3)]
        rinv = cx.sb("rinv", [128, 512], F32)
        yst = [cx.sb("yst%d" % i, [128, 512], BF16) for i in range(2)]
        b_q = [Buf("q%d" % i) for i in range(NPC)]
        b_k = [Buf("k%d" % i) for i in range(NPC)]
        b_v = [Buf("v%d" % i) for i in range(NPC)]
        b_lf, b_lfT, b_tot, b_totbc, b_ck, b_cend, b_rinv = [Buf(x) for x in "lf lfT tot totbc ck cend rinv".split()]
        b_bias4 = [Buf("b40"), Buf("b41")]
        b_pT = [Buf("pT%d" % i) for i in range(3)]
        b_yst = [Buf("ys0"), Buf("ys1")]
        u32, b_u32 = _consts_tri(cx, "u32", F32, ALU.is_ge, -1, 1)
        su32, b_su32 = _consts_tri(cx, "su32", F32, ALU.is_gt, -1, 1)
        ui32 = u32
        ones32, b_ones32 = _consts_tri(cx, "ones32", F32, ALU.is_ge, 0, 0)

        identf, b_identf = _consts_tri(cx, "identf", F32, ALU.is_equal, 1, -1)
        g4 = {nm: io["gath" + nm].rearrange("(r h d) t -> r h d t", h=8, d=128) for nm in ("q", "k", "v")}
        def ld_lfT(e):
            cv = dyn(e, 3, 0, NCORE - 1)
            return e.dma_start(out=lfT[:], in_=gathlf[cv])
        P.dma("sp", ld_lfT, b_lfT, reads=[b_glf, b_cid], writes=[b_lfT])
        P.op("pe", lambda e: e.transpose(out=ps[:, 7, 256:384], in_=lfT[:], identity=identf[:]), reads=[b_lfT, b_identf], writes=[b_ps[7]])
        P.op("dve", lambda e: e.tensor_copy(out=lf[:], in_=ps[:, 7, 256:384]), reads=[b_ps[7]], writes=[b_lf])
        vflat = vv.rearrange("p a b -> p (a b)")
        for hf in range(2):
            r0, r1 = hf * 4, hf * 4 + 4
            for sec, dst, bb in (("k", kT, b_k), ("q", qT, b_q), ("v", vflat, b_v)):
                def ld(e, sec=sec, dst=dst, r0=r0, r1=r1):
                    cv = dyn(e, 3, 0, NCORE - 1)
                    return e.dma_start(out=dst[:, r0 * PCW:r1 * PCW].rearrange("p (r t) -> p r t", r=4),
                                       in_=g4[sec][r0:r1, cv].rearrange("r d t -> d r t"))
                P.dma("sp", ld, bb[r0], reads=[io["b_gath" + sec], b_cid], writes=bb[r0:r1])

        P.op("dve", lambda e: e.tensor_reduce(out=tot[:, 0:1], in_=lfT[:], axis=AX.X, op=ALU.add), reads=[b_lfT], writes=[b_tot])
        P.op("dve", lambda e: e.tensor_scalar(out=totbc[:], in0=ones32[:], scalar1=tot[:, 0:1], scalar2=None, op0=ALU.mult),
             reads=[b_ones32, b_tot], writes=[b_totbc])
        P.op("pe", lambda e: e.matmul(ps[:, 7, 0:128], lhsT=u32[:], rhs=lf[:], start=True, stop=False), reads=[b_u32, b_lf], writes=[b_ps[7]])
        P.op("pe", lambda e: e.matmul(ps[:, 7, 0:128], lhsT=totbc[:], rhs=su32[:], start=False, stop=True), reads=[b_totbc, b_su32], writes=[b_ps[7]], waw=False)
        P.op("pe", lambda e: e.matmul(ps[:, 7, 128:256], lhsT=totbc[:], rhs=ui32[:], start=True, stop=True), reads=[b_totbc, b_u32], writes=[b_ps[7]], waw=False)
        P.op("dve", lambda e: e.tensor_copy(out=ck[:], in_=ps[:, 7, 0:128]), reads=[b_ps[7]], writes=[b_ck])
        P.op("dve", lambda e: e.tensor_copy(out=cend[:], in_=ps[:, 7, 128:256]), reads=[b_ps[7]], writes=[b_cend])

        tiles = [(qi, kb) for qi in range(nq) for kb in range(4 * qi + 4)]
        LOOK = 2

        def t_qk(t):
            qi, kb = tiles[t]
            c0 = 128 * max(0, kb - 4 * qi)
            sb_ = t % 3
            kpc, qpc = (kb * 128) // PCW, (qi * 512) // PCW
            P.op("pe", (lambda sb_, kb, qi, c0: lambda e: e.matmul(ps[:, sb_, c0:512], lhsT=kT[:, kb * 128:(kb + 1) * 128],
                                                                   rhs=qT[:, qi * 512 + c0:(qi + 1) * 512], start=True, stop=True))(sb_, kb, qi, c0),
                 reads=[b_k[kpc], b_q[qpc]], writes=[b_ps[sb_]])

        def t_exp(t):
            qi, kb = tiles[t]
            b4, bb4 = bias4[qi % 2], b_bias4[qi % 2]
            if kb == 0:
                for j2 in range(2):
                    g = 4 * qi + 2 * j2
                    P.op("dve", (lambda b4, j2, g: lambda e: e.tensor_scalar(out=b4[:, j2, 0:g + 2], in0=ck[:, 0:g + 2], scalar1=-1.0, scalar2=cend[:, g:g + 1],
                                                                             op0=ALU.mult, op1=ALU.add))(b4, j2, g),
                         reads=[b_ck, b_cend], writes=[bb4], waw=(j2 == 0))
            j0 = max(0, kb - 4 * qi)
            c0 = 128 * j0
            sb_ = t % 3
            pt, bpt = pT[sb_], b_pT[sb_]
            firstw = True
            for j2 in range(2):
                lo, hi = max(c0, 256 * j2), 256 * (j2 + 1)
                if lo >= hi:
                    continue
                P.op("act", (lambda pt, sb_, lo, hi, j2, b4, kb: lambda e: e.activation(out=pt[:, lo:hi], in_=ps[:, sb_, lo:hi],
                                                                                       func=AF.Exp, bias=b4[:, j2, kb:kb + 1], scale=SCALE))(pt, sb_, lo, hi, j2, b4, kb),
                     reads=[b_ps[sb_], bb4], writes=[bpt], waw=firstw)
                firstw = False
            if kb >= 4 * qi:
                P.op("pool", (lambda pt, c0: lambda e: e.tensor_tensor(out=pt[:, c0:c0 + 128], in0=pt[:, c0:c0 + 128], in1=tri_b[:], op=ALU.mult))(pt, c0),
                     reads=[bpt, b_tri_b], writes=[bpt])

        def t_pv(t):
            qi, kb = tiles[t]
            nkb = 4 * qi + 4
            c0 = 128 * max(0, kb - 4 * qi)
            sb_ = t % 3
            pt, bpt = pT[sb_], b_pT[sb_]
            ob, lb = 3 + qi % 2, 5 + qi % 2
            kpc = (kb * 128) // PCW
            first, last = (kb == 0), (kb == nkb - 1)
            P.op("pe", (lambda ob, kb, pt, c0, first, last: lambda e: e.matmul(ps[:, ob, c0:512], lhsT=vv[:, kb, :], rhs=pt[:, c0:512],
                                                                               start=first, stop=last, skip_group_check=True))(ob, kb, pt, c0, first, last),
                 reads=[b_v[kpc], bpt], writes=[b_ps[ob]], waw=first)
            P.op("pe", (lambda lb, pt, c0, first, last: lambda e: e.matmul(ps[:, lb, c0:512], lhsT=ones_b[:], rhs=pt[:, c0:512],
                                                                           start=first, stop=last, skip_group_check=True))(lb, pt, c0, first, last),
                 reads=[b_ones_b, bpt], writes=[b_ps[lb]], waw=first)
            if last:
                yi = qi % 2
                P.op("dve", (lambda lb: lambda e: e.reciprocal(out=rinv[:], in_=ps[:, lb, :]))(lb), reads=[b_ps[lb]], writes=[b_rinv])
                P.op("dve", (lambda ob, yi: lambda e: e.tensor_tensor(out=yst[yi][:], in0=ps[:, ob, :], in1=rinv[:], op=ALU.mult))(ob, yi),
                     reads=[b_ps[ob], b_rinv], writes=[b_yst[yi]])
                P.dma("sp", (lambda qi, yi: lambda e: e.dma_start(out=o_yatt[:, qi * 512:(qi + 1) * 512], in_=yst[yi][:]))(qi, yi),
                      b_yst[yi], reads=[b_yst[yi]], writes=[b_yatt], waw=False)
                for _ in range(4):
                    next(taps_gen, None)

        for t in range(-LOOK, len(tiles)):
            if t + LOOK < len(tiles):
                t_qk(t + LOOK)
            if t >= 0:
                t_exp(t)
                t_pv(t)
    for _ in taps_gen:
        pass
    conv_post()


def emit_C(cx, io, li):
    nc, P = cx.nc, cx.P
    cx.new_phase()
    yc_d = io["yconv"]
    ys_d = io["ysgu"]
    gathya = io["gathya"]
    gy = gathya.rearrange("hd (c t) -> c hd t", c=NCORE)
    ga_d = io["gatt"]
    h_d = io["h_in"][li]
    p_d = io["p"][li]
    wo_d = io["w_out"][li]
    wpg_d = io["w_pg"][li]
    wpp_d = io["w_pp"][li]
    gpost_d = io["gpost"][li]
    o_h = io["h_out"][li]
    hmid = io["hmid"]
    b_own, b_yconv, b_gya, b_cid = io["b_own"], io["b_yconv"], io["b_gathya"], io["b_cid"]
    b_hin, b_hout = io["b_h"][li], io["b_h"][li + 1]
    cid_sb = io["cid_sb"]

    def dyn(e, idx, lo, hi):
        key = (id(e), idx)
        if key not in cx.vals:
            reg = e.alloc_register("dyn%d" % idx)
            e.reg_load(reg, cid_sb[0:1, idx:idx + 1])
            cx.vals[key] = e.snap(reg, min_val=lo, max_val=hi)
        return cx.vals[key]

    ps = cx.ps
    b_ps = cx.b_ps
    yT = cx.sb("yT", [128, NKC, TOK], BF16)
    W = cx.sb("W", [128, NKC, D], BF16)
    wpp = cx.sb("wpp", [128, 2, D], BF16)
    gpost = cx.sb("gpost", [128, D], F32)
    gat = cx.sb("gat", [128, 1, TOK], BF16)
    hb = [cx.sb("hb%d" % i, [128, D], F32) for i in range(2)]
    tmp = cx.sb("tmp", [128, D], F32)
    hmb = cx.sb("hmb", [128, D], BF16)
    junk = hmb
    pb32 = cx.sb("pb32", [128, 256], F32)
    pbb = cx.sb("pbb", [128, 256], BF16)
    hmT = cx.sb("hmT", [128, NKC, 128], BF16)
    pTb = cx.sb("pTb", [128, 2, 128], BF16)
    sig = [cx.sb("sig%d" % i, [128, 512], F32) for i in range(2)]
    ost = cx.sb("ost", [128, D], F32)
    small = cx.sb("small", [128, 8], F32)
    b_yT = [Buf("yT%d" % i) for i in range(NKC)]
    b_W = [Buf("W%d" % i) for i in range(4)]
    b_wpp, b_gpost, b_tmp, b_junk_unused, b_hmb, b_pb32, b_pbb, b_hmT, b_pTb, b_small = [
        Buf(x) for x in "wpp gpost tmp junk hmb pb32 pbb hmT pTb small".split()]
    b_gat = Buf("gat")
    b_hb = [Buf("hb0"), Buf("hb1")]
    b_sig = [Buf("sig0"), Buf("sig1")]
    b_ost = Buf("ost")
    b_hmid = [Buf("hmid%d" % i) for i in range(NTB)]
    identb, b_identb = _consts_tri(cx, "identb", BF16, ALU.is_equal, 1, -1)

    P.dma("sp", lambda e: e.dma_start(out=gpost[:], in_=gpost_d), b_gpost, writes=[b_gpost])
    wov = wo_d.rearrange("(kc p) n -> p kc n", p=128)
    P.dma("pool", lambda e: e.dma_start(out=W[:], in_=wov), b_W[0], writes=b_W)
    for c in range(4):
        P.dma("sp", (lambda c: lambda e: e.dma_start(out=yT[:, c, :], in_=yc_d[c * 128:(c + 1) * 128, :]))(c), b_yT[c], reads=[b_yconv], writes=[b_yT[c]])
        P.dma("sp", (lambda c: lambda e: e.dma_start(out=yT[:, 4 + c, :], in_=ys_d[c * 128:(c + 1) * 128, :]))(c), b_yT[4 + c], reads=[b_own], writes=[b_yT[4 + c]])
    for hf in range(2):
        def ld_ya(e, hf=hf):
            cv = dyn(e, 3, 0, NCORE - 1)
            return e.dma_start(out=yT[:, 8 + hf * 4:12 + hf * 4, :], in_=gy[cv, hf * 512:(hf + 1) * 512, :].rearrange("(h d) t -> d h t", d=128))
        P.dma("sp", ld_ya, b_yT[8 + hf * 4], reads=[b_gya, b_cid], writes=b_yT[8 + hf * 4:12 + hf * 4])
    for c in range(8):
        P.dma("sp", (lambda c: lambda e: e.dma_start(out=gat[:, 0, :], in_=ga_d[c * 128:(c + 1) * 128, :]))(c), b_gat, reads=[b_own], writes=[b_gat])
        eng = "pool" if c % 2 == 0 else "dve"
        P.op(eng, (lambda c: lambda e: e.tensor_tensor(out=yT[:, 8 + c, :], in0=yT[:, 8 + c, :], in1=gat[:, 0, :], op=ALU.mult))(c),
             reads=[b_yT[8 + c], b_gat], writes=[b_yT[8 + c]])

    hv = h_d.rearrange("(tb p) d -> tb p d", p=128)
    hmv = hmid.rearrange("(tb p) d -> tb p d", p=128)
    ohv = o_h.rearrange("(tb p) d -> tb p d", p=128)
    pv = p_d.rearrange("(tb p) d -> tb p d", p=128)
    for tb in range(NTB):
        base = 4 * (tb % 2)
        for ct in range(4):
            for kc in range(NKC):
                P.op("pe", (lambda ct, kc, tb, base: lambda e: e.matmul(ps[:, base + ct, :], lhsT=yT[:, kc, tb * 128:(tb + 1) * 128],
                                                                        rhs=W[:, kc, ct * 512:(ct + 1) * 512], start=(kc == 0), stop=(kc == NKC - 1)))(ct, kc, tb, base),
                     reads=[b_yT[kc], b_W[ct]], writes=[b_ps[base + ct]], waw=(kc == 0))
        hbt, bhb = hb[tb % 2], b_hb[tb % 2]
        P.dma("sp", (lambda hbt, tb: lambda e: e.dma_start(out=hbt[:], in_=hv[tb]))(hbt, tb), bhb, reads=[b_hin], writes=[bhb])
        pfull = ps[:, base:base + 4, :]
        bpf = b_ps[base:base + 4]
        P.op("act", (lambda pfull: lambda e: e.activation(out=junk[:].rearrange("p (c n) -> p c n", c=4), in_=pfull, func=AF.Square, accum_out=small[:, 0:1]))(pfull),
             reads=bpf, writes=[b_hmb, b_small])
        P.op("act", lambda e: e.activation(out=small[:, 1:2], in_=small[:, 0:1], func=AF.Sqrt, bias=EPS, scale=1.0 / D), reads=[b_small], writes=[b_small])
        P.op("dve", lambda e: e.reciprocal(out=small[:, 2:3], in_=small[:, 1:2]), reads=[b_small], writes=[b_small])
        P.op("dve", (lambda pfull: lambda e: e.scalar_tensor_tensor(out=tmp[:].rearrange("p (c n) -> p c n", c=4), in0=pfull, scalar=small[:, 2:3],
                                                                    in1=gpost[:].rearrange("p (c n) -> p c n", c=4), op0=ALU.mult, op1=ALU.mult))(pfull),
             reads=bpf + [b_small, b_gpost], writes=[b_tmp])
        P.op("dve", (lambda hbt: lambda e: e.tensor_tensor(out=tmp[:], in0=tmp[:], in1=hbt[:], op=ALU.add))(hbt), reads=[b_tmp, bhb], writes=[b_tmp])
        P.dma("sp", (lambda tb: lambda e: e.dma_start(out=hmv[tb], in_=tmp[:]))(tb), b_tmp, reads=[b_tmp], writes=[b_hmid[tb]])

    wgv = wpg_d.rearrange("(kc p) n -> p kc n", p=128)
    P.dma("pool", lambda e: e.dma_start(out=W[:], in_=wgv), b_W[0], writes=b_W)
    P.dma("pool", lambda e: e.dma_start(out=wpp[:], in_=wpp_d.rearrange("(kc p) n -> p kc n", p=128)), b_wpp, writes=[b_wpp])
    rot = 0
    for tb in range(NTB):
        hbt, bhb = hb[tb % 2], b_hb[tb % 2]
        P.dma("sp", (lambda hbt, tb: lambda e: e.dma_start(out=hbt[:], in_=hmv[tb]))(hbt, tb), bhb, reads=[b_hmid[tb]], writes=[bhb])
        P.dma("sp", (lambda tb: lambda e: e.dma_start(out=pb32[:], in_=pv[tb]))(tb), b_pb32, writes=[b_pb32])
        P.op("act", (lambda hbt: lambda e: e.activation(out=hmb[:], in_=hbt[:], func=AF.Copy))(hbt), reads=[bhb], writes=[b_hmb])
        P.op("act", lambda e: e.activation(out=pbb[:], in_=pb32[:], func=AF.Copy), reads=[b_pb32], writes=[b_pbb])
        for half in range(2):
            pbk = ps[:, half, :].bitcast(BF16)
            for j in range(8):
                kc = half * 8 + j
                P.op("pe", (lambda pbk, j, kc: lambda e: e.transpose(out=pbk[:, j * 128:(j + 1) * 128], in_=hmb[:, kc * 128:(kc + 1) * 128], identity=identb[:]))(pbk, j, kc),
                     reads=[b_hmb, b_identb], writes=[b_ps[half]], waw=(j == 0))
            P.op("dve", (lambda pbk, half: lambda e: e.tensor_copy(out=hmT[:, half * 8:(half + 1) * 8, :], in_=pbk.rearrange("p (j t) -> p j t", j=8)))(pbk, half),
                 reads=[b_ps[half]], writes=[b_hmT], waw=(half == 0))
        pbk = ps[:, 0, :].bitcast(BF16)
        for j in range(2):
            P.op("pe", (lambda pbk, j: lambda e: e.transpose(out=pbk[:, j * 128:(j + 1) * 128], in_=pbb[:, j * 128:(j + 1) * 128], identity=identb[:]))(pbk, j),
                 reads=[b_pbb, b_identb], writes=[b_ps[0]], waw=(j == 0))
        P.op("dve", (lambda pbk: lambda e: e.tensor_copy(out=pTb[:], in_=pbk[:, 0:256].rearrange("p (j t) -> p j t", j=2)))(pbk),
             reads=[b_ps[0]], writes=[b_pTb])
        for ct in range(4):
            gb_, pb_ = 2 + 2 * (rot % 3), 3 + 2 * (rot % 3)
            si = rot % 2
            rot += 1
            for kc in range(NKC):
                P.op("pe", (lambda gb_, kc, ct: lambda e: e.matmul(ps[:, gb_, :], lhsT=hmT[:, kc, :], rhs=W[:, kc, ct * 512:(ct + 1) * 512],
                                                                   start=(kc == 0), stop=(kc == NKC - 1)))(gb_, kc, ct),
                     reads=[b_hmT, b_W[ct]], writes=[b_ps[gb_]], waw=(kc == 0))
            for kc in range(2):
                P.op("pe", (lambda pb_, kc, ct: lambda e: e.matmul(ps[:, pb_, :], lhsT=pTb[:, kc, :], rhs=wpp[:, kc, ct * 512:(ct + 1) * 512],
                                                                   start=(kc == 0), stop=(kc == 1)))(pb_, kc, ct),
                     reads=[b_pTb, b_wpp], writes=[b_ps[pb_]], waw=(kc == 0))
            P.op("act", (lambda si, gb_: lambda e: e.activation(out=sig[si][:], in_=ps[:, gb_, :], func=AF.Sigmoid))(si, gb_),
                 reads=[b_ps[gb_]], writes=[b_sig[si]])
            P.op("dve", (lambda si, pb_: lambda e: e.tensor_tensor(out=sig[si][:], in0=sig[si][:], in1=ps[:, pb_, :], op=ALU.mult))(si, pb_),
                 reads=[b_sig[si], b_ps[pb_]], writes=[b_sig[si]])
            P.op("dve", (lambda si, hbt, ct: lambda e: e.tensor_tensor(out=ost[:, ct * 512:(ct + 1) * 512], in0=sig[si][:], in1=hbt[:, ct * 512:(ct + 1) * 512], op=ALU.add))(si, hbt, ct),
                 reads=[b_sig[si], bhb], writes=[b_ost], waw=(ct == 0))
        P.dma("sp", (lambda tb: lambda e: e.dma_start(out=ohv[tb], in_=ost[:]))(tb),
              b_ost, reads=[b_ost], writes=[b_hout], waw=False)


I32 = mybir.dt.int32
DBG = {}


def build_fused():
    cx = Ctx("F")
    nc, P = cx.nc, cx.P
    io = {}
    x = cx.din("x", [TOK, D], F32)
    out = cx.dout("out", [TOK, D], F32)
    io["p"] = cx.din("p", [2, TOK, 256], F32)
    io["w_in"] = cx.din("w_in", [2, D, NIN], F32)
    io["gpre"] = cx.din("gpre", [2, 128, D], F32)
    io["bfb"] = cx.din("bfb", [2, 128, 8], F32)
    io["sgg"] = cx.din("sgg", [2, 128, 512], F32)
    io["sgb"] = cx.din("sgb", [2, 128, 512], F32)
    io["sguw"] = cx.din("sguw", [2, 128, 4, 128], F32)
    io["sgub"] = cx.din("sgub", [2, 128, 512], F32)
    io["cpar"] = cx.din("cpar", [2, 128, 4, 36], F32)
    io["pw"] = cx.din("pw", [2, 512, 512], F32)
    io["w_out"] = cx.din("w_out", [2, D, D], F32)
    io["w_pg"] = cx.din("w_pg", [2, D, D], F32)
    io["w_pp"] = cx.din("w_pp", [2, 256, D], F32)
    io["gpost"] = cx.din("gpost", [2, 128, D], F32)
    cid = cx.din("cid", [1, 8], I32)
    cmask = cx.din("cmask", [128, 1], F32)
    h1 = cx.dscr("h1", [TOK, D], F32)
    io["hmid"] = cx.dscr("hmid", [TOK, D], F32)
    for nm in ("q", "k", "v"):
        io["blob" + nm] = cx.dscr("blob" + nm, [1024, TOK], BF16)
        io["gath" + nm] = cx.dscr("gath" + nm, [NCORE * 1024, TOK], BF16)
    io["bloblf"] = cx.dscr("bloblf", [16, 1024], F32)
    io["gathlf"] = cx.dscr("gathlf", [NCORE * 16, 1024], F32)
    io["blobhalo"] = cx.dscr("blobhalo", [512, 32], F32)
    io["gathhalo"] = cx.dscr("gathhalo", [NCORE * 512, 32], F32)
    io["glu"] = cx.dscr("glu", [512, TOK], F32)
    io["gatt"] = cx.dscr("gatt", [1024, TOK], BF16)
    io["gconv"] = cx.dscr("gconv", [512, TOK], BF16)
    io["ysgu"] = cx.dscr("ysgu", [512, TOK], BF16)
    io["yconv"] = cx.dscr("yconv", [512, TOK], BF16)
    io["yatt"] = cx.dscr("yatt", [128, S], BF16)
    io["gathya"] = cx.dscr("gathya", [NCORE * 128, S], BF16)
    io["h_in"] = [x, h1]
    io["h_out"] = [h1, out]
    io["b_h"] = [Buf("hx"), Buf("h1"), Buf("hout")]
    for n in ("blobq", "gathq", "blobk", "gathk", "blobv", "gathv", "bloblf", "gathlf", "blobhalo", "gathhalo", "own", "yconv", "yatt", "gathya", "cid"):
        io["b_" + n] = Buf(n)

    _consts_tri(cx, "identb", BF16, ALU.is_equal, 1, -1)
    _consts_tri(cx, "identf", F32, ALU.is_equal, 1, -1)
    _consts_tri(cx, "ones_b", BF16, ALU.is_ge, 0, 0)
    _consts_tri(cx, "tri_b", BF16, ALU.is_ge, -1, 1)
    _consts_tri(cx, "u32", F32, ALU.is_ge, -1, 1)
    _consts_tri(cx, "su32", F32, ALU.is_gt, -1, 1)
    _consts_tri(cx, "ones32", F32, ALU.is_ge, 0, 0)
    cid_sb = cx.sb("cid_sb", [1, 8], I32)
    cmask_sb = cx.sb("cmask_sb", [128, 1], F32)
    io["cid_sb"], io["cmask_sb"] = cid_sb, cmask_sb
    P.dma("sp", lambda e: e.dma_start(out=cid_sb[:], in_=cid), io["b_cid"], writes=[io["b_cid"]], waw=False)
    P.dma("sp", lambda e: e.dma_start(out=cmask_sb[:], in_=cmask), io["b_cid"], writes=[io["b_cid"]], waw=False)
    cx.persist_done()

    def allgather(src, dst, b_src, b_dst, qos="P2"):
        P.dma("pool", lambda e: e.collective_compute("AllGather", ALU.bypass, replica_groups=[list(range(NCORE))],
                                                     ins=[src.opt()], outs=[dst.opt()], dma_qos=qos),
              b_dst, reads=[b_src], writes=[b_dst], inc=1)

    io["ag"] = lambda nm: allgather(io["blob" + nm], io["gath" + nm], io["b_blob" + nm], io["b_gath" + nm], qos="P3")
    nl = DBG.get("layers", 2)
    if nl == 1:
        io["h_out"] = [out, out]
    for li in range(nl):
        emit_A(cx, io, li)
        allgather(io["bloblf"], io["gathlf"], io["b_bloblf"], io["b_gathlf"])
        allgather(io["blobhalo"], io["gathhalo"], io["b_blobhalo"], io["b_gathhalo"])
        emit_B(cx, io, li, nq=DBG.get("nq", S // 512))
        allgather(io["yatt"], io["gathya"], io["b_yatt"], io["b_gathya"])
        emit_C(cx, io, li)
    return cx.finish()


_NC = {}


def _bc(v, n=128):
    v = np.asarray(v, np.float32)
    return np.ascontiguousarray(np.broadcast_to(v.reshape(1, -1), (n, v.size)))


def kernel(**inputs):
    W = {k: np.asarray(v) for k, v in inputs.items()}
    perm = w_in_perm()
    L = range(2)
    cpar = np.zeros((2, 512, 36), np.float32)
    for li in L:
        cpar[li, :, 0:31] = W["conv_dw"][li].T
        cpar[li, :, 31] = W["conv_dw_b"][li]
        cpar[li, :, 32] = W["conv_ln_g"][li]
        cpar[li, :, 33] = W["conv_ln_b"][li]
        cpar[li, :, 34] = W["conv_pw_b"][li]
    cpar = np.ascontiguousarray(cpar.reshape(2, 4, 128, 36).transpose(0, 2, 1, 3))
    common = {
        "w_in": np.ascontiguousarray(W["w_in"][:, :, perm]),
        "gpre": np.stack([_bc(W["norm_pre"][li]) for li in L]),
        "bfb": np.stack([_bc(W["b_f"][li]) for li in L]),
        "sgg": np.stack([_bc(W["sgu_ln_g"][li]) for li in L]),
        "sgb": np.stack([_bc(W["sgu_ln_b"][li]) for li in L]),
        "sguw": np.ascontiguousarray(np.transpose(W["sgu_w"], (0, 2, 1, 3))),
        "sgub": np.stack([_bc(W["sgu_b"][li].reshape(-1)) for li in L]),
        "cpar": cpar,
        "pw": np.ascontiguousarray(W["conv_pw"]),
        "w_out": np.ascontiguousarray(W["w_out"]),
        "w_pg": np.ascontiguousarray(W["w_pg"]),
        "w_pp": np.ascontiguousarray(W["w_pp"]),
        "gpost": np.stack([_bc(W["norm_post"][li]) for li in L]),
    }
    x = W["x"][0]
    in_maps = []
    for c in range(NCORE):
        cid = np.array([[c * 128, c * TOK, max(c - 1, 0), c, 0, 0, 0, 0]], np.int32)
        in_maps.append(dict(common,
                            x=np.ascontiguousarray(x[c * TOK:(c + 1) * TOK]),
                            p=np.ascontiguousarray(W["p"][:, 0, c * TOK:(c + 1) * TOK]),
                            cid=cid,
                            cmask=np.full((128, 1), 0.0 if c == 0 else 1.0, np.float32)))
    if "F" not in _NC:
        _NC["F"] = build_fused()
    res = run_bass_kernel_spmd(_NC["F"], in_maps, core_ids=list(range(NCORE)))
    out = np.concatenate([r["out"] for r in res.results], axis=0)
    return out.reshape(1, S, D).astype(np.float32)
```
